# Optimizing a Trainium2 kernel written in Bass

```python
import math
import jax, jax.numpy as jnp
from jax import lax
import numpy as np

D_MODEL = 1024
BATCH = 16
SEQ = 2048
DEPTH = 2

CTX_LEN = 256
GRID_W = 64
CHUNK = 64
NORM_EPS = 1e-6
ROPE_BASE = 10000.0

RET_HEADS = 4
RET_DK = 64
RET_DV = 128
HG_HEADS = 4
HG_DK = 128
HG_DV = 128
GDN_HEADS = 4
GDN_DK = 64
GDN_DV = 128
GDN_CONV = 4
LRU_WIDTH = 512
LRU_BLOCKS = 8
LRU_CONV = 4
LRU_C = 8.0

N_BRANCH = 4
BRANCH_WIDTH = 512
D_FF = 2816
FFN_CONV = 3
N_MOD = 6

IN_SPLITS = (
    ('ret_q', RET_HEADS * RET_DK),
    ('ret_k', RET_HEADS * RET_DK),
    ('ret_v', RET_HEADS * RET_DV),
    ('ret_g', RET_HEADS * RET_DV),
    ('hg_q', HG_HEADS * HG_DK),
    ('hg_f_fwd', HG_HEADS * HG_DK),
    ('hg_f_bwd', HG_HEADS * HG_DK),
    ('hg_i', HG_HEADS * HG_DV),
    ('hg_g', HG_HEADS * HG_DV),
    ('gdn_qkv', 2 * GDN_HEADS * GDN_DK + GDN_HEADS * GDN_DV),
    ('gdn_a', 2 * GDN_HEADS),
    ('gdn_b', 2 * GDN_HEADS),
    ('gdn_g', GDN_HEADS * GDN_DV),
    ('lru_x', LRU_WIDTH),
    ('lru_gate', LRU_WIDTH),
    ('merge_gate', N_BRANCH * D_MODEL),
)
N_IN = sum(w for _, w in IN_SPLITS)

kernel_name = 'hybrid_retention_hgrn2_gdn_rglru_block'


def _layernorm0(x):
    xf = x.astype(jnp.float32)
    mu = jnp.mean(xf, axis=-1, keepdims=True)
    var = jnp.mean(jnp.square(xf - mu), axis=-1, keepdims=True)
    return ((xf - mu) * lax.rsqrt(var + NORM_EPS)).astype(x.dtype)


def _layernorm(x, g, b):
    return _layernorm0(x) * g + b


def _modulate(x, shift, scale):
    return _layernorm0(x) * (1.0 + scale) + shift


def _gated_rmsnorm(o, w, gate):
    of = o.astype(jnp.float32)
    of = of * lax.rsqrt(jnp.mean(jnp.square(of), axis=-1, keepdims=True) + NORM_EPS)
    return of.astype(o.dtype) * w * jax.nn.silu(gate)


def _l2norm(z):
    zf = z.astype(jnp.float32)
    return zf * lax.rsqrt(jnp.sum(zf * zf, axis=-1, keepdims=True) + NORM_EPS)


def _depthwise_conv(x, w):
    k, ch = w.shape
    left = k // 2
    return lax.conv_general_dilated(
        x, w[:, None, :].astype(x.dtype), window_strides=(1,),
        padding=[(left, k - 1 - left)], dimension_numbers=('NWC', 'WIO', 'NWC'),
        feature_group_count=ch)


def _split_cols(z):
    out = {}
    off = 0
    for name, width in IN_SPLITS:
        out[name] = z[..., off:off + width]
        off += width
    return out


def _heads(z, n_heads):
    b, t, _ = z.shape
    return z.reshape(b, t, n_heads, -1).transpose(0, 2, 1, 3)


def _rope_2d(x, n_rows):
    dk = x.shape[-1]
    half, quarter = dk // 2, dk // 4
    inv_freq = ROPE_BASE ** (-jnp.arange(quarter, dtype=jnp.float32) / quarter)
    row = jnp.repeat(jnp.arange(n_rows, dtype=jnp.float32), GRID_W)
    col = (jnp.arange(n_rows * GRID_W) % GRID_W).astype(jnp.float32)

    def rotate(z, pos):
        ang = pos[:, None] * inv_freq[None, :]
        cos = jnp.cos(ang)[None, :, None, :].astype(z.dtype)
        sin = jnp.sin(ang)[None, :, None, :].astype(z.dtype)
        z1, z2 = z[..., :quarter], z[..., quarter:]
        return jnp.concatenate([z1 * cos - z2 * sin, z1 * sin + z2 * cos], axis=-1)

    return jnp.concatenate([rotate(x[..., :half], row), rotate(x[..., half:], col)], axis=-1)


def _chunk_gla(q, k, v, logf, s0):
    out_dtype = v.dtype
    q, k, v, logf = (a.astype(jnp.float32) for a in (q, k, v, logf))
    b, h, t, _ = q.shape
    n = t // CHUNK

    def chunks(z):
        return jnp.moveaxis(z.reshape(b, h, n, CHUNK, z.shape[-1]), 2, 0)

    lower = jnp.tril(jnp.ones((CHUNK, CHUNK), dtype=bool))

    def step(s, inp):
        qc, kc, vc, gc = inp
        cum = jnp.cumsum(gc, axis=2)
        rel = cum[:, :, :, None, :] - cum[:, :, None, :, :]
        decay = jnp.exp(jnp.where(lower[:, :, None], rel, -jnp.inf))
        scores = jnp.einsum('bhtd,bhsd,bhtsd->bhts', qc, kc, decay)
        o = (jnp.einsum('bhts,bhse->bhte', scores, vc)
             + jnp.einsum('bhtd,bhde->bhte', qc * jnp.exp(cum), s))
        last = cum[:, :, -1:, :]
        s_new = (jnp.exp(last[:, :, 0, :])[..., None] * s
                 + jnp.einsum('bhsd,bhse->bhde', kc * jnp.exp(last - cum), vc))
        return s_new, o

    s_fin, o = lax.scan(step, s0, (chunks(q), chunks(k), chunks(v), chunks(logf)))
    o = jnp.moveaxis(o, 0, 2).reshape(b, h, t, -1)
    return o.astype(out_dtype), s_fin


def _chunk_gated_delta(q, k, v, loga, beta, s0):
    out_dtype = v.dtype
    q, k, v, loga, beta = (a.astype(jnp.float32) for a in (q, k, v, loga, beta))
    b, h, t, _ = q.shape
    n = t // CHUNK

    def chunks(z):
        return jnp.moveaxis(z.reshape((b, h, n, CHUNK) + z.shape[3:]), 2, 0)

    incl = jnp.tril(jnp.ones((CHUNK, CHUNK), dtype=bool))
    strict = jnp.tril(jnp.ones((CHUNK, CHUNK), dtype=bool), k=-1)
    eye = jnp.eye(CHUNK, dtype=jnp.float32)

    def step(s, inp):
        qc, kc, vc, gc, bc = inp
        cum = jnp.cumsum(gc, axis=-1)
        rel = cum[..., :, None] - cum[..., None, :]
        a_mat = eye + (bc[..., :, None] * jnp.exp(jnp.where(strict, rel, -jnp.inf))
                       * jnp.einsum('bhtd,bhsd->bhts', kc, kc))
        rhs = bc[..., None] * (vc - jnp.exp(cum)[..., None] * jnp.einsum('bhtd,bhde->bhte', kc, s))
        u = lax.linalg.triangular_solve(a_mat, rhs, left_side=True, lower=True,
                                        unit_diagonal=True)
        qk = jnp.einsum('bhtd,bhsd->bhts', qc, kc) * jnp.exp(jnp.where(incl, rel, -jnp.inf))
        o = (jnp.exp(cum)[..., None] * jnp.einsum('bhtd,bhde->bhte', qc, s)
             + jnp.einsum('bhts,bhse->bhte', qk, u))
        last = cum[..., -1:]
        s_new = (jnp.exp(last)[..., None] * s
                 + jnp.einsum('bhsd,bhse->bhde', kc * jnp.exp(last - cum)[..., None], u))
        return s_new, o

    s_fin, o = lax.scan(step, s0, (chunks(q), chunks(k), chunks(v), chunks(loga), chunks(beta)))
    o = jnp.moveaxis(o, 0, 2).reshape(b, h, t, -1)
    return o.astype(out_dtype), s_fin


def _rglru_scan(log_a, xin, h0):
    out_dtype = xin.dtype
    log_a = log_a.astype(jnp.float32)
    xin = xin.astype(jnp.float32)
    a = jnp.exp(log_a)
    u = jnp.sqrt(-jnp.expm1(2.0 * log_a)) * xin

    def combine(e1, e2):
        a1, b1 = e1
        a2, b2 = e2
        return a1 * a2, a2 * b1 + b2

    a_cum, h_part = lax.associative_scan(combine, (a, u), axis=1)
    h = h_part + a_cum * h0[:, None, :]
    return h.astype(out_dtype), h[:, -1]


def _bidirectional(scan_fn, axis, s_zero, ctx_fwd, ctx_bwd, lat_fwd, lat_bwd):
    def flip(args):
        return tuple(jnp.flip(a, axis) for a in args)
    o_cf, s_cf = scan_fn(*ctx_fwd, s_zero)
    o_cb, s_cb = scan_fn(*flip(ctx_bwd), s_zero)
    o_lf, _ = scan_fn(*lat_fwd, s_cf)
    o_lb, _ = scan_fn(*flip(lat_bwd), s_cb)
    return o_cf + jnp.flip(o_cb, axis), o_lf + jnp.flip(o_lb, axis)


def _retention(zc, zl, n_rows):
    log_gamma = jnp.log1p(-jnp.exp2(-5.0 - jnp.arange(RET_HEADS, dtype=jnp.float32)))

    def prep(z, use_rope):
        b, t, _ = z['ret_q'].shape
        q = z['ret_q'].reshape(b, t, RET_HEADS, RET_DK)
        k = z['ret_k'].reshape(b, t, RET_HEADS, RET_DK) * (RET_DK ** -0.5)
        if use_rope:
            q = _rope_2d(q, n_rows)
            k = _rope_2d(k, n_rows)
        v = z['ret_v'].reshape(b, t, RET_HEADS, RET_DV)
        logf = jnp.broadcast_to(log_gamma[None, :, None, None], (b, RET_HEADS, t, RET_DK))
        return (q.transpose(0, 2, 1, 3), k.transpose(0, 2, 1, 3), v.transpose(0, 2, 1, 3), logf)

    args_c = prep(zc, False)
    args_l = prep(zl, True)
    s0 = jnp.zeros((zl['ret_q'].shape[0], RET_HEADS, RET_DK, RET_DV), jnp.float32)
    oc, ol = _bidirectional(_chunk_gla, 2, s0, args_c, args_c, args_l, args_l)

    def readout(o, z):
        o = _layernorm0(o.transpose(0, 2, 1, 3))
        gate = z['ret_g'].reshape(o.shape)
        return (o * jax.nn.silu(gate)).reshape(o.shape[0], o.shape[1], -1)

    return readout(oc, zc), readout(ol, zl)


def _hgrn2(zc, zl, lb_fwd, lb_bwd, norm_w):
    def prep(z):
        q = _heads(jax.nn.silu(z['hg_q']), HG_HEADS)
        v = _heads(z['hg_i'], HG_HEADS)

        def direction(f_logits, lb):
            f = lb + (1.0 - lb) * jax.nn.sigmoid(f_logits.astype(jnp.float32))
            return (q, _heads(1.0 - f, HG_HEADS), v, _heads(jnp.log(f), HG_HEADS))

        return direction(z['hg_f_fwd'], lb_fwd), direction(z['hg_f_bwd'], lb_bwd)

    (cf, cb), (lf, lbk) = prep(zc), prep(zl)
    s0 = jnp.zeros((zl['hg_q'].shape[0], HG_HEADS, HG_DK, HG_DV), jnp.float32)
    oc, ol = _bidirectional(_chunk_gla, 2, s0, cf, cb, lf, lbk)

    def readout(o, z):
        o = o.transpose(0, 2, 1, 3)
        gate = z['hg_g'].reshape(o.shape)
        return _gated_rmsnorm(o, norm_w, gate).reshape(o.shape[0], o.shape[1], -1)

    return readout(oc, zc), readout(ol, zl)


def _gdn(zc, zl, conv_w, a_log, dt_bias, norm_w):
    nk = GDN_HEADS * GDN_DK

    def prep(z):
        b, t, _ = z['gdn_qkv'].shape
        qkv = jax.nn.silu(_depthwise_conv(z['gdn_qkv'], conv_w))
        q = _l2norm(qkv[..., :nk].reshape(b, t, GDN_HEADS, GDN_DK)).transpose(0, 2, 1, 3) * (GDN_DK ** -0.5)
        k = _l2norm(qkv[..., nk:2 * nk].reshape(b, t, GDN_HEADS, GDN_DK)).transpose(0, 2, 1, 3)
        v = _heads(qkv[..., 2 * nk:], GDN_HEADS)
        a = z['gdn_a'].astype(jnp.float32).reshape(b, t, 2, GDN_HEADS)
        bl = z['gdn_b'].astype(jnp.float32).reshape(b, t, 2, GDN_HEADS)

        def direction(d):
            loga = -jnp.exp(a_log[d].astype(jnp.float32)) * jax.nn.softplus(a[:, :, d] + dt_bias[d])
            beta = jax.nn.sigmoid(bl[:, :, d])
            return (q, k, v, loga.transpose(0, 2, 1), beta.transpose(0, 2, 1))

        return direction(0), direction(1)

    (cf, cb), (lf, lbk) = prep(zc), prep(zl)
    s0 = jnp.zeros((zl['gdn_qkv'].shape[0], GDN_HEADS, GDN_DK, GDN_DV), jnp.float32)
    oc, ol = _bidirectional(_chunk_gated_delta, 2, s0, cf, cb, lf, lbk)

    def readout(o, z):
        o = o.transpose(0, 2, 1, 3)
        gate = z['gdn_g'].reshape(o.shape)
        return _gated_rmsnorm(o, norm_w, gate).reshape(o.shape[0], o.shape[1], -1)

    return readout(oc, zc), readout(ol, zl)


def _rglru(zc, zl, conv_w, conv_b, w_a, b_a, w_i, b_i, lam):
    def prep(z):
        b, t, _ = z['lru_x'].shape
        xc = _depthwise_conv(z['lru_x'], conv_w) + conv_b
        xb = xc.reshape(b, t, LRU_BLOCKS, LRU_WIDTH // LRU_BLOCKS)

        def direction(d):
            r = jax.nn.sigmoid(jnp.einsum('btnk,nkj->btnj', xb, w_a[d]).reshape(b, t, LRU_WIDTH) + b_a[d])
            i = jax.nn.sigmoid(jnp.einsum('btnk,nkj->btnj', xb, w_i[d]).reshape(b, t, LRU_WIDTH) + b_i[d])
            log_a = -LRU_C * r.astype(jnp.float32) * jax.nn.softplus(-lam[d].astype(jnp.float32))
            return (log_a, i * xc)

        return direction(0), direction(1)

    (cf, cb), (lf, lbk) = prep(zc), prep(zl)
    h0 = jnp.zeros((zl['lru_x'].shape[0], LRU_WIDTH), jnp.float32)
    hc, hl = _bidirectional(_rglru_scan, 1, h0, cf, cb, lf, lbk)
    return jax.nn.gelu(zc['lru_gate']) * hc, jax.nn.gelu(zl['lru_gate']) * hl


def _merge(branches, gate_logits, w_branch, w_out):
    b, t, _ = gate_logits.shape
    gates = jax.nn.sigmoid(gate_logits).reshape(b, t, N_BRANCH, D_MODEL)
    merged = gates[:, :, 0] * (branches[0] @ w_branch[0])
    for j in range(1, N_BRANCH):
        merged = merged + gates[:, :, j] * (branches[j] @ w_branch[j])
    return merged @ w_out


def _conv_ffn(h, w_up, conv_w, conv_b, w_down):
    u = _depthwise_conv(h @ w_up, conv_w) + conv_b
    val, gate = jnp.split(u, 2, axis=-1)
    return (jax.nn.silu(gate) * val) @ w_down


def setup_inputs(seed: int = 0) -> dict:
    key = jax.random.key(seed)
    ks = iter(jax.random.split(key, 40))
    f32 = jnp.float32
    beta_dn = (8.0 * DEPTH) ** -0.25
    bw = LRU_WIDTH // LRU_BLOCKS
    qkv_w = 2 * GDN_HEADS * GDN_DK + GDN_HEADS * GDN_DV

    def nrm(shape, std):
        return jax.random.normal(next(ks), shape, f32) * std

    def gain(shape):
        return 1.0 + nrm(shape, 0.02)

    x = nrm((BATCH, SEQ, D_MODEL), 1.0)
    c = nrm((BATCH, D_MODEL), 1.0)
    ctx = nrm((BATCH, CTX_LEN, D_MODEL), 1.0)
    c_ctx = nrm((D_MODEL,), 1.0)
    w_mod = nrm((DEPTH, D_MODEL, N_MOD * D_MODEL), 0.5 * D_MODEL ** -0.5)
    b_mod = nrm((DEPTH, N_MOD * D_MODEL), 0.01)
    w_in = nrm((DEPTH, D_MODEL, N_IN), D_MODEL ** -0.5)
    hg_lb = 1.0 + nrm((DEPTH, 2, HG_HEADS * HG_DK), 0.1)
    hg_norm = gain((DEPTH, HG_DV))
    gdn_conv = nrm((DEPTH, GDN_CONV, qkv_w), GDN_CONV ** -0.5)
    gdn_a_log = jnp.log(jax.random.uniform(next(ks), (DEPTH, 2, GDN_HEADS), f32, 1.0, 16.0))
    dt = jnp.exp(jax.random.uniform(next(ks), (DEPTH, 2, GDN_HEADS), f32, math.log(1e-3), math.log(1e-1)))
    gdn_dt_bias = dt + jnp.log(-jnp.expm1(-dt))
    gdn_norm = gain((DEPTH, GDN_DV))
    lru_conv = nrm((DEPTH, LRU_CONV, LRU_WIDTH), LRU_CONV ** -0.5)
    lru_conv_b = nrm((DEPTH, LRU_WIDTH), 0.01)
    lru_w_a = nrm((DEPTH, 2, LRU_BLOCKS, bw, bw), bw ** -0.5)
    lru_b_a = nrm((DEPTH, 2, LRU_WIDTH), 0.01)
    lru_w_i = nrm((DEPTH, 2, LRU_BLOCKS, bw, bw), bw ** -0.5)
    lru_b_i = nrm((DEPTH, 2, LRU_WIDTH), 0.01)
    a_c = jax.random.uniform(next(ks), (DEPTH, 2, LRU_WIDTH), f32, 0.9, 0.999)
    p = a_c ** (1.0 / LRU_C)
    lru_lam = jnp.log(p) - jnp.log1p(-p)
    w_branch = nrm((DEPTH, N_BRANCH, BRANCH_WIDTH, D_MODEL), BRANCH_WIDTH ** -0.5)
    w_out = nrm((DEPTH, D_MODEL, D_MODEL), beta_dn * D_MODEL ** -0.5)
    ln_mix_g = gain((DEPTH, D_MODEL))
    ln_mix_b = nrm((DEPTH, D_MODEL), 0.01)
    w_up = nrm((DEPTH, D_MODEL, 2 * D_FF), D_MODEL ** -0.5)
    ffn_conv = nrm((DEPTH, FFN_CONV, 2 * D_FF), FFN_CONV ** -0.5)
    ffn_conv_b = nrm((DEPTH, 2 * D_FF), 0.01)
    w_down = nrm((DEPTH, D_FF, D_MODEL), beta_dn * D_FF ** -0.5)
    ln_ffn_g = gain((DEPTH, D_MODEL))
    ln_ffn_b = nrm((DEPTH, D_MODEL), 0.01)
    return {
        'x': x, 'c': c, 'ctx': ctx, 'c_ctx': c_ctx,
        'w_mod': w_mod, 'b_mod': b_mod, 'w_in': w_in,
        'hg_lb': hg_lb, 'hg_norm': hg_norm,
        'gdn_conv': gdn_conv, 'gdn_a_log': gdn_a_log, 'gdn_dt_bias': gdn_dt_bias, 'gdn_norm': gdn_norm,
        'lru_conv': lru_conv, 'lru_conv_b': lru_conv_b, 'lru_w_a': lru_w_a, 'lru_b_a': lru_b_a,
        'lru_w_i': lru_w_i, 'lru_b_i': lru_b_i, 'lru_lam': lru_lam,
        'w_branch': w_branch, 'w_out': w_out, 'ln_mix_g': ln_mix_g, 'ln_mix_b': ln_mix_b,
        'w_up': w_up, 'ffn_conv': ffn_conv, 'ffn_conv_b': ffn_conv_b, 'w_down': w_down,
        'ln_ffn_g': ln_ffn_g, 'ln_ffn_b': ln_ffn_b,
    }


def reference(x, c, ctx, c_ctx, w_mod, b_mod, w_in, hg_lb, hg_norm, gdn_conv, gdn_a_log,
              gdn_dt_bias, gdn_norm, lru_conv, lru_conv_b, lru_w_a, lru_b_a, lru_w_i, lru_b_i,
              lru_lam, w_branch, w_out, ln_mix_g, ln_mix_b, w_up, ffn_conv, ffn_conv_b, w_down,
              ln_ffn_g, ln_ffn_b):
    alpha = (2.0 * DEPTH) ** 0.25
    n_rows = x.shape[1] // GRID_W
    lb = jnp.cumsum(jax.nn.softmax(hg_lb.astype(jnp.float32), axis=0), axis=0)
    lb = lb - lb[:1]
    h_ctx = ctx
    for i in range(DEPTH):
        mod_l = (jax.nn.silu(c) @ w_mod[i] + b_mod[i])[:, None, :]
        mod_c = (jax.nn.silu(c_ctx) @ w_mod[i] + b_mod[i])[None, None, :]
        sh1, sc1, g1, sh2, sc2, g2 = jnp.split(mod_l, N_MOD, axis=-1)
        csh1, csc1, cg1, csh2, csc2, cg2 = jnp.split(mod_c, N_MOD, axis=-1)

        zl = _split_cols(_modulate(x, sh1, sc1) @ w_in[i])
        zc = _split_cols(_modulate(h_ctx, csh1, csc1) @ w_in[i])
        ret_c, ret_l = _retention(zc, zl, n_rows)
        hg_c, hg_l = _hgrn2(zc, zl, lb[i, 0], lb[i, 1], hg_norm[i])
        gd_c, gd_l = _gdn(zc, zl, gdn_conv[i], gdn_a_log[i], gdn_dt_bias[i], gdn_norm[i])
        lr_c, lr_l = _rglru(zc, zl, lru_conv[i], lru_conv_b[i], lru_w_a[i], lru_b_a[i],
                            lru_w_i[i], lru_b_i[i], lru_lam[i])
        y_l = _merge([ret_l, hg_l, gd_l, lr_l], zl['merge_gate'], w_branch[i], w_out[i])
        x = _layernorm(alpha * x + g1 * y_l, ln_mix_g[i], ln_mix_b[i])

        f_l = _conv_ffn(_modulate(x, sh2, sc2), w_up[i], ffn_conv[i], ffn_conv_b[i], w_down[i])
        x = _layernorm(alpha * x + g2 * f_l, ln_ffn_g[i], ln_ffn_b[i])

        if i < DEPTH - 1:
            y_c = _merge([ret_c, hg_c, gd_c, lr_c], zc['merge_gate'], w_branch[i], w_out[i])
            h_ctx = _layernorm(alpha * h_ctx + cg1 * y_c, ln_mix_g[i], ln_mix_b[i])
            f_c = _conv_ffn(_modulate(h_ctx, csh2, csc2), w_up[i], ffn_conv[i], ffn_conv_b[i], w_down[i])
            h_ctx = _layernorm(alpha * h_ctx + cg2 * f_c, ln_ffn_g[i], ln_ffn_b[i])
    return x
```

```python
import contextlib
import numpy as np
import concourse.bass as bass
import concourse.mybir as mybir
from concourse.bass_utils import run_bass_kernel_spmd

F32 = mybir.dt.float32
BF16 = mybir.dt.bfloat16
AF = mybir.ActivationFunctionType
ALU = mybir.AluOpType

D = 1024
KC = 8
N_IN = 10768
DFF = 2816
EPS = 1e-6
ALPHA = 4.0 ** 0.25
OFF = dict(ret_q=0, ret_k=256, ret_v=512, ret_g=1024, hg_q=1536, hg_ff=2048, hg_fb=2560, hg_i=3072,
           hg_g=3584, gdn_qkv=4096, gdn_a=5120, gdn_b=5128, gdn_g=5136, lru_x=5648, lru_gate=6160, mg=6672)


class Buf:
    __slots__ = ("name", "w", "r")

    def __init__(self, name=""):
        self.name = name
        self.w = {}
        self.r = {}


class Ins:
    __slots__ = ("eng", "fn", "deps", "needed", "tok", "isdma")

    def __init__(self, eng, fn, isdma=False):
        self.eng = eng
        self.fn = fn
        self.deps = []
        self.needed = False
        self.tok = None
        self.isdma = isdma


class TT:
    __slots__ = ("t", "b", "held")

    def __init__(self, t, b):
        self.t = t
        self.b = b
        self.held = False

    def __getitem__(self, k):
        return self.t[k]


STQ = "act"
INV_DT = BF16
NOSYNC_ENGINES = ()


class Prog:
    NQ = 32

    def __init__(self, nc):
        self.nc = nc
        self.q = {"pe": [], "act": [], "dve": [], "pool": [], "sp": []}
        self.ndma = 0
        self.dmas = []
        self.bar_gen = 0
        self.bar_set = []
        self.eng_gen = {k: 0 for k in self.q}
        self.last = {k: None for k in self.q}
        self.stack = contextlib.ExitStack()
        self.psb = []
        self.psi = 0
        self.uid = 0

    def sb(self, st, shape, dt, name=None):
        self.uid += 1
        name = (name or "t") + "_%d" % self.uid
        t = st.enter_context(self.nc.sbuf_tensor(name, list(shape), dt))
        return TT(t, Buf(name))

    def init_psum(self):
        for i in range(8):
            t = self.stack.enter_context(self.nc.psum_tensor("ps%d" % i, [128, 512], F32))
            self.psb.append(TT(t, Buf("ps%d" % i)))

    def ps(self, hold=False):
        for _ in range(8):
            p = self.psb[self.psi % 8]
            self.psi += 1
            if not p.held:
                p.held = hold
                return p
        raise RuntimeError("no free PSUM bank")

    def rel(self, *banks):
        for p in banks:
            p.held = False

    @contextlib.contextmanager
    def scope(self):
        st = contextlib.ExitStack()
        try:
            yield st
        finally:
            self.barrier()
            st.close()

    def barrier(self):
        s = [v for v in self.last.values() if v is not None]
        s += self.dmas[-self.NQ:]
        self.bar_set = s
        self.bar_gen += 1

    def _add(self, eng, fn, reads, writes, isdma=False):
        ins = Ins(eng, fn, isdma)
        key = ("dma", self.ndma % self.NQ) if isdma else eng
        deps = {}

        def need(p):
            if p is None or p is ins:
                return
            if (not p.isdma) and (not isdma) and p.eng == eng and (eng == "pe" or eng in NOSYNC_ENGINES):
                return
            deps[id(p)] = p

        for b in reads:
            for p in b.w.values():
                need(p)
        for b in writes:
            for p in b.w.values():
                need(p)
            for p in b.r.values():
                need(p)
        if self.eng_gen[eng] < self.bar_gen:
            for p in self.bar_set:
                if p.isdma or p.eng != eng:
                    deps[id(p)] = p
            self.eng_gen[eng] = self.bar_gen
        if isdma:
            j = self.ndma
            if j >= self.NQ:
                deps[id(self.dmas[j - self.NQ])] = self.dmas[j - self.NQ]
            ins.tok = (j % self.NQ, 16 * (j // self.NQ + 1))
            self.dmas.append(ins)
            self.ndma += 1
        for p in deps.values():
            p.needed = True
        ins.deps = list(deps.values())
        for b in reads:
            b.r[key] = ins
        for b in writes:
            b.w[key] = ins
        self.q[eng].append(ins)
        if not isdma:
            self.last[eng] = ins
        return ins

    def op(self, eng, fn, reads=(), writes=()):
        return self._add(eng, fn, [x.b if isinstance(x, TT) else x for x in reads],
                         [x.b if isinstance(x, TT) else x for x in writes])

    def dma(self, eng, out, in_, reads=(), writes=()):
        return self._add(eng, lambda e: e.dma_start(out=out, in_=in_),
                         [x.b if isinstance(x, TT) else x for x in reads],
                         [x.b if isinstance(x, TT) else x for x in writes], isdma=True)

    def emit(self, final_wait=()):
        nc = self.nc
        st = self.stack
        esem = {k: st.enter_context(nc.semaphore("s_" + k)) for k in self.q}
        dsem = [st.enter_context(nc.semaphore("d%d" % i)) for i in range(self.NQ)]
        block = st.enter_context(nc.Block())
        for k, lst in self.q.items():
            c = 0
            for ins in lst:
                if ins.isdma:
                    continue
                if ins.needed:
                    c += 1
                    ins.tok = (k, c)

        def run(k, e):
            waited = {}
            for ins in self.q[k]:
                for p in ins.deps:
                    s, v = p.tok
                    if waited.get(s, 0) >= v:
                        continue
                    waited[s] = v
                    e.wait_ge(dsem[s] if p.isdma else esem[s], v)
                h = ins.fn(e)
                if ins.isdma:
                    h.then_inc(dsem[ins.tok[0]], 16)
                elif ins.needed:
                    h.then_inc(esem[k], 1)
            if k == "sp":
                for p in final_wait:
                    s, v = p.tok
                    e.wait_ge(dsem[s] if p.isdma else esem[s], v)

        @block.sync
        def _(e):
            run("sp", e)

        @block.scalar
        def _(e):
            run("act", e)

        @block.vector
        def _(e):
            run("dve", e)

        @block.gpsimd
        def _(e):
            run("pool", e)

        @block.tensor
        def _(e):
            run("pe", e)


def mkap(base, dims):
    return bass.AP(tensor=base.tensor, offset=base.offset, ap=[list(base.ap[0])] + [list(d) for d in dims])


class Cfg:
    def __init__(self, nseq, Tc, Tl, mixers=("ret", "hg", "gdn", "lru"), depth=2):
        self.nseq, self.Tc, self.Tl, self.mixers, self.depth = nseq, Tc, Tl, mixers, depth
        self.nct, self.nlt = Tc // 128, Tl // 128
        self.NT = self.nct + self.nlt
        self.LO = Tc + 4
        self.TP = self.LO + Tl + 2
        self.ts = [2 + 128 * i for i in range(self.nct)] + [self.LO + 128 * i for i in range(self.nlt)]
        self.blocks = [(0, self.LO)]
        g = 0
        while g * 512 < Tl:
            self.blocks.append((self.LO + 512 * g, min(512, Tl - 512 * g)))
            g += 1
        self.fwd = list(range(self.NT))
        self.bwd = list(range(self.nct - 1, -1, -1)) + list(range(self.NT - 1, self.nct - 1, -1))

    def tile_block(self, i):
        return 0 if i < self.nct else 1 + (i - self.nct) // 4


def make_consts(cfg):
    c = {}
    TP, LO = cfg.TP, cfg.LO
    c["ident"] = np.eye(128, dtype=np.float32)
    c["ones"] = np.ones((128, 128), np.float32)
    c["bo64"] = np.kron(np.eye(2), np.ones((64, 64))).astype(np.float32)
    inv_freq = (np.float32(10000.0) ** (-np.arange(16, dtype=np.float32) / np.float32(16))).astype(np.float32)
    cos = np.ones((128, TP), np.float32)
    sin = np.zeros((128, TP), np.float32)
    t = np.arange(cfg.Tl)
    row = (t // 64).astype(np.float32)
    col = (t % 64).astype(np.float32)
    for d in range(64):
        half, i, part = d // 32, d % 16, (d % 32) // 16
        pos = row if half == 0 else col
        ang = (pos * inv_freq[i]).astype(np.float32)
        cs, sn = np.cos(ang).astype(np.float32), np.sin(ang).astype(np.float32)
        for r in (d, d + 64):
            cos[r, LO:LO + cfg.Tl] = cs
            sin[r, LO:LO + cfg.Tl] = (-sn if part == 0 else sn)
    c["cos"], c["sin"] = cos, sin
    gam = (1.0 - 2.0 ** (-5.0 - np.arange(4))).astype(np.float64)
    s = np.arange(128)[:, None]
    tt = np.arange(128)[None, :]
    dmf = np.zeros((128, 4, 128))
    dmb = np.zeros((128, 4, 128))
    for h in range(4):
        dmf[:, h, :] = np.where(tt >= s, gam[h] ** np.maximum(tt - s, 0), 0.0)
        dmb[:, h, :] = np.where(s >= tt, gam[h] ** np.maximum(s - tt, 0), 0.0)
    c["ret_dmf"] = dmf.reshape(128, 512).astype(np.float32)
    c["ret_dmb"] = dmb.reshape(128, 512).astype(np.float32)
    qf = np.zeros((128, 2, 128))
    qb = np.zeros((128, 2, 128))
    g128 = np.zeros((128, 2))
    for p in range(2):
        for r in range(128):
            h = 2 * p + r // 64
            qf[r, p, :] = gam[h] ** (np.arange(128) + 1)
            qb[r, p, :] = gam[h] ** (128 - np.arange(128))
            g128[r, p] = gam[h] ** 128
    c["ret_qf"] = qf.reshape(128, 256).astype(np.float32)
    c["ret_qb"] = qb.reshape(128, 256).astype(np.float32)
    c["ret_g128"] = g128.astype(np.float32)
    kf = np.zeros((128, 4))
    kb = np.zeros((128, 4))
    for h in range(4):
        kf[:, h] = gam[h] ** (127 - np.arange(128))
        kb[:, h] = gam[h] ** np.arange(128)
    c["ret_kf"] = kf.astype(np.float32)
    c["ret_kb"] = kb.astype(np.float32)
    j = np.arange(128)[:, None]
    blk = lambda a: a // 32
    same = blk(j) == blk(tt)
    for nm, le, r_off in (("f", j <= tt, 15), ("b", j >= tt, 16)):
        ccum = (same & le).astype(np.float32)
        rcol = (32 * (np.arange(128) // 32) + r_off)
        cref = ccum - ccum[:, rcol]
        c["hg_cc_" + nm] = np.concatenate([cref, ccum], axis=1).astype(np.float32)
        c["hg_cgt_" + nm] = (same & (~le)).astype(np.float32)
        c["hg_mask_" + nm] = ccum.copy()
    clast = np.zeros((128, 4), np.float32)
    for cc in range(4):
        clast[32 * cc:32 * cc + 32, cc] = 1.0
    c["hg_clast"] = clast
    for nm, le in (("f", j <= tt), ("b", j >= tt)):
        c["gd_cinc_" + nm] = le.astype(np.float32)
        c["gd_cgt_" + nm] = (~le).astype(np.float32)
        c["gd_mb_" + nm] = np.where(le, 0.0, -1e5).astype(np.float32)
        c["gd_st_" + nm] = (le & (j != tt)).astype(np.float32)
        pos = (lambda a: a) if nm == "f" else (lambda a: 127 - a)
        mo = np.zeros((128, 7, 128), np.float32)
        for lv in range(7):
            b = 1 << lv
            ps_, pt_ = pos(j), pos(tt)
            mo[:, lv, :] = ((ps_ // (2 * b) == pt_ // (2 * b)) & (ps_ % (2 * b) < b) & (pt_ % (2 * b) >= b))
        c["gd_mo_" + nm] = mo.reshape(128, 7 * 128)
    return c


def host_layout(inputs, cfg, core_b):
    m = {}
    f = lambda a: np.ascontiguousarray(np.asarray(a, dtype=np.float32))
    L = cfg.depth
    m["x"] = f(inputs["x"][core_b])
    m["ctx"] = f(inputs["ctx"][core_b])
    cc = np.concatenate([np.asarray(inputs["c"])[core_b], np.asarray(inputs["c_ctx"])[None, :]], axis=0)
    m["cT"] = f(cc.T.reshape(KC, 128, cfg.nseq + 1).transpose(1, 0, 2))
    m["w_mod"] = f(inputs["w_mod"])
    m["b_modT"] = f(np.asarray(inputs["b_mod"]).reshape(L, 48, 128).transpose(0, 2, 1))
    m["b_mod"] = f(inputs["b_mod"]).reshape(L, 1, 6144)
    w_in = np.asarray(inputs["w_in"])
    m["w_in"] = f(w_in)
    perm = np.zeros(256, np.int64)
    for h in range(4):
        for d in range(64):
            part = (d % 32) // 16
            perm[h * 64 + d] = h * 64 + (d + 16 if part == 0 else d - 16)
    m["w_rope"] = f(np.concatenate([w_in[:, :, OFF["ret_q"] + perm], w_in[:, :, OFF["ret_k"] + perm]], axis=2))
    hl = np.asarray(inputs["hg_lb"])
    m["hg_lbT"] = f(hl.reshape(L, 2, 4, 128).transpose(0, 3, 1, 2).reshape(L, 128, 8))
    m["hg_lb"] = f(hl.reshape(L, 1, 1024))
    m["hg_norm"] = f(np.tile(np.asarray(inputs["hg_norm"]), (1, 4))).reshape(L, 1, 512)
    m["gdn_norm"] = f(np.tile(np.asarray(inputs["gdn_norm"]), (1, 4))).reshape(L, 1, 512)
    m["gdn_convT"] = f(np.asarray(inputs["gdn_conv"]).reshape(L, 4, 8, 128).transpose(0, 3, 2, 1))
    m["gdn_alog"] = f(inputs["gdn_a_log"]).reshape(L, 1, 8)
    m["gdn_dtb"] = f(inputs["gdn_dt_bias"]).reshape(L, 1, 8)
    m["lru_convT"] = f(np.asarray(inputs["lru_conv"]).reshape(L, 4, 4, 128).transpose(0, 3, 2, 1))
    m["lru_convbT"] = f(np.asarray(inputs["lru_conv_b"]).reshape(L, 4, 128).transpose(0, 2, 1))
    m["lru_w_a"] = f(inputs["lru_w_a"])
    m["lru_w_i"] = f(inputs["lru_w_i"])
    for nm in ("lru_b_a", "lru_b_i", "lru_lam"):
        m[nm + "T"] = f(np.asarray(inputs[nm]).reshape(L, 2, 4, 128).transpose(0, 3, 1, 2).reshape(L, 128, 8))
    m["w_branch"] = f(inputs["w_branch"])
    m["w_out"] = f(inputs["w_out"])
    m["w_up"] = f(inputs["w_up"])
    m["w_down"] = f(inputs["w_down"])
    m["ffn_convT"] = f(np.asarray(inputs["ffn_conv"]).reshape(L, 3, 44, 128).transpose(0, 3, 2, 1))
    m["ffn_convbT"] = f(np.asarray(inputs["ffn_conv_b"]).reshape(L, 44, 128).transpose(0, 2, 1))
    for nm in ("ln_mix_g", "ln_mix_b", "ln_ffn_g", "ln_ffn_b"):
        m[nm] = f(inputs[nm]).reshape(L, 1, D)
    return m


class Builder:
    def __init__(self, cfg, in_shapes):
        self.cfg = cfg
        nc = self.nc = bass.Bass("TRN2", target_bir_lowering=False)
        P = self.P = Prog(nc)
        self.dr = {k: nc.dram_tensor(k, list(s), F32, kind="ExternalInput").ap() for k, s in in_shapes.items()}
        nseq, NT, TP = cfg.nseq, cfg.NT, cfg.TP
        self.out = nc.dram_tensor("out", [nseq, cfg.Tl, D], F32, kind="ExternalOutput").ap()
        self.XA = nc.dram_tensor("XA", [nseq, NT * 128, D], F32, kind="Internal").ap()
        self.XM = nc.dram_tensor("XM", [nseq, NT * 128, D], F32, kind="Internal").ap()
        self.BRT = nc.dram_tensor("BRT", [16, 128, TP], BF16, kind=("ExternalOutput" if getattr(cfg, "debug", False) else "Internal")).ap()
        self.ACTD = nc.dram_tensor("ACTD", [22, 128, TP], BF16, kind="Internal").ap()
        self.bXA, self.bXM, self.bBRT, self.bACTD = Buf("XA"), Buf("XM"), Buf("BRT"), Buf("ACTD")
        self.OPD = nc.dram_tensor("OPD", [2, NT * 128, 512], F32, kind="Internal").ap()
        self.bOPD = Buf("OPD")
        P.init_psum()
        self.outs = []
        self.CN = {}
        top = P.stack
        self.in_shapes = in_shapes
        for k in ("ident", "ones"):
            self.CN[k] = P.sb(top, in_shapes["c_" + k], F32, "c_" + k)
            P.dma("sp", self.CN[k][:], self.dr["c_" + k][:], (), [self.CN[k]])
        self.ident = self.CN["ident"]
        self.eps_t = P.sb(top, [128, 1], F32, "eps")
        self.ms(self.eps_t[:], EPS, [self.eps_t])
        self.identb = P.sb(top, [128, 128], BF16, "identb")
        self.cp(self.identb[:], self.ident[:], [self.ident], [self.identb])
        self.stg = [P.sb(top, [128, 2048], F32, "stg") for _ in range(2)]
        self.stgi = 0

    def const(self, st, k):
        t_ = self.P.sb(st, self.in_shapes["c_" + k], F32, "c_" + k)
        self.P.dma("sp", t_[:], self.dr["c_" + k][:], (), [t_])
        return t_

    def V(self, eng, fn, r=(), w=()):
        return self.P.op(eng, fn, r, w)

    def mm(self, o, lhsT, rhs, start, stop, r, w):
        return self.P.op("pe", lambda e: e.matmul(o, lhsT=lhsT, rhs=rhs, start=start, stop=stop), r, w)

    def tr(self, o, in_, ident, r, w):
        return self.P.op("pe", lambda e: e.transpose(o, in_, ident), r, w)

    def act(self, o, i, func, r, w, scale=1.0, bias=0.0):
        return self.P.op("act", lambda e: e.activation(out=o, in_=i, func=func, bias=bias, scale=scale), r, w)

    def tt(self, o, a, b, op, r, w, eng="dve"):
        return self.P.op(eng, lambda e: e.tensor_tensor(out=o, in0=a, in1=b, op=op), r, w)

    def ts(self, o, a, s1, s2, op0, op1, r, w, eng="dve"):
        if op1 is None:
            return self.P.op(eng, lambda e: e.tensor_scalar(out=o, in0=a, scalar1=s1, scalar2=None, op0=op0), r, w)
        return self.P.op(eng, lambda e: e.tensor_scalar(out=o, in0=a, scalar1=s1, scalar2=s2, op0=op0, op1=op1), r, w)

    def stt(self, o, a, s, b, op0, op1, r, w):
        return self.P.op("dve", lambda e: e.scalar_tensor_tensor(out=o, in0=a, scalar=s, in1=b, op0=op0, op1=op1), r, w)

    def cp(self, o, i, r, w, eng="dve"):
        return self.P.op(eng, lambda e: e.tensor_copy(out=o, in_=i), r, w)

    def ms(self, o, val, w, eng="dve"):
        return self.P.op(eng, lambda e: e.memset(o, val), (), w)

    def wview(self, name, l, c0, n):
        return self.dr[name][l].rearrange("(k p) n -> p k n", p=128)[:, :, c0:c0 + n]

    def load_cast(self, dst, src, K_, n):
        P = self.P
        g = max(1, 2048 // n)
        for k0 in range(0, K_, g):
            kk = min(g, K_ - k0)
            sg = self.stg[self.stgi % 2]
            self.stgi += 1
            sv = sg[:, 0:kk * n].rearrange("p (k n) -> p k n", n=n)
            P.dma("sp", sv, src[:, k0:k0 + kk, :], (), [sg])
            self.cp(dst[:, k0:k0 + kk, :], sv, [sg], [self._dst_tt], eng="pool")

    def loadw(self, dst, name, l, c0, n, d0=0):
        self._dst_tt = dst
        self.load_cast(dst[:, :, d0:d0 + n], self.wview(name, l, c0, n), KC, n)

    def do_stats(self, xt, s6, mv, rs):
        for hh in range(2):
            self.V("dve", lambda e, hh=hh: e.bn_stats(out=s6[:, hh, :], in_=xt[:, hh * 512:(hh + 1) * 512]), [xt], [s6])
        self.V("dve", lambda e: e.bn_aggr(out=mv[:], in_=s6[:].rearrange("p a b -> p (a b)")), [s6], [mv])
        self.act(rs[:], mv[:, 1:2], AF.Sqrt, [mv, self.eps_t], [rs], bias=self.eps_t[:, 0:1])
        self.V("dve", lambda e: e.reciprocal(out=rs[:], in_=rs[:]), [rs], [rs])

    def x_src(self, l, s, i):
        nct = self.cfg.nct
        if l == 0:
            if i < nct:
                return self.dr["ctx"][s, i * 128:(i + 1) * 128, :], []
            return self.dr["x"][s, (i - nct) * 128:(i - nct + 1) * 128, :], []
        return self.XA[s, i * 128:(i + 1) * 128, :], [self.bXA]

    def ln_mod_T(self, xt, xn, st3, i, s, j_sh, j_sc):
        cfg, P = self.cfg, self.P
        s6, mv, rs = st3
        slot = s if i >= cfg.nct else cfg.nseq
        self.do_stats(xt, s6, mv, rs)
        self.ts(xn[:], xt[:], mv[:, 0:1], rs[:, 0:1], ALU.subtract, ALU.mult, [xt, mv, rs], [xn])
        hb = self.hTb[cfg.tile_block(i)]
        modT, hT = self.modT, self.hT
        for half in range(2):
            pt = P.ps()
            for k4 in range(4):
                k = half * 4 + k4
                self.tr(pt[:, k4 * 128:(k4 + 1) * 128], xn[:, k * 128:(k + 1) * 128], self.ident[:], [xn, self.ident], [pt])
            for k4 in range(4):
                k = half * 4 + k4
                self.act(hT[:, k, cfg.ts[i]:cfg.ts[i] + 128], pt[:, k4 * 128:(k4 + 1) * 128], AF.Identity,
                         [pt, modT], [hb], scale=modT[:, j_sc * 8 + k, slot:slot + 1],
                         bias=modT[:, j_sh * 8 + k, slot:slot + 1])

    def st3(self, st):
        P = self.P
        return (P.sb(st, [128, 2, 6], F32, "s6"), P.sb(st, [128, 2], F32, "mv"), P.sb(st, [128, 1], F32, "rs"))

    def proj_fm(self, wt, wcol, b):
        c0, n = self.cfg.blocks[b]
        pp = self.P.ps()
        for k in range(KC):
            self.mm(pp[:, 0:n], wt[:, k, wcol:wcol + 128], self.hT[:, k, c0:c0 + n], k == 0, k == KC - 1,
                    [wt, self.hTb[b]], [pp])
        return pp

    def proj_tm(self, wt, wcol, ncols, i):
        pp = self.P.ps()
        c = self.cfg.ts[i]
        for k in range(KC):
            self.mm(pp[:, 0:ncols], self.hT[:, k, c:c + 128], wt[:, k, wcol:wcol + ncols], k == 0, k == KC - 1,
                    [wt, self.hTb[self.cfg.tile_block(i)]], [pp])
        return pp

    def store_br(self, brt, chunk, i):
        c = self.cfg.ts[i]
        self.P.dma(STQ, self.BRT[chunk, :, c:c + 128], brt[:], [brt], [self.bBRT])

    def build(self):
        cfg, P, dr = self.cfg, self.P, self.dr
        nseq, NT, TP, nct = cfg.nseq, cfg.NT, cfg.TP, cfg.nct
        for l in range(cfg.depth):
            last = (l == cfg.depth - 1)
            with P.scope() as sl:
                self.modT = modT = P.sb(sl, [128, 48, nseq + 1], F32, "modT")
                self.scT = scT = P.sb(sl, [128, KC, nseq + 1], F32, "scT")
                P.dma("sp", scT[:], dr["cT"][:], (), [scT])
                self.act(scT[:], scT[:], AF.Silu, [scT], [scT])
                with P.scope() as s0:
                    bmt = P.sb(s0, [128, 48], F32, "bmt")
                    P.dma("sp", bmt[:], dr["b_modT"][l], (), [bmt])
                    pm = P.ps()
                    wb = [P.sb(s0, [128, KC, 512], F32, "wm%d" % i) for i in range(2)]
                    n1 = nseq + 1
                    for g in range(12):
                        w = wb[g % 2]
                        P.dma("sp", w[:], self.wview("w_mod", l, g * 512, 512), (), [w])
                        for cc in range(4):
                            q = g * 4 + cc
                            for k in range(KC):
                                self.mm(pm[:, q * n1:(q + 1) * n1], w[:, k, cc * 128:(cc + 1) * 128], scT[:, k, :],
                                        k == 0, k == KC - 1, [w, scT], [pm])
                    self.tt(modT[:], pm[:, 0:48 * n1].rearrange("p (q s) -> p q s", s=n1),
                            mkap(bmt[:], [[1, 48], [0, n1]]), ALU.add, [pm, bmt], [modT])
                    for jj in (1, 4):
                        self.ts(modT[:, jj * 8:(jj + 1) * 8, :], modT[:, jj * 8:(jj + 1) * 8, :], 1.0, None, ALU.add, None,
                                [modT], [modT])
                for s in range(nseq):
                    self.seq_layer(l, s, last, sl)
        P.emit(final_wait=self.outs)
        return self.nc

    def grow(self, st, l, jj, slot):
        P, dr, nseq = self.P, self.dr, self.cfg.nseq
        g = P.sb(st, [128, D], F32, "grow")
        with P.scope() as s1:
            scB = P.sb(s1, [128, KC, 128], F32, "scB")
            self.cp(scB[:], mkap(self.scT[:, 0, slot:slot + 1], [[nseq + 1, KC], [0, 128]]), [self.scT], [scB])
            brow = P.sb(s1, [1, D], F32, "brow")
            P.dma("sp", brow[:], dr["b_mod"][l][:, jj * D:(jj + 1) * D], (), [brow])
            w = P.sb(s1, [128, KC, 512], F32, "wg")
            ones = self.CN["ones"]
            for nb in range(2):
                P.dma("sp", w[:], self.wview("w_mod", l, jj * D + nb * 512, 512), (), [w])
                pg = P.ps()
                for k in range(KC):
                    self.mm(pg[:], scB[:, k, :], w[:, k, :], k == 0, False, [scB, w], [pg])
                self.mm(pg[:], ones[0:1, :], brow[0:1, nb * 512:(nb + 1) * 512], False, True, [ones, brow], [pg])
                self.cp(g[:, nb * 512:(nb + 1) * 512], pg[:], [pg], [g])
        return g

    def rowbc(self, st, name, l, n=D):
        t_ = self.P.sb(st, [128, n], F32, name)
        self.P.dma("sp", t_[:], self.dr[name][l].to_broadcast([128, n]), (), [t_])
        return t_

    def seq_layer(self, l, s, last, sl):
        cfg, P, dr = self.cfg, self.P, self.dr
        nseq, NT, TP, nct = cfg.nseq, cfg.NT, cfg.TP, cfg.nct
        with P.scope() as ss:
            self.hT = hT = P.sb(ss, [128, KC, TP], BF16, "hT")
            self.hTb = hTb = [Buf("hTb%d" % i) for i in range(len(cfg.blocks))]
            self.ms(hT[:, :, 0:2], 0.0, hTb[:1])
            self.ms(hT[:, :, cfg.LO - 2:cfg.LO], 0.0, hTb[:1])
            self.ms(hT[:, :, TP - 2:TP], 0.0, hTb[-1:])
            with P.scope() as sa:
                xts = [P.sb(sa, [128, D], F32, "xa") for _ in range(2)]
                xns = [P.sb(sa, [128, D], F32, "xn") for _ in range(2)]
                st3 = [self.st3(sa) for _ in range(2)]
                for i in range(NT):
                    xt = xts[i % 2]
                    src, rb = self.x_src(l, s, i)
                    P.dma("sp", xt[:], src, rb, [xt])
                    self.ln_mod_T(xt, xns[i % 2], st3[i % 2], i, s, 0, 1)
            if "lru" in cfg.mixers:
                self.mixer_lru(l, s)
            else:
                self.zero_branch(12)
            for nm, ch in (("ret", 0), ("hg", 4), ("gdn", 8)):
                if nm in cfg.mixers:
                    getattr(self, "mixer_" + nm)(l, s)
                else:
                    self.zero_branch(ch)
            self.phase_c(l, s, last)
            self.phase_de(l, s, last)

    def zero_branch(self, ch0):
        P = self.P
        with P.scope() as st:
            z = P.sb(st, [128, self.cfg.TP], BF16, "zb")
            self.ms(z[:], 0.0, [z])
            for c in range(4):
                P.dma(STQ, self.BRT[ch0 + c], z[:], [z], [self.bBRT])

    def mixer_lru(self, l, s):
        cfg, P, dr = self.cfg, self.P, self.dr
        TP, LO = cfg.TP, cfg.LO
        segs = [(2, cfg.Tc), (LO, cfg.Tl)]
        with P.scope() as st:
            cw = P.sb(st, [128, 4, 4], F32, "lcw")
            P.dma("sp", cw[:], dr["lru_convT"][l], (), [cw])
            cb = P.sb(st, [128, 4], F32, "lcb")
            P.dma("sp", cb[:], dr["lru_convbT"][l], (), [cb])
            par = {}
            for nm in ("lru_b_a", "lru_b_i", "lru_lam"):
                par[nm] = P.sb(st, [128, 8], F32, nm)
                P.dma("sp", par[nm][:], dr[nm + "T"][l], (), [par[nm]])
            y = P.sb(st, [128, 8], F32, "ly")
            w_ = P.sb(st, [128, 8], F32, "lw")
            w2 = P.sb(st, [128, 8], F32, "lw2")
            pl = P.sb(st, [128, 8], F32, "lpl")
            nsp = P.sb(st, [128, 8], F32, "nsp")
            nsp2 = P.sb(st, [128, 8], F32, "nsp2")
            self.act(y[:], par["lru_lam"][:], AF.Exp, [par["lru_lam"]], [y], scale=-1.0)
            self.ts(w_[:], y[:], 2.0, None, ALU.add, None, [y], [w_])
            self.V("dve", lambda e: e.reciprocal(out=w_[:], in_=w_[:]), [w_], [w_])
            self.tt(w_[:], w_[:], y[:], ALU.mult, [w_, y], [w_])
            self.tt(w2[:], w_[:], w_[:], ALU.mult, [w_], [w2])
            self.ts(pl[:], w2[:], 1.0 / 9.0, 1.0 / 7.0, ALU.mult, ALU.add, [w2], [pl])
            for cf in (1.0 / 5.0, 1.0 / 3.0, 1.0):
                self.tt(pl[:], pl[:], w2[:], ALU.mult, [pl, w2], [pl])
                self.ts(pl[:], pl[:], cf, None, ALU.add, None, [pl], [pl])
            self.tt(pl[:], pl[:], w_[:], ALU.mult, [pl, w_], [pl])
            self.ts(nsp[:], pl[:], -16.0, None, ALU.mult, None, [pl], [nsp])
            self.ts(nsp2[:], pl[:], -32.0, None, ALU.mult, None, [pl], [nsp2])
            wbd = P.sb(st, [128, 2, 2, 4, 128], F32, "wbd")
            self.ms(wbd[:].rearrange("p a b c d -> p (a b c d)"), 0.0, [wbd])
            for ai, nm in enumerate(("lru_w_a", "lru_w_i")):
                for d_ in range(2):
                    for blk in range(8):
                        c, hf = blk // 2, blk % 2
                        P.dma("sp", wbd[hf * 64:(hf + 1) * 64, ai, d_, c, hf * 64:(hf + 1) * 64], dr[nm][l, d_, blk], (), [wbd])
            wt = P.sb(st, [128, KC, 1024], BF16, "lwt")
            self.loadw(wt, "w_in", l, OFF["lru_x"], 1024)
            def bufs():
                B = dict(xs=P.sb(st, [128, TP], F32, "lxs"), xc=P.sb(st, [128, TP], F32, "lxc"), hs=P.sb(st, [128, TP], F32, "lhs"),
                         av=P.sb(st, [128, TP], F32, "lav"), uv=P.sb(st, [128, TP], F32, "luv"), t1=P.sb(st, [128, TP], F32, "lt1"),
                         brt=P.sb(st, [128, TP], BF16, "lbr"))
                B["hd"] = B["t1"]
                self.ms(B["xs"][:, TP - 2:TP], 0.0, [B["xs"]])
                return B

            def chunk(c, B):
                xs, xc, hs, av, uv, t1, hd, brt = B["xs"], B["xc"], B["hs"], B["av"], B["uv"], B["t1"], B["hd"], B["brt"]
                gg = xs
                for b in range(len(cfg.blocks)):
                    c0, n = cfg.blocks[b]
                    pp = self.proj_fm(wt, c * 128, b)
                    self.cp(xs[:, c0:c0 + n], pp[:, 0:n], [pp], [xs])
                yield
                n_ = TP - 4
                self.ts(xc[:, 2:2 + n_], xs[:, 0:n_], cw[:, c, 0:1], cb[:, c:c + 1], ALU.mult, ALU.add, [xs, cw, cb], [xc])
                for j in (1, 2, 3):
                    self.stt(xc[:, 2:2 + n_], xs[:, j:j + n_], cw[:, c, j:j + 1], xc[:, 2:2 + n_], ALU.mult, ALU.add,
                             [xs, cw, xc], [xc])
                yield
                for d_ in range(2):
                    pc = d_ * 4 + c
                    for b in range(len(cfg.blocks)):
                        c0, n = cfg.blocks[b]
                        pa = P.ps()
                        self.mm(pa[:, 0:n], wbd[:, 0, d_, c, :], xc[:, c0:c0 + n], True, True, [wbd, xc], [pa])
                        pi = P.ps()
                        self.mm(pi[:, 0:n], wbd[:, 1, d_, c, :], xc[:, c0:c0 + n], True, True, [wbd, xc], [pi])
                        self.act(t1[:, c0:c0 + n], pa[:, 0:n], AF.Sigmoid, [pa, par["lru_b_a"]], [t1], bias=par["lru_b_a"][:, pc:pc + 1])
                        self.act(av[:, c0:c0 + n], t1[:, c0:c0 + n], AF.Exp, [t1, nsp], [av], scale=nsp[:, pc:pc + 1])
                        self.act(uv[:, c0:c0 + n], t1[:, c0:c0 + n], AF.Exp, [t1, nsp2], [uv], scale=nsp2[:, pc:pc + 1])
                        self.act(uv[:, c0:c0 + n], uv[:, c0:c0 + n], AF.Sqrt, [uv], [uv], scale=-1.0, bias=1.0)
                        self.act(t1[:, c0:c0 + n], pi[:, 0:n], AF.Sigmoid, [pi, par["lru_b_i"], t1], [t1], bias=par["lru_b_i"][:, pc:pc + 1])
                        self.tt(uv[:, c0:c0 + n], uv[:, c0:c0 + n], t1[:, c0:c0 + n], ALU.mult, [uv, t1], [uv])
                        self.tt(uv[:, c0:c0 + n], uv[:, c0:c0 + n], xc[:, c0:c0 + n], ALU.mult, [uv, xc], [uv])
                        yield
                    dst = hs if d_ == 0 else hd
                    if d_ == 0:
                        (a0, n0), (a1, n1) = segs
                        self.V("dve", lambda e, a0=a0, n0=n0, dst=dst: e.tensor_tensor_scan(
                            out=dst[:, a0:a0 + n0], data0=av[:, a0:a0 + n0], data1=uv[:, a0:a0 + n0], initial=0.0,
                            op0=ALU.mult, op1=ALU.add), [av, uv], [dst])
                        self.V("dve", lambda e, a0=a0, n0=n0, a1=a1, n1=n1, dst=dst: e.tensor_tensor_scan(
                            out=dst[:, a1:a1 + n1], data0=av[:, a1:a1 + n1], data1=uv[:, a1:a1 + n1],
                            initial=dst[:, a0 + n0 - 1:a0 + n0], op0=ALU.mult, op1=ALU.add), [av, uv, dst], [dst])
                    else:
                        (a0, n0), (a1, n1) = segs
                        self.V("dve", lambda e, a0=a0, n0=n0, dst=dst: e.tensor_tensor_scan(
                            out=dst[:, a0:a0 + n0][:, ::-1], data0=av[:, a0:a0 + n0][:, ::-1], data1=uv[:, a0:a0 + n0][:, ::-1],
                            initial=0.0, op0=ALU.mult, op1=ALU.add), [av, uv], [dst])
                        self.V("dve", lambda e, a0=a0, a1=a1, n1=n1, dst=dst: e.tensor_tensor_scan(
                            out=dst[:, a1:a1 + n1][:, ::-1], data0=av[:, a1:a1 + n1][:, ::-1], data1=uv[:, a1:a1 + n1][:, ::-1],
                            initial=dst[:, a0:a0 + 1], op0=ALU.mult, op1=ALU.add), [av, uv, dst], [dst])
                        self.tt(hs[:], hs[:], hd[:], ALU.add, [hs, hd], [hs])
                yield
                for b in range(len(cfg.blocks)):
                    c0, n = cfg.blocks[b]
                    pp = self.proj_fm(wt, 512 + c * 128, b)
                    self.cp(gg[:, c0:c0 + n], pp[:, 0:n], [pp], [gg])
                    self.tt(t1[:, c0:c0 + n], gg[:, c0:c0 + n], gg[:, c0:c0 + n], ALU.mult, [gg], [t1])
                    self.ts(t1[:, c0:c0 + n], t1[:, c0:c0 + n], 0.044715, 1.0, ALU.mult, ALU.add, [t1], [t1])
                    self.tt(t1[:, c0:c0 + n], t1[:, c0:c0 + n], gg[:, c0:c0 + n], ALU.mult, [t1, gg], [t1])
                    self.act(t1[:, c0:c0 + n], t1[:, c0:c0 + n], AF.Sigmoid, [t1], [t1], scale=1.5957691216057308)
                    self.tt(t1[:, c0:c0 + n], t1[:, c0:c0 + n], gg[:, c0:c0 + n], ALU.mult, [t1, gg], [t1])
                    self.tt(brt[:, c0:c0 + n], t1[:, c0:c0 + n], hs[:, c0:c0 + n], ALU.mult, [t1, hs], [brt])
                    yield
                self.ms(brt[:, TP - 2:TP], 0.0, [brt])
                P.dma(STQ, self.BRT[12 + c], brt[:], [brt], [self.bBRT])

            BB = [bufs(), bufs()]
            for c0_ in (0, 2):
                self.run_lockstep([chunk(c0_, BB[0]), chunk(c0_ + 1, BB[1])])


    @staticmethod
    def run_lockstep(gens):
        gens = list(gens)
        while gens:
            for g in list(gens):
                try:
                    next(g)
                except StopIteration:
                    gens.remove(g)

    def put_opart(self, po, stage, di, i):
        self.act(stage[:], po[:], AF.Identity, [po], [stage])
        self.P.dma(STQ, self.OPD[di, i * 128:(i + 1) * 128, :], stage[:], [stage], [self.bOPD])

    def readout(self, st_bufs, kind, i, wg, gcol, ch0, wrow):
        P = self.P
        otot, sq, sg, brb, brt, s6, mv, rs = st_bufs
        P.dma("sp", otot[:], self.OPD[0, i * 128:(i + 1) * 128, :], [self.bOPD], [otot])
        P.dma("sp", sq[:], self.OPD[1, i * 128:(i + 1) * 128, :], [self.bOPD], [sq])
        self.tt(otot[:], otot[:], sq[:], ALU.add, [otot, sq], [otot], eng="pool")
        pg = self.proj_tm(wg, gcol, 512, i)
        self.act(sg[:], pg[:], AF.Silu, [pg], [sg])
        yield
        if kind == "ln":
            for h in range(4):
                self.V("dve", lambda e, h=h: e.bn_stats(out=s6[:, h, :], in_=otot[:, h * 128:(h + 1) * 128]), [otot], [s6])
            for h in range(4):
                self.V("dve", lambda e, h=h: e.bn_aggr(out=mv[:, h, :], in_=s6[:, h, :]), [s6], [mv])
            self.act(rs[:], mv[:, :, 1], AF.Sqrt, [mv, self.eps_t], [rs], bias=self.eps_t[:, 0:1])
            yield
            self.V("dve", lambda e: e.reciprocal(out=rs[:], in_=rs[:]), [rs], [rs])
            for h in range(4):
                self.ts(otot[:, h * 128:(h + 1) * 128], otot[:, h * 128:(h + 1) * 128], mv[:, h, 0:1], rs[:, h:h + 1],
                        ALU.subtract, ALU.mult, [otot, mv, rs], [otot])
        else:
            self.tt(sq[:], otot[:], otot[:], ALU.mult, [otot], [sq])
            self.V("dve", lambda e: e.tensor_reduce(out=rs[:], in_=sq[:].rearrange("p (h e) -> p h e", h=4),
                                                    axis=mybir.AxisListType.X, op=ALU.add), [sq], [rs])
            self.act(rs[:], rs[:], AF.Sqrt, [rs, self.eps_t], [rs], scale=1.0 / 128.0, bias=self.eps_t[:, 0:1])
            yield
            self.V("dve", lambda e: e.reciprocal(out=rs[:], in_=rs[:]), [rs], [rs])
            for h in range(4):
                self.ts(otot[:, h * 128:(h + 1) * 128], otot[:, h * 128:(h + 1) * 128], rs[:, h:h + 1], None, ALU.mult, None,
                        [otot, rs], [otot])
        if wrow is not None:
            self.tt(sg[:], sg[:], wrow[:], ALU.mult, [sg, wrow], [sg], eng="pool")
        yield
        self.tt(brb[:], otot[:], sg[:], ALU.mult, [otot, sg], [brb])
        pt = P.ps(True)
        for h in range(4):
            self.tr(pt[:, h * 128:(h + 1) * 128], brb[:, h * 128:(h + 1) * 128], self.ident[:], [brb, self.ident], [pt])
        yield
        self.act(brt[:].rearrange("p c t -> p (c t)"), pt[:, 0:512], AF.Identity, [pt], [brt])
        P.rel(pt)
        c = self.cfg.ts[i]
        P.dma(STQ, self.BRT[ch0:ch0 + 4, :, c:c + 128].rearrange("c p t -> p c t"), brt[:], [brt], [self.bBRT])

    def readout_sweep(self, st, kind, wg, gcol, ch0, wrow):
        rbs = [self.readout_bufs(st) for _ in range(2)]
        NT = self.cfg.NT
        for i0 in range(0, NT, 2):
            self.run_lockstep([self.readout(rbs[k], kind, i0 + k, wg, gcol, ch0, wrow) for k in range(2) if i0 + k < NT])

    def readout_bufs(self, st):
        P = self.P
        sq_ = P.sb(st, [128, 512], F32, "sq")
        return (P.sb(st, [128, 512], F32, "otot"), sq_, P.sb(st, [128, 512], F32, "sg"),
                sq_, P.sb(st, [128, 4, 128], BF16, "brt"),
                P.sb(st, [128, 4, 6], F32, "s6h"), P.sb(st, [128, 4, 2], F32, "mvh"), P.sb(st, [128, 4], F32, "rsh"))

    def mixer_ret(self, l, s):
        cfg, P, dr = self.cfg, self.P, self.dr
        NT, TP = cfg.NT, cfg.TP
        with P.scope() as st:
            dm = {"f": self.const(st, "ret_dmf"), "b": self.const(st, "ret_dmb")}
            qd_c = {"f": self.const(st, "ret_qf"), "b": self.const(st, "ret_qb")}
            kd_c = {"f": self.const(st, "ret_kf"), "b": self.const(st, "ret_kb")}
            g128 = self.const(st, "ret_g128")
            wvg = P.sb(st, [128, KC, 1024], BF16, "rwv")
            self.loadw(wvg, "w_in", l, 512, 1024)
            qT = P.sb(st, [128, 2, TP], BF16, "rq")
            kTz = P.sb(st, [128, 4, TP], BF16, "rkTz")
            self.ms(kTz[:].rearrange("p a b -> p (a b)"), 0.0, [kTz])
            v_tm = P.sb(st, [128, NT, 512], BF16, "rv")
            k_tm = P.sb(st, [128, NT, 256], BF16, "rkt")
            with P.scope() as sp:
                cos, sin = self.const(sp, "cos"), self.const(sp, "sin")
                wt = P.sb(sp, [128, KC, 512], BF16, "rwt")
                self.loadw(wt, "w_in", l, 0, 512)
                wr = P.sb(sp, [128, KC, 512], BF16, "rwr")
                self.loadw(wr, "w_rope", l, 0, 512)
                t1 = P.sb(sp, [128, 512], F32, "rt1")
                t2 = P.sb(sp, [128, 512], F32, "rt2")
                k32 = P.sb(sp, [128, 2, TP], F32, "rk32")
                for a in range(2):
                    scl = 1.0 if a == 0 else 0.125
                    for c in range(2):
                        for b in range(len(cfg.blocks)):
                            c0, n = cfg.blocks[b]
                            pp = self.proj_fm(wt, a * 256 + c * 128, b)
                            pr = self.proj_fm(wr, a * 256 + c * 128, b)
                            self.stt(t1[:, 0:n], pp[:, 0:n], scl, cos[:, c0:c0 + n], ALU.mult, ALU.mult, [pp, cos], [t1])
                            self.stt(t2[:, 0:n], pr[:, 0:n], scl, sin[:, c0:c0 + n], ALU.mult, ALU.mult, [pr, sin], [t2])
                            if a == 0:
                                self.tt(qT[:, c, c0:c0 + n], t1[:, 0:n], t2[:, 0:n], ALU.add, [t1, t2], [qT])
                            else:
                                self.tt(k32[:, c, c0:c0 + n], t1[:, 0:n], t2[:, 0:n], ALU.add, [t1, t2], [k32])
                                for hh in range(2):
                                    self.cp(kTz[hh * 64:(hh + 1) * 64, 2 * c + hh, c0:c0 + n], k32[hh * 64:(hh + 1) * 64, c, c0:c0 + n],
                                            [k32], [kTz])
                for i in range(NT):
                    pv = self.proj_tm(wvg, 0, 512, i)
                    self.cp(v_tm[:, i, :], pv[:], [pv], [v_tm])
                    pt = P.ps()
                    for c in range(2):
                        self.tr(pt[:, c * 128:(c + 1) * 128], k32[:, c, cfg.ts[i]:cfg.ts[i] + 128], self.ident[:], [k32, self.ident], [pt])
                    self.cp(k_tm[:, i, :], pt[:, 0:256], [pt], [k_tm])
            def temps():
                kdz = P.sb(st, [128, 4, 128], BF16, "rkdz")
                self.ms(kdz[:].rearrange("p a b -> p (a b)"), 0.0, [kdz])
                qd = P.sb(st, [128, 4, 128], BF16, "rqd")
                self.ms(qd[:].rearrange("p a b -> p (a b)"), 0.0, [qd])
                return dict(S=P.sb(st, [128, 2, 128], BF16, "rS"), kdz=kdz, qd=qd, ST=P.sb(st, [128, 512], BF16, "rST"),
                            og=P.sb(st, [128, 512], F32, "rog"))

            def dpass(di, T):
                dname, order = (("f", cfg.fwd), ("b", cfg.bwd))[di]
                S, kdz, qd, ST, og = T["S"], T["kdz"], T["qd"], T["ST"], T["og"]
                self.ms(S[:].rearrange("p a b -> p (a b)"), 0.0, [S])
                for i in order:
                    tc = cfg.ts[i]
                    for h in range(4):
                        pr_, a = h // 2, h % 2
                        self.tt(qd[a * 64:(a + 1) * 64, h, :], qT[a * 64:(a + 1) * 64, pr_, tc:tc + 128],
                                qd_c[dname][a * 64:(a + 1) * 64, pr_ * 128:(pr_ + 1) * 128], ALU.mult, [qT, qd_c[dname]], [qd])
                    psc = P.ps(True)
                    for h in range(4):
                        self.mm(psc[:, h * 128:(h + 1) * 128], kTz[:, h, tc:tc + 128], qT[:, h // 2, tc:tc + 128], True, True,
                                [kTz, qT], [psc])
                    yield
                    self.tt(ST[:], psc[:], dm[dname][:], ALU.mult, [psc, dm[dname]], [ST])
                    P.rel(psc)
                    for h in range(4):
                        a = h % 2
                        self.act(kdz[:, h, a * 64:(a + 1) * 64], k_tm[:, i, h * 64:(h + 1) * 64], AF.Copy, [k_tm, kd_c[dname]], [kdz],
                                 scale=kd_c[dname][:, h:h + 1])
                    yield
                    po = P.ps(True)
                    for h in range(4):
                        self.mm(po[:, h * 128:(h + 1) * 128], ST[:, h * 128:(h + 1) * 128], v_tm[:, i, h * 128:(h + 1) * 128],
                                True, False, [ST, v_tm], [po])
                        self.mm(po[:, h * 128:(h + 1) * 128], qd[:, h, :], S[:, h // 2, :], False, True, [qd, S], [po])
                    pS = P.ps(True)
                    for pr_ in range(2):
                        for a in range(2):
                            h = 2 * pr_ + a
                            self.mm(pS[:, pr_ * 128:(pr_ + 1) * 128], kdz[:, h, :], v_tm[:, i, h * 128:(h + 1) * 128],
                                    a == 0, a == 1, [kdz, v_tm], [pS])
                    yield
                    self.put_opart(po, og, di, i)
                    for pr_ in range(2):
                        self.stt(S[:, pr_, :], S[:, pr_, :], g128[:, pr_:pr_ + 1], pS[:, pr_ * 128:(pr_ + 1) * 128],
                                 ALU.mult, ALU.add, [S, g128, pS], [S])
                    P.rel(po, pS)
                    yield

            TT_ = [temps(), temps()]
            self.run_lockstep([dpass(0, TT_[0]), dpass(1, TT_[1])])
            self.readout_sweep(st, "ln", wvg, 512, 0, None)

    def mixer_hg(self, l, s):
        cfg, P, dr = self.cfg, self.P, self.dr
        NT, TP = cfg.NT, cfg.TP
        with P.scope() as st:
            cc = {"f": self.const(st, "hg_cc_f"), "b": self.const(st, "hg_cc_b")}
            cgt = {"f": self.const(st, "hg_cgt_f"), "b": self.const(st, "hg_cgt_b")}
            msk = {"f": self.const(st, "hg_mask_f"), "b": self.const(st, "hg_mask_b")}
            clast = self.const(st, "hg_clast")
            lbT = P.sb(st, [128, 8], F32, "lbT")
            omlT = P.sb(st, [128, 8], F32, "omlT")
            nomlT = P.sb(st, [128, 8], F32, "nomlT")
            lbrow = P.sb(st, [128, 1024], F32, "lbrow")
            omlrow = P.sb(st, [128, 1024], F32, "omlrow")
            if l == 0:
                self.ms(lbT[:], 0.0, [lbT])
                self.ms(lbrow[:], 0.0, [lbrow])
            else:
                with P.scope() as s1:
                    a0 = P.sb(s1, [128, 8], F32, "a0")
                    r0 = P.sb(s1, [128, 1024], F32, "r0")
                    P.dma("sp", a0[:], dr["hg_lbT"][0], (), [a0])
                    P.dma("sp", lbT[:], dr["hg_lbT"][1], (), [lbT])
                    self.tt(lbT[:], lbT[:], a0[:], ALU.subtract, [lbT, a0], [lbT])
                    self.act(lbT[:], lbT[:], AF.Sigmoid, [lbT], [lbT])
                    P.dma("sp", r0[:], dr["hg_lb"][0].to_broadcast([128, 1024]), (), [r0])
                    P.dma("sp", lbrow[:], dr["hg_lb"][1].to_broadcast([128, 1024]), (), [lbrow])
                    self.tt(lbrow[:], lbrow[:], r0[:], ALU.subtract, [lbrow, r0], [lbrow])
                    self.act(lbrow[:], lbrow[:], AF.Sigmoid, [lbrow], [lbrow])
            self.ts(omlT[:], lbT[:], -1.0, 1.0, ALU.mult, ALU.add, [lbT], [omlT])
            self.ts(nomlT[:], omlT[:], -1.0, None, ALU.mult, None, [omlT], [nomlT])
            self.ts(omlrow[:], lbrow[:], -1.0, 1.0, ALU.mult, ALU.add, [lbrow], [omlrow])
            wig = P.sb(st, [128, KC, 1024], BF16, "hwig")
            self.loadw(wig, "w_in", l, OFF["hg_i"], 1024)
            qT = P.sb(st, [128, 4, TP], BF16, "hqT")
            with P.scope() as sp:
                wq = P.sb(sp, [128, KC, 512], BF16, "hwq")
                self.loadw(wq, "w_in", l, OFF["hg_q"], 512)
                for h in range(4):
                    for b in range(len(cfg.blocks)):
                        c0, n = cfg.blocks[b]
                        pp = self.proj_fm(wq, h * 128, b)
                        self.act(qT[:, h, c0:c0 + n], pp[:, 0:n], AF.Silu, [pp], [qT])
            wrow = self.rowbc(st, "hg_norm", l, 512)

            def temps(di):
                F = lambda nm, dt=F32: P.sb(st, [128, 512], dt, nm)
                T = dict(sig=F("hsig"), logf=F("hlogf"), ktm=F("hktm"), sigT=F("hsigT"), kTt=F("hkTt"), E1=F("hE1"), Ec=F("hEc"),
                         EL=P.sb(st, [128, 16], F32, "hEL"), qq=F("hqq", BF16), kk=F("hkk", BF16), Kd=F("hKd", BF16),
                         ST=F("hST", BF16), v=F("hv", BF16), QeZ=P.sb(st, [128, 4 * 512], BF16, "hQeZ"),
                         KdZ=P.sb(st, [128, 4, 512], BF16, "hKdZ"), Sst=P.sb(st, [128, 5, 512], BF16, "hSst"),
                         wf=P.sb(st, [128, KC, 512], BF16, "hwf"), og=F("hog"))
                self.ms(T["QeZ"][:], 0.0, [T["QeZ"]])
                self.loadw(T["wf"], "w_in", l, OFF["hg_ff"] if di == 0 else OFF["hg_fb"], 512)
                return T

            def dpass(di, T):
                dname, order = (("f", cfg.fwd), ("b", cfg.bwd))[di]
                sig, logf, ktm, sigT, kTt, E1, Ec, EL = T["sig"], T["logf"], T["ktm"], T["sigT"], T["kTt"], T["E1"], T["Ec"], T["EL"]
                qq, kk, Kd, ST, vt, QeZ, KdZ, Sst, wf, og = (T["qq"], T["kk"], T["Kd"], T["ST"], T["v"], T["QeZ"], T["KdZ"], T["Sst"],
                                                             T["wf"], T["og"])
                E2, Ed = sigT, sig
                self.ms(Sst[:, 0, :], 0.0, [Sst])
                corder = [0, 1, 2, 3] if di == 0 else [3, 2, 1, 0]
                for i in order:
                    tc = cfg.ts[i]
                    hb = self.hTb[cfg.tile_block(i)]
                    pf = self.proj_tm(wf, 0, 512, i)
                    self.act(sig[:], pf[:], AF.Sigmoid, [pf], [sig])
                    pv = self.proj_tm(wig, 0, 512, i)
                    self.act(vt[:], pv[:], AF.Identity, [pv], [vt])
                    yield
                    if l > 0:
                        self.tt(sig[:], sig[:], omlrow[:, di * 512:(di + 1) * 512], ALU.mult, [sig, omlrow], [sig])
                        self.tt(sig[:], sig[:], lbrow[:, di * 512:(di + 1) * 512], ALU.add, [sig, lbrow], [sig])
                    self.act(logf[:], sig[:], AF.Ln, [sig], [logf])
                    self.ts(ktm[:], sig[:], -1.0, 1.0, ALU.mult, ALU.add, [sig], [ktm], eng="pool")
                    yield
                    pff = P.ps(True)
                    for h in range(4):
                        self.tr(pff[:, h * 128:(h + 1) * 128], ktm[:, h * 128:(h + 1) * 128], self.ident[:], [ktm, self.ident], [pff])
                    yield
                    self.act(kTt[:], pff[:], AF.Identity, [pff], [kTt])
                    P.rel(pff)
                    pcr, pcc, pcl = P.ps(True), P.ps(True), P.ps(True)
                    for h in range(4):
                        lf = logf[:, h * 128:(h + 1) * 128]
                        self.mm(pcr[:, h * 128:(h + 1) * 128], lf, cc[dname][:, 0:128], True, True, [logf, cc[dname]], [pcr])
                        self.mm(pcc[:, h * 128:(h + 1) * 128], lf, cc[dname][:, 128:256], True, True, [logf, cc[dname]], [pcc])
                        self.mm(pcl[:, h * 4:(h + 1) * 4], lf, clast[:], True, True, [logf, clast], [pcl])
                    yield
                    self.act(E1[:], pcr[:], AF.Exp, [pcr], [E1])
                    self.act(E2[:], pcr[:], AF.Exp, [pcr], [E2], scale=-1.0)
                    self.act(Ec[:], pcc[:], AF.Exp, [pcc], [Ec])
                    self.act(EL[:], pcl[:, 0:16], AF.Exp, [pcl], [EL])
                    P.rel(pcr, pcc, pcl)
                    plc = P.ps(True)
                    self.mm(plc[:], cgt[dname][:], logf[:], True, True, [cgt[dname], logf], [plc])
                    yield
                    self.tt(qq[:].rearrange("p (h t) -> p h t", h=4), qT[:, :, tc:tc + 128], E1[:].rearrange("p (h t) -> p h t", h=4),
                            ALU.mult, [qT, E1], [qq])
                    self.tt(kk[:], kTt[:], E2[:], ALU.mult, [kTt, E2], [kk])
                    self.act(Ed[:], plc[:], AF.Exp, [plc], [Ed])
                    P.rel(plc)
                    yield
                    self.tt(mkap(QeZ[:], [[512, 4], [160, 4], [1, 32]]),
                            qT[:, :, tc:tc + 128].rearrange("p h (c j) -> p h c j", c=4),
                            Ec[:].rearrange("p (h c j) -> p h c j", h=4, c=4), ALU.mult, [qT, Ec], [QeZ])
                    self.tt(Kd[:], ktm[:], Ed[:], ALU.mult, [ktm, Ed], [Kd], eng="pool")
                    psc = P.ps(True)
                    for h in range(4):
                        self.mm(psc[:, h * 128:(h + 1) * 128], kk[:, h * 128:(h + 1) * 128], qq[:, h * 128:(h + 1) * 128],
                                True, True, [kk, qq], [psc])
                    yield
                    for c in range(4):
                        self.act(KdZ[:, c, :], Kd[:], AF.Copy, [Kd, clast], [KdZ], scale=clast[:, c:c + 1])
                    self.tt(ST[:].rearrange("p (h t) -> p h t", h=4), psc[:].rearrange("p (h t) -> p h t", h=4),
                            mkap(msk[dname][:], [[0, 4], [1, 128]]), ALU.mult, [psc, msk[dname]], [ST])
                    P.rel(psc)
                    yield
                    for kx, c in enumerate(corder):
                        pS = P.ps(True)
                        for h in range(4):
                            self.mm(pS[:, h * 128:(h + 1) * 128], KdZ[:, c, h * 128:(h + 1) * 128], vt[:, h * 128:(h + 1) * 128],
                                    True, True, [KdZ, vt], [pS])
                        yield
                        for h in range(4):
                            self.stt(Sst[:, kx + 1, h * 128:(h + 1) * 128], Sst[:, kx, h * 128:(h + 1) * 128],
                                     EL[:, h * 4 + c:h * 4 + c + 1], pS[:, h * 128:(h + 1) * 128], ALU.mult, ALU.add,
                                     [Sst, EL, pS], [Sst])
                        P.rel(pS)
                        yield
                    po = P.ps(True)
                    for h in range(4):
                        self.mm(po[:, h * 128:(h + 1) * 128], ST[:, h * 128:(h + 1) * 128], vt[:, h * 128:(h + 1) * 128],
                                True, False, [ST, vt], [po])
                        for kx, c in enumerate(corder):
                            self.mm(po[:, h * 128:(h + 1) * 128], QeZ[:, h * 512 + c * 128:h * 512 + (c + 1) * 128],
                                    Sst[:, kx, h * 128:(h + 1) * 128], False, kx == 3, [QeZ, Sst], [po])
                    yield
                    self.put_opart(po, og, di, i)
                    P.rel(po)
                    self.cp(Sst[:, 0, :], Sst[:, 4, :], [Sst], [Sst], eng="pool")
                    yield

            TT_ = [temps(0), temps(1)]
            self.run_lockstep([dpass(0, TT_[0]), dpass(1, TT_[1])])
            self.readout_sweep(st, "rms", wig, 512, 4, wrow)

    def mixer_gdn(self, l, s):
        cfg, P, dr = self.cfg, self.P, self.dr
        NT, TP = cfg.NT, cfg.TP
        ident, ones = self.ident, self.CN["ones"]
        hv = lambda t_: t_[:].rearrange("p (h t) -> p h t", h=4)
        with P.scope() as st:
            cinc = {"f": self.const(st, "gd_cinc_f"), "b": self.const(st, "gd_cinc_b")}
            cgt = {"f": self.const(st, "gd_cgt_f"), "b": self.const(st, "gd_cgt_b")}
            mb = {"f": self.const(st, "gd_mb_f"), "b": self.const(st, "gd_mb_b")}
            stc = {"f": self.const(st, "gd_st_f"), "b": self.const(st, "gd_st_b")}
            mo = {"f": self.const(st, "gd_mo_f"), "b": self.const(st, "gd_mo_b")}
            qT = P.sb(st, [128, 2, TP], BF16, "gqT")
            kTz = P.sb(st, [128, 4, TP], BF16, "gkTz")
            self.ms(kTz[:].rearrange("p a b -> p (a b)"), 0.0, [kTz])
            k_tm = P.sb(st, [128, NT, 256], BF16, "gktm")
            v_tm = P.sb(st, [128, NT, 512], BF16, "gvtm")
            with P.scope() as sp:
                bo64 = self.const(sp, "bo64")
                cw = P.sb(sp, [128, 8, 4], F32, "gcw")
                P.dma("sp", cw[:], dr["gdn_convT"][l], (), [cw])
                wqkv = P.sb(sp, [128, KC, 1024], BF16, "gwqkv")
                self.loadw(wqkv, "w_in", l, OFF["gdn_qkv"], 1024)
                xs = P.sb(sp, [128, TP], F32, "gxs")
                y = P.sb(sp, [128, TP], F32, "gy")
                sq = P.sb(sp, [128, 512], F32, "gsq")
                rn = P.sb(sp, [128, 512], F32, "grn")
                self.ms(xs[:, TP - 2:TP], 0.0, [xs])
                self.ms(y[:, 0:2], 0.0, [y])
                self.ms(y[:, TP - 2:TP], 0.0, [y])
                n_ = TP - 4
                for c in range(8):
                    for b in range(len(cfg.blocks)):
                        c0, n = cfg.blocks[b]
                        pp = self.proj_fm(wqkv, c * 128, b)
                        self.cp(xs[:, c0:c0 + n], pp[:, 0:n], [pp], [xs])
                    self.ts(y[:, 2:2 + n_], xs[:, 0:n_], cw[:, c, 0:1], None, ALU.mult, None, [xs, cw], [y])
                    for j in (1, 2, 3):
                        self.stt(y[:, 2:2 + n_], xs[:, j:j + n_], cw[:, c, j:j + 1], y[:, 2:2 + n_], ALU.mult, ALU.add, [xs, cw, y], [y])
                    self.act(y[:, 2:2 + n_], y[:, 2:2 + n_], AF.Silu, [y], [y])
                    if c < 4:
                        for b in range(len(cfg.blocks)):
                            c0, n = cfg.blocks[b]
                            self.tt(sq[:, 0:n], y[:, c0:c0 + n], y[:, c0:c0 + n], ALU.mult, [y], [sq])
                            pss = P.ps()
                            self.mm(pss[:, 0:n], bo64[:], sq[:, 0:n], True, True, [bo64, sq], [pss])
                            self.act(rn[:, 0:n], pss[:, 0:n], AF.Sqrt, [pss, self.eps_t], [rn], bias=self.eps_t[:, 0:1])
                            self.V("dve", lambda e, n=n: e.reciprocal(out=rn[:, 0:n], in_=rn[:, 0:n]), [rn], [rn])
                            if c < 2:
                                self.stt(qT[:, c, c0:c0 + n], y[:, c0:c0 + n], 0.125, rn[:, 0:n], ALU.mult, ALU.mult, [y, rn], [qT])
                            else:
                                self.tt(y[:, c0:c0 + n], y[:, c0:c0 + n], rn[:, 0:n], ALU.mult, [y, rn], [y])
                                for hh in range(2):
                                    self.cp(kTz[hh * 64:(hh + 1) * 64, 2 * (c - 2) + hh, c0:c0 + n], y[hh * 64:(hh + 1) * 64, c0:c0 + n],
                                            [y], [kTz])
                    if c >= 2:
                        for i in range(NT):
                            pt = P.ps()
                            self.tr(pt[:, 0:128], y[:, cfg.ts[i]:cfg.ts[i] + 128], ident[:], [y, ident], [pt])
                            if c < 4:
                                self.cp(k_tm[:, i, (c - 2) * 128:(c - 1) * 128], pt[:, 0:128], [pt], [k_tm])
                            else:
                                self.cp(v_tm[:, i, (c - 4) * 128:(c - 3) * 128], pt[:, 0:128], [pt], [v_tm])
            alog = self.rowbc(st, "gdn_alog", l, 8)
            dtb = self.rowbc(st, "gdn_dtb", l, 8)
            negA = P.sb(st, [128, 8], F32, "gnegA")
            self.act(negA[:], alog[:], AF.Exp, [alog], [negA])
            self.ts(negA[:], negA[:], -1.0, None, ALU.mult, None, [negA], [negA])
            wab = P.sb(st, [128, KC, 16], BF16, "gwab")
            self.loadw(wab, "w_in", l, OFF["gdn_a"], 16)
            wg = P.sb(st, [128, KC, 512], BF16, "gwg")
            self.loadw(wg, "w_in", l, OFF["gdn_g"], 512)
            wrow = self.rowbc(st, "gdn_norm", l, 512)
            def temps():
                F = lambda nm, dt=F32: P.sb(st, [128, 512], dt, nm)
                T = dict(G4=F("gG4"), Cg4=F("gCg4"), Dinc=F("gDinc"), Xo=F("gXo", INV_DT), Bm=F("gBm", INV_DT), Cm=F("gCm", INV_DT),
                         ZTs=F("gZTs", INV_DT), Xb=F("gXb", INV_DT),
                         QKm=F("gQKm", BF16), Bmb=F("gBmb", BF16), U0=F("gU0", BF16), sm=P.sb(st, [128, 8, 4], F32, "gsm"),
                         Ke=P.sb(st, [128, 256], BF16, "gKe"), WZ=P.sb(st, [128, 4, 128], BF16, "gWZ"),
                         KdZ=P.sb(st, [128, 4, 128], BF16, "gKdZ"), QtZ=P.sb(st, [128, 4, 128], BF16, "gQtZ"),
                         MT=P.sb(st, [128, 2, 128], BF16, "gMT"), elp=P.sb(st, [128, 2], F32, "gelp"),
                         Gbp=P.sb(st, [128, 2, 128], F32, "gGbp"), QeT=P.sb(st, [128, 2, 128], F32, "gQeT"),
                         S=P.sb(st, [128, 2, 128], BF16, "gS"), og=F("gog"))
                for z in ("WZ", "KdZ", "QtZ"):
                    self.ms(T[z][:].rearrange("p a b -> p (a b)"), 0.0, [T[z]])
                return T

            def dpass(di, T):
                dname, order = (("f", cfg.fwd), ("b", cfg.bwd))[di]
                G4, Cg4, Dinc, Xo, Bm, Cm, ZTs = T["G4"], T["Cg4"], T["Dinc"], T["Xo"], T["Bm"], T["Cm"], T["ZTs"]
                Dst, X = G4, T["Xb"]
                QKm, Bmb, U0, sm, Ke, WZ, KdZ, QtZ = T["QKm"], T["Bmb"], T["U0"], T["sm"], T["Ke"], T["WZ"], T["KdZ"], T["QtZ"]
                MT, elp, Gbp, QeT, S, og = T["MT"], T["elp"], T["Gbp"], T["QeT"], T["S"], T["og"]
                g_, beta = sm[:, 3, :], sm[:, 4, :]
                self.ms(S[:].rearrange("p a b -> p (a b)"), 0.0, [S])
                ci, cg, mbd, std, mod = cinc[dname], cgt[dname], mb[dname], stc[dname], mo[dname]
                for i in order:
                    tc = cfg.ts[i]
                    pab = self.proj_tm(wab, 0, 16, i)
                    self.tt(sm[:, 0, :], pab[:, di * 4:di * 4 + 4], dtb[:, di * 4:di * 4 + 4], ALU.add, [pab, dtb], [sm])
                    self.act(beta, pab[:, 8 + di * 4:8 + di * 4 + 4], AF.Sigmoid, [pab], [sm])
                    self.act(sm[:, 1, :], sm[:, 0, :], AF.Exp, [sm], [sm])
                    self.act(sm[:, 2, :], sm[:, 1, :], AF.Ln, [sm], [sm], bias=1.0)
                    yield
                    self.tt(g_, sm[:, 2, :], negA[:, di * 4:di * 4 + 4], ALU.mult, [sm, negA], [sm])
                    pcm = P.ps(True)
                    self.mm(pcm[:, 0:4], ci[:], g_, True, True, [ci, sm], [pcm])
                    self.mm(pcm[:, 4:8], cg[:], g_, True, True, [cg, sm], [pcm])
                    self.mm(pcm[:, 8:12], ones[:], g_, True, True, [ones, sm], [pcm])
                    gb = mkap(g_, [[1, 4], [0, 128]])
                    self.tt(hv(G4), mkap(ones[:], [[0, 4], [1, 128]]), gb, ALU.mult, [ones, sm], [G4])
                    self.stt(hv(Cg4), mkap(ci[:], [[0, 4], [1, 128]]), -1.0, gb, ALU.mult, ALU.mult, [ci, sm], [Cg4])
                    yield
                    self.act(sm[:, 5:8, :].rearrange("p a b -> p (a b)"), pcm[:, 0:12], AF.Exp, [pcm], [sm])
                    P.rel(pcm)
                    self.cp(Gbp[:].rearrange("p r (a t) -> p r a t", a=2), hv(G4)[:, :, 0:64].rearrange("p (r a) t -> p r a t", a=2),
                            [G4], [Gbp])
                    prel = P.ps(True)
                    for h in range(4):
                        o_ = prel[:, h * 128:(h + 1) * 128]
                        self.mm(o_, G4[:, h * 128:(h + 1) * 128], ci[:], True, False, [G4, ci], [prel])
                        self.mm(o_, Cg4[:, h * 128:(h + 1) * 128], ones[:], False, False, [Cg4, ones], [prel])
                        self.mm(o_, ident[:], mbd[:], False, True, [ident, mbd], [prel])
                    pec = P.ps(True)
                    for pr_ in range(2):
                        self.mm(pec[:, pr_ * 128:(pr_ + 1) * 128], Gbp[:, pr_, :], ci[:], True, True, [Gbp, ci], [pec])
                    yield
                    self.act(Dinc[:], prel[:], AF.Exp, [prel], [Dinc])
                    self.act(QeT[:].rearrange("p a b -> p (a b)"), pec[:, 0:256], AF.Exp, [pec], [QeT])
                    P.rel(prel, pec)
                    for h in range(4):
                        a = h % 2
                        self.act(Ke[:, h * 64:(h + 1) * 64], k_tm[:, i, h * 64:(h + 1) * 64], AF.Copy, [k_tm, sm], [Ke], scale=sm[:, 5, h:h + 1])
                        self.act(KdZ[:, h, a * 64:(a + 1) * 64], k_tm[:, i, h * 64:(h + 1) * 64], AF.Copy, [k_tm, sm], [KdZ],
                                 scale=sm[:, 6, h:h + 1])
                    pkk, pqk = P.ps(True), P.ps(True)
                    for h in range(4):
                        self.mm(pkk[:, h * 128:(h + 1) * 128], kTz[:, h, tc:tc + 128], kTz[:, h, tc:tc + 128], True, True, [kTz], [pkk])
                        self.mm(pqk[:, h * 128:(h + 1) * 128], kTz[:, h, tc:tc + 128], qT[:, h // 2, tc:tc + 128], True, True, [kTz, qT], [pqk])
                    yield
                    self.tt(hv(Dst), hv(Dinc), mkap(std[:], [[0, 4], [1, 128]]), ALU.mult, [Dinc, std], [Dst])
                    self.tt(QKm[:], pqk[:], Dinc[:], ALU.mult, [pqk, Dinc], [QKm])
                    self.tt(QeT[:], QeT[:], qT[:, :, tc:tc + 128], ALU.mult, [QeT, qT], [QeT])
                    yield
                    for h in range(4):
                        hs_ = slice(h * 128, (h + 1) * 128)
                        self.stt(X[:, hs_], pkk[:, hs_], sm[:, 4, h:h + 1], Dst[:, hs_], ALU.mult, ALU.mult, [pkk, sm, Dst], [X])
                    P.rel(pkk, pqk)
                    idn = ident if INV_DT == F32 else self.identb
                    self.cp(hv(Bm), mkap(idn[:], [[0, 4], [1, 128]]), [idn], [Bm])
                    self.cp(hv(Cm), mkap(idn[:], [[0, 4], [1, 128]]), [idn], [Cm])
                    yield
                    for lv in range(7):
                        self.tt(hv(Xo), hv(X), mkap(mod[:, lv * 128:(lv + 1) * 128], [[0, 4], [1, 128]]), ALU.mult, [X, mod], [Xo])
                        pz = P.ps(True)
                        for h in range(4):
                            hs_ = slice(h * 128, (h + 1) * 128)
                            self.mm(pz[:, hs_], Xo[:, hs_], Cm[:, hs_], True, True, [Xo, Cm], [pz])
                        yield
                        self.act(ZTs[:], pz[:], AF.Identity, [pz], [ZTs])
                        P.rel(pz)
                        pb2, pc2 = P.ps(True), P.ps(True)
                        for h in range(4):
                            hs_ = slice(h * 128, (h + 1) * 128)
                            self.mm(pb2[:, hs_], ZTs[:, hs_], Bm[:, hs_], True, True, [ZTs, Bm], [pb2])
                            self.mm(pc2[:, hs_], Bm[:, hs_], ZTs[:, hs_], True, True, [Bm, ZTs], [pc2])
                        yield
                        self.tt(Bm[:], Bm[:], pb2[:], ALU.subtract, [Bm, pb2], [Bm])
                        self.tt(Cm[:], Cm[:], pc2[:], ALU.subtract, [Cm, pc2], [Cm])
                        P.rel(pb2, pc2)
                        yield
                    self.act(Bmb[:], Bm[:], AF.Identity, [Bm], [Bmb])
                    pu, pw = P.ps(True), P.ps(True)
                    for h in range(4):
                        hs_ = slice(h * 128, (h + 1) * 128)
                        self.mm(pu[:, hs_], Bmb[:, hs_], v_tm[:, i, hs_], True, True, [Bmb, v_tm], [pu])
                    for h in range(4):
                        self.mm(pw[:, h * 64:(h + 1) * 64], Bmb[:, h * 128:(h + 1) * 128], Ke[:, h * 64:(h + 1) * 64], True, True, [Bmb, Ke], [pw])
                    yield
                    for h in range(4):
                        a = h % 2
                        self.act(U0[:, h * 128:(h + 1) * 128], pu[:, h * 128:(h + 1) * 128], AF.Copy, [pu, sm], [U0], scale=sm[:, 4, h:h + 1])
                        self.ts(WZ[:, h, a * 64:(a + 1) * 64], pw[:, h * 64:(h + 1) * 64], sm[:, 4, h:h + 1], None, ALU.mult, None, [pw, sm], [WZ])
                    P.rel(pu, pw)
                    for pr_ in range(2):
                        for a in range(2):
                            self.cp(elp[a * 64:(a + 1) * 64, pr_:pr_ + 1], sm[a * 64:(a + 1) * 64, 7, 2 * pr_ + a:2 * pr_ + a + 1], [sm], [elp])
                    yield
                    pmt, pqt = P.ps(True), P.ps(True)
                    for pr_ in range(2):
                        ps_ = slice(pr_ * 128, (pr_ + 1) * 128)
                        for a in range(2):
                            h = 2 * pr_ + a
                            self.mm(pmt[:, ps_], WZ[:, h, :], KdZ[:, h, :], a == 0, a == 1, [WZ, KdZ], [pmt])
                        for a in range(2):
                            h = 2 * pr_ + a
                            self.mm(pqt[:, ps_], WZ[:, h, :], QKm[:, h * 128:(h + 1) * 128], a == 0, a == 1, [WZ, QKm], [pqt])
                    yield
                    for pr_ in range(2):
                        ps_ = slice(pr_ * 128, (pr_ + 1) * 128)
                        self.stt(MT[:, pr_, :], ident[:], elp[:, pr_:pr_ + 1], pmt[:, ps_], ALU.mult, ALU.subtract, [ident, elp, pmt], [MT])
                    for h in range(4):
                        pr_, a = h // 2, h % 2
                        self.tt(QtZ[a * 64:(a + 1) * 64, h, :], QeT[a * 64:(a + 1) * 64, pr_, :], pqt[a * 64:(a + 1) * 64, pr_ * 128:(pr_ + 1) * 128],
                                ALU.subtract, [QeT, pqt], [QtZ])
                    P.rel(pmt, pqt)
                    yield
                    po = P.ps(True)
                    for h in range(4):
                        hs_ = slice(h * 128, (h + 1) * 128)
                        self.mm(po[:, hs_], QKm[:, hs_], U0[:, hs_], True, False, [QKm, U0], [po])
                        self.mm(po[:, hs_], QtZ[:, h, :], S[:, h // 2, :], False, True, [QtZ, S], [po])
                    pS = P.ps(True)
                    for pr_ in range(2):
                        ps_ = slice(pr_ * 128, (pr_ + 1) * 128)
                        for a in range(2):
                            h = 2 * pr_ + a
                            self.mm(pS[:, ps_], KdZ[:, h, :], U0[:, h * 128:(h + 1) * 128], a == 0, False, [KdZ, U0], [pS])
                        self.mm(pS[:, ps_], MT[:, pr_, :], S[:, pr_, :], False, True, [MT, S], [pS])
                    yield
                    self.put_opart(po, og, di, i)
                    self.cp(S[:].rearrange("p a b -> p (a b)"), pS[:, 0:256], [pS], [S])
                    P.rel(po, pS)
                    yield

            TT_ = [temps(), temps()]
            self.run_lockstep([dpass(0, TT_[0]), dpass(1, TT_[1])])
            self.readout_sweep(st, "rms", wg, 0, 8, wrow)

    def ln_affine(self, xpre, xo, st3, grow_, brow_):
        s6, mv, rs = st3
        self.do_stats(xpre, s6, mv, rs)
        self.ts(xo[:], xpre[:], mv[:, 0:1], rs[:, 0:1], ALU.subtract, ALU.mult, [xpre, mv, rs], [xo])
        self.tt(xo[:], xo[:], grow_[:], ALU.mult, [xo, grow_], [xo])
        self.tt(xo[:], xo[:], brow_[:], ALU.add, [xo, brow_], [xo])

    def phase_c(self, l, s, last):
        cfg, P, dr = self.cfg, self.P, self.dr
        nseq, NT, TP, nct = cfg.nseq, cfg.NT, cfg.TP, cfg.nct
        with P.scope() as st:
            G1 = {s: self.grow(st, l, 2, s)}
            if not last:
                G1[nseq] = self.grow(st, l, 2, nseq)
            wbr = P.sb(st, [128, 16, D], BF16, "wbr")
            self._dst_tt = wbr
            self.load_cast(wbr[:], dr["w_branch"][l].rearrange("j (k p) n -> p (j k) n", p=128), 16, D)
            wout = P.sb(st, [128, KC, D], BF16, "wout")
            self.loadw(wout, "w_out", l, 0, D)
            lg = self.rowbc(st, "ln_mix_g", l)
            lb = self.rowbc(st, "ln_mix_b", l)
            brb = P.sb(st, [128, 16, 512], BF16, "brb")
            wmg = [P.sb(st, [128, KC, 512], BF16, "wmg") for _ in range(2)]
            gsb = [P.sb(st, [128, 512], F32, "gsb") for _ in range(2)]
            macc = P.sb(st, [128, 512], F32, "macc")
            mgT = P.sb(st, [128, KC, 512], BF16, "mgT")
            xts = [P.sb(st, [128, D], F32, "xc") for _ in range(2)]
            xps = [P.sb(st, [128, D], F32, "xp")] * 2
            xms = [P.sb(st, [128, D], F32, "xm") for _ in range(2)]
            xns = [P.sb(st, [128, D], F32, "xn")] * 2
            st3 = [self.st3(st) for _ in range(2)]
            st3b = [self.st3(st) for _ in range(2)]
            it = 0
            for b in range(len(cfg.blocks)):
                if last and b == 0:
                    continue
                c0, n = cfg.blocks[b]
                P.dma("sp", brb[:, :, 0:n], self.BRT[:, :, c0:c0 + n].rearrange("c p t -> p c t"), [self.bBRT], [brb])
                for nch in range(8):
                    wm = wmg[nch % 2]
                    for j in range(4):
                        self.loadw(wm, "w_in", l, OFF["mg"] + j * D + nch * 128, 128, d0=j * 128)
                    for j in range(4):
                        pp = self.proj_fm(wm, j * 128, b)
                        g = gsb[j % 2]
                        self.act(g[:, 0:n], pp[:, 0:n], AF.Sigmoid, [pp], [g])
                        pb = P.ps()
                        for kk in range(4):
                            self.mm(pb[:, 0:n], wbr[:, j * 4 + kk, nch * 128:(nch + 1) * 128], brb[:, j * 4 + kk, 0:n],
                                    kk == 0, kk == 3, [wbr, brb], [pb])
                        if j == 0:
                            self.tt(macc[:, 0:n], g[:, 0:n], pb[:, 0:n], ALU.mult, [g, pb], [macc])
                        else:
                            self.tt(g[:, 0:n], g[:, 0:n], pb[:, 0:n], ALU.mult, [g, pb], [g])
                            if j < 3:
                                self.tt(macc[:, 0:n], macc[:, 0:n], g[:, 0:n], ALU.add, [macc, g], [macc])
                            else:
                                self.tt(mgT[:, nch, 0:n], macc[:, 0:n], g[:, 0:n], ALU.add, [macc, g], [mgT])
                for i in range(NT):
                    if cfg.tile_block(i) != b:
                        continue
                    tc = cfg.ts[i] - c0
                    xt, xp, xm, xn = xts[it % 2], xps[it % 2], xms[it % 2], xns[it % 2]
                    slot = s if i >= nct else nseq
                    src, rb = self.x_src(l, s, i)
                    P.dma("sp", xt[:], src, rb, [xt])
                    for nb in range(2):
                        py = P.ps()
                        for k in range(KC):
                            self.mm(py[:], mgT[:, k, tc:tc + 128], wout[:, k, nb * 512:(nb + 1) * 512], k == 0, k == KC - 1,
                                    [mgT, wout], [py])
                        sl_ = slice(nb * 512, (nb + 1) * 512)
                        self.tt(xp[:, sl_], py[:], G1[slot][:, sl_], ALU.mult, [py, G1[slot]], [xp])
                        self.stt(xp[:, sl_], xt[:, sl_], ALPHA, xp[:, sl_], ALU.mult, ALU.add, [xt, xp], [xp])
                    self.ln_affine(xp, xm, st3[it % 2], lg, lb)
                    P.dma(STQ, self.XM[s, i * 128:(i + 1) * 128, :], xm[:], [xm], [self.bXM])
                    self.ln_mod_T(xm, xn, st3b[it % 2], i, s, 3, 4)
                    it += 1

    def phase_d(self, l, s, last, hook=None):
        cfg, P, dr = self.cfg, self.P, self.dr
        TP = cfg.TP
        with P.scope() as st:
            cw = P.sb(st, [128, 44, 3], F32, "fcw")
            P.dma("sp", cw[:], dr["ffn_convT"][l], (), [cw])
            cb = P.sb(st, [128, 44], F32, "fcb")
            P.dma("sp", cb[:], dr["ffn_convbT"][l], (), [cb])
            wts = [P.sb(st, [128, KC, 256], BF16, "wup") for _ in range(2)]
            us = [[P.sb(st, [128, TP], F32, "us") for _ in range(2)] for _ in range(2)]
            cv = [[P.sb(st, [128, TP], F32, "cv") for _ in range(2)] for _ in range(2)]
            ab = [P.sb(st, [128, TP], BF16, "ab") for _ in range(2)]
            for a in us:
                for u in a:
                    self.ms(u[:], 0.0, [u])
            for a in ab:
                self.ms(a[:], 0.0, [a])
            n_ = TP - 4
            for c in range(22):
                if c == 2 and hook is not None:
                    hook()
                wt = wts[c % 2]
                self.loadw(wt, "w_up", l, c * 128, 128, d0=0)
                self.loadw(wt, "w_up", l, DFF + c * 128, 128, d0=128)
                u2, c2, a_ = us[c % 2], cv[c % 2], ab[c % 2]
                for b in range(len(cfg.blocks)):
                    if last and b == 0:
                        continue
                    c0, n = cfg.blocks[b]
                    for vg in range(2):
                        pp = self.proj_fm(wt, vg * 128, b)
                        if vg == 0:
                            self.cp(u2[vg][:, c0:c0 + n], pp[:, 0:n], [pp], [u2[vg]])
                        else:
                            self.act(u2[vg][:, c0:c0 + n], pp[:, 0:n], AF.Identity, [pp], [u2[vg]])
                for vg in range(2):
                    ch = c + 22 * vg
                    self.ts(c2[vg][:, 2:2 + n_], u2[vg][:, 1:1 + n_], cw[:, ch, 0:1], cb[:, ch:ch + 1], ALU.mult, ALU.add,
                            [u2[vg], cw, cb], [c2[vg]])
                    for j in (1, 2):
                        self.stt(c2[vg][:, 2:2 + n_], u2[vg][:, 1 + j:1 + j + n_], cw[:, ch, j:j + 1], c2[vg][:, 2:2 + n_],
                                 ALU.mult, ALU.add, [u2[vg], cw, c2[vg]], [c2[vg]])
                self.act(c2[1][:, 2:2 + n_], c2[1][:, 2:2 + n_], AF.Silu, [c2[1]], [c2[1]])
                self.tt(a_[:, 2:2 + n_], c2[1][:, 2:2 + n_], c2[0][:, 2:2 + n_], ALU.mult, [c2[0], c2[1]], [a_])
                P.dma(STQ, self.ACTD[c], a_[:], [a_], [self.bACTD])

    def phase_de(self, l, s, last):
        cfg, P, dr = self.cfg, self.P, self.dr
        nseq = cfg.nseq
        with P.scope() as st:
            G2 = {s: self.grow(st, l, 5, s)}
            if not last:
                G2[nseq] = self.grow(st, l, 5, nseq)
            wd = P.sb(st, [128, 22, D], BF16, "wd")
            lg = self.rowbc(st, "ln_ffn_g", l)
            lb = self.rowbc(st, "ln_ffn_b", l)

            def load_wd():
                self._dst_tt = wd
                self.load_cast(wd[:], dr["w_down"][l].rearrange("(k p) n -> p k n", p=128), 22, D)

            self.phase_d(l, s, last, hook=load_wd)
            self.phase_e(l, s, last, st, G2, wd, lg, lb)

    def phase_e(self, l, s, last, st, G2, wd, lg, lb):
        cfg, P, dr = self.cfg, self.P, self.dr
        nseq, NT, TP, nct = cfg.nseq, cfg.NT, cfg.TP, cfg.nct
        if True:
            ats = [P.sb(st, [128, 22, 128], BF16, "at") for _ in range(2)]
            xts = [P.sb(st, [128, D], F32, "xe") for _ in range(2)]
            xps = [P.sb(st, [128, D], F32, "xpe") for _ in range(2)]
            xos = [P.sb(st, [128, D], F32, "xo") for _ in range(2)]
            st3 = [self.st3(st) for _ in range(2)]
            it = 0
            for i in range(NT):
                if last and i < nct:
                    continue
                at, xt, xp, xo = ats[it % 2], xts[it % 2], xps[it % 2], xos[it % 2]
                slot = s if i >= nct else nseq
                tcol = cfg.ts[i]
                P.dma("sp", at[:], self.ACTD[:, :, tcol:tcol + 128].rearrange("c p t -> p c t"), [self.bACTD], [at])
                P.dma("sp", xt[:], self.XM[s, i * 128:(i + 1) * 128, :], [self.bXM], [xt])
                for nb in range(2):
                    py = P.ps()
                    for k in range(22):
                        self.mm(py[:], at[:, k, :], wd[:, k, nb * 512:(nb + 1) * 512], k == 0, k == 21, [at, wd], [py])
                    sl_ = slice(nb * 512, (nb + 1) * 512)
                    self.tt(xp[:, sl_], py[:], G2[slot][:, sl_], ALU.mult, [py, G2[slot]], [xp])
                    self.stt(xp[:, sl_], xt[:, sl_], ALPHA, xp[:, sl_], ALU.mult, ALU.add, [xt, xp], [xp])
                self.ln_affine(xp, xo, st3[it % 2], lg, lb)
                if last:
                    d_ = P.dma(STQ, self.out[s, (i - nct) * 128:(i - nct + 1) * 128, :], xo[:], [xo], [])
                    self.outs.append(d_)
                else:
                    P.dma(STQ, self.XA[s, i * 128:(i + 1) * 128, :], xo[:], [xo], [self.bXA])
                it += 1


_CACHE = {}


def run_cfg(inputs, cfg, core_batches):
    consts = make_consts(cfg)
    maps = []
    for cb in core_batches:
        m = host_layout(inputs, cfg, cb)
        for k, v in consts.items():
            m["c_" + k] = np.ascontiguousarray(v, dtype=np.float32)
        maps.append(m)
    shapes = {k: v.shape for k, v in maps[0].items()}
    key = (cfg.nseq, cfg.Tc, cfg.Tl, tuple(cfg.mixers))
    if key not in _CACHE:
        _CACHE[key] = Builder(cfg, shapes).build()
    nc = _CACHE[key]
    res = run_bass_kernel_spmd(nc, maps, core_ids=list(range(len(maps))))
    if getattr(cfg, "debug", False):
        global DBG
        DBG = res.results
    return np.concatenate([np.asarray(r["out"]) for r in res.results], axis=0)


def kernel(**inputs):
    B = np.asarray(inputs["x"]).shape[0]
    ncores = 8
    per = B // ncores
    cfg = Cfg(per, np.asarray(inputs["ctx"]).shape[1], np.asarray(inputs["x"]).shape[1])
    cbs = [list(range(c * per, (c + 1) * per)) for c in range(ncores)]
    return run_cfg(inputs, cfg, cbs).astype(np.float32)
```

```python
import contextlib
import numpy as np
import concourse.bass as bass
import concourse.mybir as mybir
from concourse.bass_utils import run_bass_kernel_spmd

F32 = mybir.dt.float32
BF16 = mybir.dt.bfloat16
AF = mybir.ActivationFunctionType
ALU = mybir.AluOpType

D = 1024
KC = 8
N_IN = 10768
DFF = 2816
EPS = 1e-6
ALPHA = 4.0 ** 0.25
OFF = dict(ret_q=0, ret_k=256, ret_v=512, ret_g=1024, hg_q=1536, hg_ff=2048, hg_fb=2560, hg_i=3072,
           hg_g=3584, gdn_qkv=4096, gdn_a=5120, gdn_b=5128, gdn_g=5136, lru_x=5648, lru_gate=6160, mg=6672)


class Buf:
    __slots__ = ("name", "w", "r")

    def __init__(self, name=""):
        self.name = name
        self.w = {}
        self.r = {}


class Ins:
    __slots__ = ("eng", "fn", "deps", "needed", "tok", "isdma")

    def __init__(self, eng, fn, isdma=False):
        self.eng = eng
        self.fn = fn
        self.deps = []
        self.needed = False
        self.tok = None
        self.isdma = isdma


class TT:
    __slots__ = ("t", "b", "held")

    def __init__(self, t, b):
        self.t = t
        self.b = b
        self.held = False

    def __getitem__(self, k):
        return self.t[k]


STQ = "act"
INV_DT = BF16
NOSYNC_ENGINES = ()


class Prog:
    NQ = 32

    def __init__(self, nc):
        self.nc = nc
        self.q = {"pe": [], "act": [], "dve": [], "pool": [], "sp": []}
        self.ndma = 0
        self.dmas = []
        self.bar_gen = 0
        self.bar_set = []
        self.eng_gen = {k: 0 for k in self.q}
        self.last = {k: None for k in self.q}
        self.stack = contextlib.ExitStack()
        self.psb = []
        self.psi = 0
        self.uid = 0

    def sb(self, st, shape, dt, name=None):
        self.uid += 1
        name = (name or "t") + "_%d" % self.uid
        t = st.enter_context(self.nc.sbuf_tensor(name, list(shape), dt))
        return TT(t, Buf(name))

    def init_psum(self):
        for i in range(8):
            t = self.stack.enter_context(self.nc.psum_tensor("ps%d" % i, [128, 512], F32))
            self.psb.append(TT(t, Buf("ps%d" % i)))

    def ps(self, hold=False):
        for _ in range(8):
            p = self.psb[self.psi % 8]
            self.psi += 1
            if not p.held:
                p.held = hold
                return p
        raise RuntimeError("no free PSUM bank")

    def rel(self, *banks):
        for p in banks:
            p.held = False

    @contextlib.contextmanager
    def scope(self):
        st = contextlib.ExitStack()
        try:
            yield st
        finally:
            self.barrier()
            st.close()

    def barrier(self):
        s = [v for v in self.last.values() if v is not None]
        s += self.dmas[-self.NQ:]
        self.bar_set = s
        self.bar_gen += 1

    def _add(self, eng, fn, reads, writes, isdma=False):
        ins = Ins(eng, fn, isdma)
        key = ("dma", self.ndma % self.NQ) if isdma else eng
        deps = {}

        def need(p):
            if p is None or p is ins:
                return
            if (not p.isdma) and (not isdma) and p.eng == eng and (eng == "pe" or eng in NOSYNC_ENGINES):
                return
            deps[id(p)] = p

        for b in reads:
            for p in b.w.values():
                need(p)
        for b in writes:
            for p in b.w.values():
                need(p)
            for p in b.r.values():
                need(p)
        if self.eng_gen[eng] < self.bar_gen:
            for p in self.bar_set:
                if p.isdma or p.eng != eng:
                    deps[id(p)] = p
            self.eng_gen[eng] = self.bar_gen
        if isdma:
            j = self.ndma
            if j >= self.NQ:
                deps[id(self.dmas[j - self.NQ])] = self.dmas[j - self.NQ]
            ins.tok = (j % self.NQ, 16 * (j // self.NQ + 1))
            self.dmas.append(ins)
            self.ndma += 1
        for p in deps.values():
            p.needed = True
        ins.deps = list(deps.values())
        for b in reads:
            b.r[key] = ins
        for b in writes:
            b.w[key] = ins
        self.q[eng].append(ins)
        if not isdma:
            self.last[eng] = ins
        return ins

    def op(self, eng, fn, reads=(), writes=()):
        return self._add(eng, fn, [x.b if isinstance(x, TT) else x for x in reads],
                         [x.b if isinstance(x, TT) else x for x in writes])

    def dma(self, eng, out, in_, reads=(), writes=()):
        return self._add(eng, lambda e: e.dma_start(out=out, in_=in_),
                         [x.b if isinstance(x, TT) else x for x in reads],
                         [x.b if isinstance(x, TT) else x for x in writes], isdma=True)

    def emit(self, final_wait=()):
        nc = self.nc
        st = self.stack
        esem = {k: st.enter_context(nc.semaphore("s_" + k)) for k in self.q}
        dsem = [st.enter_context(nc.semaphore("d%d" % i)) for i in range(self.NQ)]
        block = st.enter_context(nc.Block())
        for k, lst in self.q.items():
            c = 0
            for ins in lst:
                if ins.isdma:
                    continue
                if ins.needed:
                    c += 1
                    ins.tok = (k, c)

        def run(k, e):
            waited = {}
            for ins in self.q[k]:
                for p in ins.deps:
                    s, v = p.tok
                    if waited.get(s, 0) >= v:
                        continue
                    waited[s] = v
                    e.wait_ge(dsem[s] if p.isdma else esem[s], v)
                h = ins.fn(e)
                if ins.isdma:
                    h.then_inc(dsem[ins.tok[0]], 16)
                elif ins.needed:
                    h.then_inc(esem[k], 1)
            if k == "sp":
                for p in final_wait:
                    s, v = p.tok
                    e.wait_ge(dsem[s] if p.isdma else esem[s], v)

        @block.sync
        def _(e):
            run("sp", e)

        @block.scalar
        def _(e):
            run("act", e)

        @block.vector
        def _(e):
            run("dve", e)

        @block.gpsimd
        def _(e):
            run("pool", e)

        @block.tensor
        def _(e):
            run("pe", e)


def mkap(base, dims):
    return bass.AP(tensor=base.tensor, offset=base.offset, ap=[list(base.ap[0])] + [list(d) for d in dims])


class Cfg:
    def __init__(self, nseq, Tc, Tl, mixers=("ret", "hg", "gdn", "lru"), depth=2):
        self.nseq, self.Tc, self.Tl, self.mixers, self.depth = nseq, Tc, Tl, mixers, depth
        self.nct, self.nlt = Tc // 128, Tl // 128
        self.NT = self.nct + self.nlt
        self.LO = Tc + 4
        self.TP = self.LO + Tl + 2
        self.ts = [2 + 128 * i for i in range(self.nct)] + [self.LO + 128 * i for i in range(self.nlt)]
        self.blocks = [(0, self.LO)]
        g = 0
        while g * 512 < Tl:
            self.blocks.append((self.LO + 512 * g, min(512, Tl - 512 * g)))
            g += 1
        self.fwd = list(range(self.NT))
        self.bwd = list(range(self.nct - 1, -1, -1)) + list(range(self.NT - 1, self.nct - 1, -1))

    def tile_block(self, i):
        return 0 if i < self.nct else 1 + (i - self.nct) // 4


def make_consts(cfg):
    c = {}
    TP, LO = cfg.TP, cfg.LO
    c["ident"] = np.eye(128, dtype=np.float32)
    c["ones"] = np.ones((128, 128), np.float32)
    c["bo64"] = np.kron(np.eye(2), np.ones((64, 64))).astype(np.float32)
    inv_freq = (np.float32(10000.0) ** (-np.arange(16, dtype=np.float32) / np.float32(16))).astype(np.float32)
    cos = np.ones((128, TP), np.float32)
    sin = np.zeros((128, TP), np.float32)
    t = np.arange(cfg.Tl)
    row = (t // 64).astype(np.float32)
    col = (t % 64).astype(np.float32)
    for d in range(64):
        half, i, part = d // 32, d % 16, (d % 32) // 16
        pos = row if half == 0 else col
        ang = (pos * inv_freq[i]).astype(np.float32)
        cs, sn = np.cos(ang).astype(np.float32), np.sin(ang).astype(np.float32)
        for r in (d, d + 64):
            cos[r, LO:LO + cfg.Tl] = cs
            sin[r, LO:LO + cfg.Tl] = (-sn if part == 0 else sn)
    c["cos"], c["sin"] = cos, sin
    gam = (1.0 - 2.0 ** (-5.0 - np.arange(4))).astype(np.float64)
    s = np.arange(128)[:, None]
    tt = np.arange(128)[None, :]
    dmf = np.zeros((128, 4, 128))
    dmb = np.zeros((128, 4, 128))
    for h in range(4):
        dmf[:, h, :] = np.where(tt >= s, gam[h] ** np.maximum(tt - s, 0), 0.0)
        dmb[:, h, :] = np.where(s >= tt, gam[h] ** np.maximum(s - tt, 0), 0.0)
    c["ret_dmf"] = dmf.reshape(128, 512).astype(np.float32)
    c["ret_dmb"] = dmb.reshape(128, 512).astype(np.float32)
    qf = np.zeros((128, 2, 128))
    qb = np.zeros((128, 2, 128))
    g128 = np.zeros((128, 2))
    for p in range(2):
        for r in range(128):
            h = 2 * p + r // 64
            qf[r, p, :] = gam[h] ** (np.arange(128) + 1)
            qb[r, p, :] = gam[h] ** (128 - np.arange(128))
            g128[r, p] = gam[h] ** 128
    c["ret_qf"] = qf.reshape(128, 256).astype(np.float32)
    c["ret_qb"] = qb.reshape(128, 256).astype(np.float32)
    c["ret_g128"] = g128.astype(np.float32)
    kf = np.zeros((128, 4))
    kb = np.zeros((128, 4))
    for h in range(4):
        kf[:, h] = gam[h] ** (127 - np.arange(128))
        kb[:, h] = gam[h] ** np.arange(128)
    c["ret_kf"] = kf.astype(np.float32)
    c["ret_kb"] = kb.astype(np.float32)
    j = np.arange(128)[:, None]
    blk = lambda a: a // 32
    same = blk(j) == blk(tt)
    for nm, le, r_off in (("f", j <= tt, 15), ("b", j >= tt, 16)):
        ccum = (same & le).astype(np.float32)
        rcol = (32 * (np.arange(128) // 32) + r_off)
        cref = ccum - ccum[:, rcol]
        c["hg_cc_" + nm] = np.concatenate([cref, ccum], axis=1).astype(np.float32)
        c["hg_cgt_" + nm] = (same & (~le)).astype(np.float32)
        c["hg_mask_" + nm] = ccum.copy()
    clast = np.zeros((128, 4), np.float32)
    for cc in range(4):
        clast[32 * cc:32 * cc + 32, cc] = 1.0
    c["hg_clast"] = clast
    for nm, le in (("f", j <= tt), ("b", j >= tt)):
        c["gd_cinc_" + nm] = le.astype(np.float32)
        c["gd_cgt_" + nm] = (~le).astype(np.float32)
        c["gd_mb_" + nm] = np.where(le, 0.0, -1e5).astype(np.float32)
        c["gd_st_" + nm] = (le & (j != tt)).astype(np.float32)
        pos = (lambda a: a) if nm == "f" else (lambda a: 127 - a)
        mo = np.zeros((128, 7, 128), np.float32)
        for lv in range(7):
            b = 1 << lv
            ps_, pt_ = pos(j), pos(tt)
            mo[:, lv, :] = ((ps_ // (2 * b) == pt_ // (2 * b)) & (ps_ % (2 * b) < b) & (pt_ % (2 * b) >= b))
        c["gd_mo_" + nm] = mo.reshape(128, 7 * 128)
    return c


def host_layout(inputs, cfg, core_b):
    m = {}
    f = lambda a: np.ascontiguousarray(np.asarray(a, dtype=np.float32))
    L = cfg.depth
    m["x"] = f(inputs["x"][core_b])
    m["ctx"] = f(inputs["ctx"][core_b])
    cc = np.concatenate([np.asarray(inputs["c"])[core_b], np.asarray(inputs["c_ctx"])[None, :]], axis=0)
    m["cT"] = f(cc.T.reshape(KC, 128, cfg.nseq + 1).transpose(1, 0, 2))
    m["w_mod"] = f(inputs["w_mod"])
    m["b_modT"] = f(np.asarray(inputs["b_mod"]).reshape(L, 48, 128).transpose(0, 2, 1))
    m["b_mod"] = f(inputs["b_mod"]).reshape(L, 1, 6144)
    w_in = np.asarray(inputs["w_in"])
    m["w_in"] = f(w_in)
    perm = np.zeros(256, np.int64)
    for h in range(4):
        for d in range(64):
            part = (d % 32) // 16
            perm[h * 64 + d] = h * 64 + (d + 16 if part == 0 else d - 16)
    m["w_rope"] = f(np.concatenate([w_in[:, :, OFF["ret_q"] + perm], w_in[:, :, OFF["ret_k"] + perm]], axis=2))
    hl = np.asarray(inputs["hg_lb"])
    m["hg_lbT"] = f(hl.reshape(L, 2, 4, 128).transpose(0, 3, 1, 2).reshape(L, 128, 8))
    m["hg_lb"] = f(hl.reshape(L, 1, 1024))
    m["hg_norm"] = f(np.tile(np.asarray(inputs["hg_norm"]), (1, 4))).reshape(L, 1, 512)
    m["gdn_norm"] = f(np.tile(np.asarray(inputs["gdn_norm"]), (1, 4))).reshape(L, 1, 512)
    m["gdn_convT"] = f(np.asarray(inputs["gdn_conv"]).reshape(L, 4, 8, 128).transpose(0, 3, 2, 1))
    m["gdn_alog"] = f(inputs["gdn_a_log"]).reshape(L, 1, 8)
    m["gdn_dtb"] = f(inputs["gdn_dt_bias"]).reshape(L, 1, 8)
    m["lru_convT"] = f(np.asarray(inputs["lru_conv"]).reshape(L, 4, 4, 128).transpose(0, 3, 2, 1))
    m["lru_convbT"] = f(np.asarray(inputs["lru_conv_b"]).reshape(L, 4, 128).transpose(0, 2, 1))
    m["lru_w_a"] = f(inputs["lru_w_a"])
    m["lru_w_i"] = f(inputs["lru_w_i"])
    for nm in ("lru_b_a", "lru_b_i", "lru_lam"):
        m[nm + "T"] = f(np.asarray(inputs[nm]).reshape(L, 2, 4, 128).transpose(0, 3, 1, 2).reshape(L, 128, 8))
    m["w_branch"] = f(inputs["w_branch"])
    m["w_out"] = f(inputs["w_out"])
    m["w_up"] = f(inputs["w_up"])
    m["w_down"] = f(inputs["w_down"])
    m["ffn_convT"] = f(np.asarray(inputs["ffn_conv"]).reshape(L, 3, 44, 128).transpose(0, 3, 2, 1))
    m["ffn_convbT"] = f(np.asarray(inputs["ffn_conv_b"]).reshape(L, 44, 128).transpose(0, 2, 1))
    for nm in ("ln_mix_g", "ln_mix_b", "ln_ffn_g", "ln_ffn_b"):
        m[nm] = f(inputs[nm]).reshape(L, 1, D)
    return m


class Builder:
    def __init__(self, cfg, in_shapes):
        self.cfg = cfg
        nc = self.nc = bass.Bass("TRN2", target_bir_lowering=False)
        P = self.P = Prog(nc)
        self.dr = {k: nc.dram_tensor(k, list(s), F32, kind="ExternalInput").ap() for k, s in in_shapes.items()}
        nseq, NT, TP = cfg.nseq, cfg.NT, cfg.TP
        self.out = nc.dram_tensor("out", [nseq, cfg.Tl, D], F32, kind="ExternalOutput").ap()
        self.XA = nc.dram_tensor("XA", [nseq, NT * 128, D], F32, kind="Internal").ap()
        self.XM = nc.dram_tensor("XM", [nseq, NT * 128, D], F32, kind="Internal").ap()
        self.BRT = nc.dram_tensor("BRT", [16, 128, TP], BF16, kind=("ExternalOutput" if getattr(cfg, "debug", False) else "Internal")).ap()
        self.ACTD = nc.dram_tensor("ACTD", [22, 128, TP], BF16, kind="Internal").ap()
        self.bXA, self.bXM, self.bBRT, self.bACTD = Buf("XA"), Buf("XM"), Buf("BRT"), Buf("ACTD")
        self.OPD = nc.dram_tensor("OPD", [2, NT * 128, 512], F32, kind="Internal").ap()
        self.bOPD = Buf("OPD")
        P.init_psum()
        self.outs = []
        self.CN = {}
        top = P.stack
        self.in_shapes = in_shapes
        for k in ("ident", "ones"):
            self.CN[k] = P.sb(top, in_shapes["c_" + k], F32, "c_" + k)
            P.dma("sp", self.CN[k][:], self.dr["c_" + k][:], (), [self.CN[k]])
        self.ident = self.CN["ident"]
        self.eps_t = P.sb(top, [128, 1], F32, "eps")
        self.ms(self.eps_t[:], EPS, [self.eps_t])
        self.identb = P.sb(top, [128, 128], BF16, "identb")
        self.cp(self.identb[:], self.ident[:], [self.ident], [self.identb])
        self.stg = [P.sb(top, [128, 2048], F32, "stg") for _ in range(2)]
        self.stgi = 0

    def const(self, st, k):
        t_ = self.P.sb(st, self.in_shapes["c_" + k], F32, "c_" + k)
        self.P.dma("sp", t_[:], self.dr["c_" + k][:], (), [t_])
        return t_

    def V(self, eng, fn, r=(), w=()):
        return self.P.op(eng, fn, r, w)

    def mm(self, o, lhsT, rhs, start, stop, r, w):
        return self.P.op("pe", lambda e: e.matmul(o, lhsT=lhsT, rhs=rhs, start=start, stop=stop), r, w)

    def tr(self, o, in_, ident, r, w):
        return self.P.op("pe", lambda e: e.transpose(o, in_, ident), r, w)

    def act(self, o, i, func, r, w, scale=1.0, bias=0.0):
        return self.P.op("act", lambda e: e.activation(out=o, in_=i, func=func, bias=bias, scale=scale), r, w)

    def tt(self, o, a, b, op, r, w, eng="dve"):
        return self.P.op(eng, lambda e: e.tensor_tensor(out=o, in0=a, in1=b, op=op), r, w)

    def ts(self, o, a, s1, s2, op0, op1, r, w, eng="dve"):
        if op1 is None:
            return self.P.op(eng, lambda e: e.tensor_scalar(out=o, in0=a, scalar1=s1, scalar2=None, op0=op0), r, w)
        return self.P.op(eng, lambda e: e.tensor_scalar(out=o, in0=a, scalar1=s1, scalar2=s2, op0=op0, op1=op1), r, w)

    def stt(self, o, a, s, b, op0, op1, r, w):
        return self.P.op("dve", lambda e: e.scalar_tensor_tensor(out=o, in0=a, scalar=s, in1=b, op0=op0, op1=op1), r, w)

    def cp(self, o, i, r, w, eng="dve"):
        return self.P.op(eng, lambda e: e.tensor_copy(out=o, in_=i), r, w)

    def ms(self, o, val, w, eng="dve"):
        return self.P.op(eng, lambda e: e.memset(o, val), (), w)

    def wview(self, name, l, c0, n):
        return self.dr[name][l].rearrange("(k p) n -> p k n", p=128)[:, :, c0:c0 + n]

    def load_cast(self, dst, src, K_, n):
        P = self.P
        g = max(1, 2048 // n)
        for k0 in range(0, K_, g):
            kk = min(g, K_ - k0)
            sg = self.stg[self.stgi % 2]
            self.stgi += 1
            sv = sg[:, 0:kk * n].rearrange("p (k n) -> p k n", n=n)
            P.dma("sp", sv, src[:, k0:k0 + kk, :], (), [sg])
            self.cp(dst[:, k0:k0 + kk, :], sv, [sg], [self._dst_tt], eng="pool")

    def loadw(self, dst, name, l, c0, n, d0=0):
        self._dst_tt = dst
        self.load_cast(dst[:, :, d0:d0 + n], self.wview(name, l, c0, n), KC, n)

    def do_stats(self, xt, s6, mv, rs):
        for hh in range(2):
            self.V("dve", lambda e, hh=hh: e.bn_stats(out=s6[:, hh, :], in_=xt[:, hh * 512:(hh + 1) * 512]), [xt], [s6])
        self.V("dve", lambda e: e.bn_aggr(out=mv[:], in_=s6[:].rearrange("p a b -> p (a b)")), [s6], [mv])
        self.act(rs[:], mv[:, 1:2], AF.Sqrt, [mv, self.eps_t], [rs], bias=self.eps_t[:, 0:1])
        self.V("dve", lambda e: e.reciprocal(out=rs[:], in_=rs[:]), [rs], [rs])

    def x_src(self, l, s, i):
        nct = self.cfg.nct
        if l == 0:
            if i < nct:
                return self.dr["ctx"][s, i * 128:(i + 1) * 128, :], []
            return self.dr["x"][s, (i - nct) * 128:(i - nct + 1) * 128, :], []
        return self.XA[s, i * 128:(i + 1) * 128, :], [self.bXA]

    def ln_mod_T(self, xt, xn, st3, i, s, j_sh, j_sc):
        cfg, P = self.cfg, self.P
        s6, mv, rs = st3
        slot = s if i >= cfg.nct else cfg.nseq
        self.do_stats(xt, s6, mv, rs)
        self.ts(xn[:], xt[:], mv[:, 0:1], rs[:, 0:1], ALU.subtract, ALU.mult, [xt, mv, rs], [xn])
        hb = self.hTb[cfg.tile_block(i)]
        modT, hT = self.modT, self.hT
        for half in range(2):
            pt = P.ps()
            for k4 in range(4):
                k = half * 4 + k4
                self.tr(pt[:, k4 * 128:(k4 + 1) * 128], xn[:, k * 128:(k + 1) * 128], self.ident[:], [xn, self.ident], [pt])
            for k4 in range(4):
                k = half * 4 + k4
                self.act(hT[:, k, cfg.ts[i]:cfg.ts[i] + 128], pt[:, k4 * 128:(k4 + 1) * 128], AF.Identity,
                         [pt, modT], [hb], scale=modT[:, j_sc * 8 + k, slot:slot + 1],
                         bias=modT[:, j_sh * 8 + k, slot:slot + 1])

    def st3(self, st):
        P = self.P
        return (P.sb(st, [128, 2, 6], F32, "s6"), P.sb(st, [128, 2], F32, "mv"), P.sb(st, [128, 1], F32, "rs"))

    def proj_fm(self, wt, wcol, b):
        c0, n = self.cfg.blocks[b]
        pp = self.P.ps()
        for k in range(KC):
            self.mm(pp[:, 0:n], wt[:, k, wcol:wcol + 128], self.hT[:, k, c0:c0 + n], k == 0, k == KC - 1,
                    [wt, self.hTb[b]], [pp])
        return pp

    def proj_tm(self, wt, wcol, ncols, i):
        pp = self.P.ps()
        c = self.cfg.ts[i]
        for k in range(KC):
            self.mm(pp[:, 0:ncols], self.hT[:, k, c:c + 128], wt[:, k, wcol:wcol + ncols], k == 0, k == KC - 1,
                    [wt, self.hTb[self.cfg.tile_block(i)]], [pp])
        return pp

    def store_br(self, brt, chunk, i):
        c = self.cfg.ts[i]
        self.P.dma(STQ, self.BRT[chunk, :, c:c + 128], brt[:], [brt], [self.bBRT])

    def build(self):
        cfg, P, dr = self.cfg, self.P, self.dr
        nseq, NT, TP, nct = cfg.nseq, cfg.NT, cfg.TP, cfg.nct
        for l in range(cfg.depth):
            last = (l == cfg.depth - 1)
            with P.scope() as sl:
                self.modT = modT = P.sb(sl, [128, 48, nseq + 1], F32, "modT")
                self.scT = scT = P.sb(sl, [128, KC, nseq + 1], F32, "scT")
                P.dma("sp", scT[:], dr["cT"][:], (), [scT])
                self.act(scT[:], scT[:], AF.Silu, [scT], [scT])
                with P.scope() as s0:
                    bmt = P.sb(s0, [128, 48], F32, "bmt")
                    P.dma("sp", bmt[:], dr["b_modT"][l], (), [bmt])
                    pm = P.ps()
                    wb = [P.sb(s0, [128, KC, 512], F32, "wm%d" % i) for i in range(2)]
                    n1 = nseq + 1
                    for g in range(12):
                        w = wb[g % 2]
                        P.dma("sp", w[:], self.wview("w_mod", l, g * 512, 512), (), [w])
                        for cc in range(4):
                            q = g * 4 + cc
                            for k in range(KC):
                                self.mm(pm[:, q * n1:(q + 1) * n1], w[:, k, cc * 128:(cc + 1) * 128], scT[:, k, :],
                                        k == 0, k == KC - 1, [w, scT], [pm])
                    self.tt(modT[:], pm[:, 0:48 * n1].rearrange("p (q s) -> p q s", s=n1),
                            mkap(bmt[:], [[1, 48], [0, n1]]), ALU.add, [pm, bmt], [modT])
                    for jj in (1, 4):
                        self.ts(modT[:, jj * 8:(jj + 1) * 8, :], modT[:, jj * 8:(jj + 1) * 8, :], 1.0, None, ALU.add, None,
                                [modT], [modT])
                for s in range(nseq):
                    self.seq_layer(l, s, last, sl)
        P.emit(final_wait=self.outs)
        return self.nc

    def grow(self, st, l, jj, slot):
        P, dr, nseq = self.P, self.dr, self.cfg.nseq
        g = P.sb(st, [128, D], F32, "grow")
        with P.scope() as s1:
            scB = P.sb(s1, [128, KC, 128], F32, "scB")
            self.cp(scB[:], mkap(self.scT[:, 0, slot:slot + 1], [[nseq + 1, KC], [0, 128]]), [self.scT], [scB])
            brow = P.sb(s1, [1, D], F32, "brow")
            P.dma("sp", brow[:], dr["b_mod"][l][:, jj * D:(jj + 1) * D], (), [brow])
            w = P.sb(s1, [128, KC, 512], F32, "wg")
            ones = self.CN["ones"]
            for nb in range(2):
                P.dma("sp", w[:], self.wview("w_mod", l, jj * D + nb * 512, 512), (), [w])
                pg = P.ps()
                for k in range(KC):
                    self.mm(pg[:], scB[:, k, :], w[:, k, :], k == 0, False, [scB, w], [pg])
                self.mm(pg[:], ones[0:1, :], brow[0:1, nb * 512:(nb + 1) * 512], False, True, [ones, brow], [pg])
                self.cp(g[:, nb * 512:(nb + 1) * 512], pg[:], [pg], [g])
        return g

    def rowbc(self, st, name, l, n=D):
        t_ = self.P.sb(st, [128, n], F32, name)
        self.P.dma("sp", t_[:], self.dr[name][l].to_broadcast([128, n]), (), [t_])
        return t_

    def seq_layer(self, l, s, last, sl):
        cfg, P, dr = self.cfg, self.P, self.dr
        nseq, NT, TP, nct = cfg.nseq, cfg.NT, cfg.TP, cfg.nct
        with P.scope() as ss:
            self.hT = hT = P.sb(ss, [128, KC, TP], BF16, "hT")
            self.hTb = hTb = [Buf("hTb%d" % i) for i in range(len(cfg.blocks))]
            self.ms(hT[:, :, 0:2], 0.0, hTb[:1])
            self.ms(hT[:, :, cfg.LO - 2:cfg.LO], 0.0, hTb[:1])
            self.ms(hT[:, :, TP - 2:TP], 0.0, hTb[-1:])
            with P.scope() as sa:
                xts = [P.sb(sa, [128, D], F32, "xa") for _ in range(2)]
                xns = [P.sb(sa, [128, D], F32, "xn") for _ in range(2)]
                st3 = [self.st3(sa) for _ in range(2)]
                for i in range(NT):
                    xt = xts[i % 2]
                    src, rb = self.x_src(l, s, i)
                    P.dma("sp", xt[:], src, rb, [xt])
                    self.ln_mod_T(xt, xns[i % 2], st3[i % 2], i, s, 0, 1)
            if "lru" in cfg.mixers:
                self.mixer_lru(l, s)
            else:
                self.zero_branch(12)
            for nm, ch in (("ret", 0), ("hg", 4), ("gdn", 8)):
                if nm in cfg.mixers:
                    getattr(self, "mixer_" + nm)(l, s)
                else:
                    self.zero_branch(ch)
            self.phase_c(l, s, last)
            self.phase_de(l, s, last)

    def zero_branch(self, ch0):
        P = self.P
        with P.scope() as st:
            z = P.sb(st, [128, self.cfg.TP], BF16, "zb")
            self.ms(z[:], 0.0, [z])
            for c in range(4):
                P.dma(STQ, self.BRT[ch0 + c], z[:], [z], [self.bBRT])

    def mixer_lru(self, l, s):
        cfg, P, dr = self.cfg, self.P, self.dr
        TP, LO = cfg.TP, cfg.LO
        segs = [(2, cfg.Tc), (LO, cfg.Tl)]
        with P.scope() as st:
            cw = P.sb(st, [128, 4, 4], F32, "lcw")
            P.dma("sp", cw[:], dr["lru_convT"][l], (), [cw])
            cb = P.sb(st, [128, 4], F32, "lcb")
            P.dma("sp", cb[:], dr["lru_convbT"][l], (), [cb])
            par = {}
            for nm in ("lru_b_a", "lru_b_i", "lru_lam"):
                par[nm] = P.sb(st, [128, 8], F32, nm)
                P.dma("sp", par[nm][:], dr[nm + "T"][l], (), [par[nm]])
            y = P.sb(st, [128, 8], F32, "ly")
            w_ = P.sb(st, [128, 8], F32, "lw")
            w2 = P.sb(st, [128, 8], F32, "lw2")
            pl = P.sb(st, [128, 8], F32, "lpl")
            nsp = P.sb(st, [128, 8], F32, "nsp")
            nsp2 = P.sb(st, [128, 8], F32, "nsp2")
            self.act(y[:], par["lru_lam"][:], AF.Exp, [par["lru_lam"]], [y], scale=-1.0)
            self.ts(w_[:], y[:], 2.0, None, ALU.add, None, [y], [w_])
            self.V("dve", lambda e: e.reciprocal(out=w_[:], in_=w_[:]), [w_], [w_])
            self.tt(w_[:], w_[:], y[:], ALU.mult, [w_, y], [w_])
            self.tt(w2[:], w_[:], w_[:], ALU.mult, [w_], [w2])
            self.ts(pl[:], w2[:], 1.0 / 9.0, 1.0 / 7.0, ALU.mult, ALU.add, [w2], [pl])
            for cf in (1.0 / 5.0, 1.0 / 3.0, 1.0):
                self.tt(pl[:], pl[:], w2[:], ALU.mult, [pl, w2], [pl])
                self.ts(pl[:], pl[:], cf, None, ALU.add, None, [pl], [pl])
            self.tt(pl[:], pl[:], w_[:], ALU.mult, [pl, w_], [pl])
            self.ts(nsp[:], pl[:], -16.0, None, ALU.mult, None, [pl], [nsp])
            self.ts(nsp2[:], pl[:], -32.0, None, ALU.mult, None, [pl], [nsp2])
            wbd = P.sb(st, [128, 2, 2, 4, 128], F32, "wbd")
            self.ms(wbd[:].rearrange("p a b c d -> p (a b c d)"), 0.0, [wbd])
            for ai, nm in enumerate(("lru_w_a", "lru_w_i")):
                for d_ in range(2):
                    for blk in range(8):
                        c, hf = blk // 2, blk % 2
                        P.dma("sp", wbd[hf * 64:(hf + 1) * 64, ai, d_, c, hf * 64:(hf + 1) * 64], dr[nm][l, d_, blk], (), [wbd])
            wt = P.sb(st, [128, KC, 1024], BF16, "lwt")
            self.loadw(wt, "w_in", l, OFF["lru_x"], 1024)
            def bufs():
                B = dict(xs=P.sb(st, [128, TP], F32, "lxs"), xc=P.sb(st, [128, TP], F32, "lxc"), hs=P.sb(st, [128, TP], F32, "lhs"),
                         av=P.sb(st, [128, TP], F32, "lav"), uv=P.sb(st, [128, TP], F32, "luv"), t1=P.sb(st, [128, TP], F32, "lt1"),
                         brt=P.sb(st, [128, TP], BF16, "lbr"))
                B["hd"] = B["t1"]
                self.ms(B["xs"][:, TP - 2:TP], 0.0, [B["xs"]])
                return B

            def chunk(c, B):
                xs, xc, hs, av, uv, t1, hd, brt = B["xs"], B["xc"], B["hs"], B["av"], B["uv"], B["t1"], B["hd"], B["brt"]
                gg = xs
                for b in range(len(cfg.blocks)):
                    c0, n = cfg.blocks[b]
                    pp = self.proj_fm(wt, c * 128, b)
                    self.cp(xs[:, c0:c0 + n], pp[:, 0:n], [pp], [xs])
                yield
                n_ = TP - 4
                self.ts(xc[:, 2:2 + n_], xs[:, 0:n_], cw[:, c, 0:1], cb[:, c:c + 1], ALU.mult, ALU.add, [xs, cw, cb], [xc])
                for j in (1, 2, 3):
                    self.stt(xc[:, 2:2 + n_], xs[:, j:j + n_], cw[:, c, j:j + 1], xc[:, 2:2 + n_], ALU.mult, ALU.add,
                             [xs, cw, xc], [xc])
                yield
                for d_ in range(2):
                    pc = d_ * 4 + c
                    for b in range(len(cfg.blocks)):
                        c0, n = cfg.blocks[b]
                        pa = P.ps()
                        self.mm(pa[:, 0:n], wbd[:, 0, d_, c, :], xc[:, c0:c0 + n], True, True, [wbd, xc], [pa])
                        pi = P.ps()
                        self.mm(pi[:, 0:n], wbd[:, 1, d_, c, :], xc[:, c0:c0 + n], True, True, [wbd, xc], [pi])
                        self.act(t1[:, c0:c0 + n], pa[:, 0:n], AF.Sigmoid, [pa, par["lru_b_a"]], [t1], bias=par["lru_b_a"][:, pc:pc + 1])
                        self.act(av[:, c0:c0 + n], t1[:, c0:c0 + n], AF.Exp, [t1, nsp], [av], scale=nsp[:, pc:pc + 1])
                        self.act(uv[:, c0:c0 + n], t1[:, c0:c0 + n], AF.Exp, [t1, nsp2], [uv], scale=nsp2[:, pc:pc + 1])
                        self.act(uv[:, c0:c0 + n], uv[:, c0:c0 + n], AF.Sqrt, [uv], [uv], scale=-1.0, bias=1.0)
                        self.act(t1[:, c0:c0 + n], pi[:, 0:n], AF.Sigmoid, [pi, par["lru_b_i"], t1], [t1], bias=par["lru_b_i"][:, pc:pc + 1])
                        self.tt(uv[:, c0:c0 + n], uv[:, c0:c0 + n], t1[:, c0:c0 + n], ALU.mult, [uv, t1], [uv])
                        self.tt(uv[:, c0:c0 + n], uv[:, c0:c0 + n], xc[:, c0:c0 + n], ALU.mult, [uv, xc], [uv])
                        yield
                    dst = hs if d_ == 0 else hd
                    if d_ == 0:
                        (a0, n0), (a1, n1) = segs
                        self.V("dve", lambda e, a0=a0, n0=n0, dst=dst: e.tensor_tensor_scan(
                            out=dst[:, a0:a0 + n0], data0=av[:, a0:a0 + n0], data1=uv[:, a0:a0 + n0], initial=0.0,
                            op0=ALU.mult, op1=ALU.add), [av, uv], [dst])
                        self.V("dve", lambda e, a0=a0, n0=n0, a1=a1, n1=n1, dst=dst: e.tensor_tensor_scan(
                            out=dst[:, a1:a1 + n1], data0=av[:, a1:a1 + n1], data1=uv[:, a1:a1 + n1],
                            initial=dst[:, a0 + n0 - 1:a0 + n0], op0=ALU.mult, op1=ALU.add), [av, uv, dst], [dst])
                    else:
                        (a0, n0), (a1, n1) = segs
                        self.V("dve", lambda e, a0=a0, n0=n0, dst=dst: e.tensor_tensor_scan(
                            out=dst[:, a0:a0 + n0][:, ::-1], data0=av[:, a0:a0 + n0][:, ::-1], data1=uv[:, a0:a0 + n0][:, ::-1],
                            initial=0.0, op0=ALU.mult, op1=ALU.add), [av, uv], [dst])
                        self.V("dve", lambda e, a0=a0, a1=a1, n1=n1, dst=dst: e.tensor_tensor_scan(
                            out=dst[:, a1:a1 + n1][:, ::-1], data0=av[:, a1:a1 + n1][:, ::-1], data1=uv[:, a1:a1 + n1][:, ::-1],
                            initial=dst[:, a0:a0 + 1], op0=ALU.mult, op1=ALU.add), [av, uv, dst], [dst])
                        self.tt(hs[:], hs[:], hd[:], ALU.add, [hs, hd], [hs])
                yield
                for b in range(len(cfg.blocks)):
                    c0, n = cfg.blocks[b]
                    pp = self.proj_fm(wt, 512 + c * 128, b)
                    self.cp(gg[:, c0:c0 + n], pp[:, 0:n], [pp], [gg])
                    self.tt(t1[:, c0:c0 + n], gg[:, c0:c0 + n], gg[:, c0:c0 + n], ALU.mult, [gg], [t1])
                    self.ts(t1[:, c0:c0 + n], t1[:, c0:c0 + n], 0.044715, 1.0, ALU.mult, ALU.add, [t1], [t1])
                    self.tt(t1[:, c0:c0 + n], t1[:, c0:c0 + n], gg[:, c0:c0 + n], ALU.mult, [t1, gg], [t1])
                    self.act(t1[:, c0:c0 + n], t1[:, c0:c0 + n], AF.Sigmoid, [t1], [t1], scale=1.5957691216057308)
                    self.tt(t1[:, c0:c0 + n], t1[:, c0:c0 + n], gg[:, c0:c0 + n], ALU.mult, [t1, gg], [t1])
                    self.tt(brt[:, c0:c0 + n], t1[:, c0:c0 + n], hs[:, c0:c0 + n], ALU.mult, [t1, hs], [brt])
                    yield
                self.ms(brt[:, TP - 2:TP], 0.0, [brt])
                P.dma(STQ, self.BRT[12 + c], brt[:], [brt], [self.bBRT])

            BB = [bufs(), bufs()]
            for c0_ in (0, 2):
                self.run_lockstep([chunk(c0_, BB[0]), chunk(c0_ + 1, BB[1])])


    @staticmethod
    def run_lockstep(gens):
        gens = list(gens)
        while gens:
            for g in list(gens):
                try:
                    next(g)
                except StopIteration:
                    gens.remove(g)

    def put_opart(self, po, stage, di, i):
        self.act(stage[:], po[:], AF.Identity, [po], [stage])
        self.P.dma(STQ, self.OPD[di, i * 128:(i + 1) * 128, :], stage[:], [stage], [self.bOPD])

    def readout(self, st_bufs, kind, i, wg, gcol, ch0, wrow):
        P = self.P
        otot, sq, sg, brb, brt, s6, mv, rs = st_bufs
        P.dma("sp", otot[:], self.OPD[0, i * 128:(i + 1) * 128, :], [self.bOPD], [otot])
        P.dma("sp", sq[:], self.OPD[1, i * 128:(i + 1) * 128, :], [self.bOPD], [sq])
        self.tt(otot[:], otot[:], sq[:], ALU.add, [otot, sq], [otot], eng="pool")
        pg = self.proj_tm(wg, gcol, 512, i)
        self.act(sg[:], pg[:], AF.Silu, [pg], [sg])
        yield
        if kind == "ln":
            for h in range(4):
                self.V("dve", lambda e, h=h: e.bn_stats(out=s6[:, h, :], in_=otot[:, h * 128:(h + 1) * 128]), [otot], [s6])
            for h in range(4):
                self.V("dve", lambda e, h=h: e.bn_aggr(out=mv[:, h, :], in_=s6[:, h, :]), [s6], [mv])
            self.act(rs[:], mv[:, :, 1], AF.Sqrt, [mv, self.eps_t], [rs], bias=self.eps_t[:, 0:1])
            yield
            self.V("dve", lambda e: e.reciprocal(out=rs[:], in_=rs[:]), [rs], [rs])
            for h in range(4):
                self.ts(otot[:, h * 128:(h + 1) * 128], otot[:, h * 128:(h + 1) * 128], mv[:, h, 0:1], rs[:, h:h + 1],
                        ALU.subtract, ALU.mult, [otot, mv, rs], [otot])
        else:
            self.tt(sq[:], otot[:], otot[:], ALU.mult, [otot], [sq])
            self.V("dve", lambda e: e.tensor_reduce(out=rs[:], in_=sq[:].rearrange("p (h e) -> p h e", h=4),
                                                    axis=mybir.AxisListType.X, op=ALU.add), [sq], [rs])
            self.act(rs[:], rs[:], AF.Sqrt, [rs, self.eps_t], [rs], scale=1.0 / 128.0, bias=self.eps_t[:, 0:1])
            yield
            self.V("dve", lambda e: e.reciprocal(out=rs[:], in_=rs[:]), [rs], [rs])
            for h in range(4):
                self.ts(otot[:, h * 128:(h + 1) * 128], otot[:, h * 128:(h + 1) * 128], rs[:, h:h + 1], None, ALU.mult, None,
                        [otot, rs], [otot])
        if wrow is not None:
            self.tt(sg[:], sg[:], wrow[:], ALU.mult, [sg, wrow], [sg], eng="pool")
        yield
        self.tt(brb[:], otot[:], sg[:], ALU.mult, [otot, sg], [brb])
        pt = P.ps(True)
        for h in range(4):
            self.tr(pt[:, h * 128:(h + 1) * 128], brb[:, h * 128:(h + 1) * 128], self.ident[:], [brb, self.ident], [pt])
        yield
        self.act(brt[:].rearrange("p c t -> p (c t)"), pt[:, 0:512], AF.Identity, [pt], [brt])
        P.rel(pt)
        c = self.cfg.ts[i]
        P.dma(STQ, self.BRT[ch0:ch0 + 4, :, c:c + 128].rearrange("c p t -> p c t"), brt[:], [brt], [self.bBRT])

    def readout_sweep(self, st, kind, wg, gcol, ch0, wrow):
        rbs = [self.readout_bufs(st) for _ in range(2)]
        NT = self.cfg.NT
        for i0 in range(0, NT, 2):
            self.run_lockstep([self.readout(rbs[k], kind, i0 + k, wg, gcol, ch0, wrow) for k in range(2) if i0 + k < NT])

    def readout_bufs(self, st):
        P = self.P
        sq_ = P.sb(st, [128, 512], F32, "sq")
        return (P.sb(st, [128, 512], F32, "otot"), sq_, P.sb(st, [128, 512], F32, "sg"),
                sq_, P.sb(st, [128, 4, 128], BF16, "brt"),
                P.sb(st, [128, 4, 6], F32, "s6h"), P.sb(st, [128, 4, 2], F32, "mvh"), P.sb(st, [128, 4], F32, "rsh"))

    def mixer_ret(self, l, s):
        cfg, P, dr = self.cfg, self.P, self.dr
        NT, TP = cfg.NT, cfg.TP
        with P.scope() as st:
            dm = {"f": self.const(st, "ret_dmf"), "b": self.const(st, "ret_dmb")}
            qd_c = {"f": self.const(st, "ret_qf"), "b": self.const(st, "ret_qb")}
            kd_c = {"f": self.const(st, "ret_kf"), "b": self.const(st, "ret_kb")}
            g128 = self.const(st, "ret_g128")
            wvg = P.sb(st, [128, KC, 1024], BF16, "rwv")
            self.loadw(wvg, "w_in", l, 512, 1024)
            qT = P.sb(st, [128, 2, TP], BF16, "rq")
            kTz = P.sb(st, [128, 4, TP], BF16, "rkTz")
            self.ms(kTz[:].rearrange("p a b -> p (a b)"), 0.0, [kTz])
            v_tm = P.sb(st, [128, NT, 512], BF16, "rv")
            k_tm = P.sb(st, [128, NT, 256], BF16, "rkt")
            with P.scope() as sp:
                cos, sin = self.const(sp, "cos"), self.const(sp, "sin")
                wt = P.sb(sp, [128, KC, 512], BF16, "rwt")
                self.loadw(wt, "w_in", l, 0, 512)
                wr = P.sb(sp, [128, KC, 512], BF16, "rwr")
                self.loadw(wr, "w_rope", l, 0, 512)
                t1 = P.sb(sp, [128, 512], F32, "rt1")
                t2 = P.sb(sp, [128, 512], F32, "rt2")
                k32 = P.sb(sp, [128, 2, TP], F32, "rk32")
                for a in range(2):
                    scl = 1.0 if a == 0 else 0.125
                    for c in range(2):
                        for b in range(len(cfg.blocks)):
                            c0, n = cfg.blocks[b]
                            pp = self.proj_fm(wt, a * 256 + c * 128, b)
                            pr = self.proj_fm(wr, a * 256 + c * 128, b)
                            self.stt(t1[:, 0:n], pp[:, 0:n], scl, cos[:, c0:c0 + n], ALU.mult, ALU.mult, [pp, cos], [t1])
                            self.stt(t2[:, 0:n], pr[:, 0:n], scl, sin[:, c0:c0 + n], ALU.mult, ALU.mult, [pr, sin], [t2])
                            if a == 0:
                                self.tt(qT[:, c, c0:c0 + n], t1[:, 0:n], t2[:, 0:n], ALU.add, [t1, t2], [qT])
                            else:
                                self.tt(k32[:, c, c0:c0 + n], t1[:, 0:n], t2[:, 0:n], ALU.add, [t1, t2], [k32])
                                for hh in range(2):
                                    self.cp(kTz[hh * 64:(hh + 1) * 64, 2 * c + hh, c0:c0 + n], k32[hh * 64:(hh + 1) * 64, c, c0:c0 + n],
                                            [k32], [kTz])
                for i in range(NT):
                    pv = self.proj_tm(wvg, 0, 512, i)
                    self.cp(v_tm[:, i, :], pv[:], [pv], [v_tm])
                    pt = P.ps()
                    for c in range(2):
                        self.tr(pt[:, c * 128:(c + 1) * 128], k32[:, c, cfg.ts[i]:cfg.ts[i] + 128], self.ident[:], [k32, self.ident], [pt])
                    self.cp(k_tm[:, i, :], pt[:, 0:256], [pt], [k_tm])
            def temps():
                kdz = P.sb(st, [128, 4, 128], BF16, "rkdz")
                self.ms(kdz[:].rearrange("p a b -> p (a b)"), 0.0, [kdz])
                qd = P.sb(st, [128, 4, 128], BF16, "rqd")
                self.ms(qd[:].rearrange("p a b -> p (a b)"), 0.0, [qd])
                return dict(S=P.sb(st, [128, 2, 128], BF16, "rS"), kdz=kdz, qd=qd, ST=P.sb(st, [128, 512], BF16, "rST"),
                            og=P.sb(st, [128, 512], F32, "rog"))

            def dpass(di, T):
                dname, order = (("f", cfg.fwd), ("b", cfg.bwd))[di]
                S, kdz, qd, ST, og = T["S"], T["kdz"], T["qd"], T["ST"], T["og"]
                self.ms(S[:].rearrange("p a b -> p (a b)"), 0.0, [S])
                for i in order:
                    tc = cfg.ts[i]
                    for h in range(4):
                        pr_, a = h // 2, h % 2
                        self.tt(qd[a * 64:(a + 1) * 64, h, :], qT[a * 64:(a + 1) * 64, pr_, tc:tc + 128],
                                qd_c[dname][a * 64:(a + 1) * 64, pr_ * 128:(pr_ + 1) * 128], ALU.mult, [qT, qd_c[dname]], [qd])
                    psc = P.ps(True)
                    for h in range(4):
                        self.mm(psc[:, h * 128:(h + 1) * 128], kTz[:, h, tc:tc + 128], qT[:, h // 2, tc:tc + 128], True, True,
                                [kTz, qT], [psc])
                    yield
                    self.tt(ST[:], psc[:], dm[dname][:], ALU.mult, [psc, dm[dname]], [ST])
                    P.rel(psc)
                    for h in range(4):
                        a = h % 2
                        self.act(kdz[:, h, a * 64:(a + 1) * 64], k_tm[:, i, h * 64:(h + 1) * 64], AF.Copy, [k_tm, kd_c[dname]], [kdz],
                                 scale=kd_c[dname][:, h:h + 1])
                    yield
                    po = P.ps(True)
                    for h in range(4):
                        self.mm(po[:, h * 128:(h + 1) * 128], ST[:, h * 128:(h + 1) * 128], v_tm[:, i, h * 128:(h + 1) * 128],
                                True, False, [ST, v_tm], [po])
                        self.mm(po[:, h * 128:(h + 1) * 128], qd[:, h, :], S[:, h // 2, :], False, True, [qd, S], [po])
                    pS = P.ps(True)
                    for pr_ in range(2):
                        for a in range(2):
                            h = 2 * pr_ + a
                            self.mm(pS[:, pr_ * 128:(pr_ + 1) * 128], kdz[:, h, :], v_tm[:, i, h * 128:(h + 1) * 128],
                                    a == 0, a == 1, [kdz, v_tm], [pS])
                    yield
                    self.put_opart(po, og, di, i)
                    for pr_ in range(2):
                        self.stt(S[:, pr_, :], S[:, pr_, :], g128[:, pr_:pr_ + 1], pS[:, pr_ * 128:(pr_ + 1) * 128],
                                 ALU.mult, ALU.add, [S, g128, pS], [S])
                    P.rel(po, pS)
                    yield

            TT_ = [temps(), temps()]
            self.run_lockstep([dpass(0, TT_[0]), dpass(1, TT_[1])])
            self.readout_sweep(st, "ln", wvg, 512, 0, None)

    def mixer_hg(self, l, s):
        cfg, P, dr = self.cfg, self.P, self.dr
        NT, TP = cfg.NT, cfg.TP
        with P.scope() as st:
            cc = {"f": self.const(st, "hg_cc_f"), "b": self.const(st, "hg_cc_b")}
            cgt = {"f": self.const(st, "hg_cgt_f"), "b": self.const(st, "hg_cgt_b")}
            msk = {"f": self.const(st, "hg_mask_f"), "b": self.const(st, "hg_mask_b")}
            clast = self.const(st, "hg_clast")
            lbT = P.sb(st, [128, 8], F32, "lbT")
            omlT = P.sb(st, [128, 8], F32, "omlT")
            nomlT = P.sb(st, [128, 8], F32, "nomlT")
            lbrow = P.sb(st, [128, 1024], F32, "lbrow")
            omlrow = P.sb(st, [128, 1024], F32, "omlrow")
            if l == 0:
                self.ms(lbT[:], 0.0, [lbT])
                self.ms(lbrow[:], 0.0, [lbrow])
            else:
                with P.scope() as s1:
                    a0 = P.sb(s1, [128, 8], F32, "a0")
                    r0 = P.sb(s1, [128, 1024], F32, "r0")
                    P.dma("sp", a0[:], dr["hg_lbT"][0], (), [a0])
                    P.dma("sp", lbT[:], dr["hg_lbT"][1], (), [lbT])
                    self.tt(lbT[:], lbT[:], a0[:], ALU.subtract, [lbT, a0], [lbT])
                    self.act(lbT[:], lbT[:], AF.Sigmoid, [lbT], [lbT])
                    P.dma("sp", r0[:], dr["hg_lb"][0].to_broadcast([128, 1024]), (), [r0])
                    P.dma("sp", lbrow[:], dr["hg_lb"][1].to_broadcast([128, 1024]), (), [lbrow])
                    self.tt(lbrow[:], lbrow[:], r0[:], ALU.subtract, [lbrow, r0], [lbrow])
                    self.act(lbrow[:], lbrow[:], AF.Sigmoid, [lbrow], [lbrow])
            self.ts(omlT[:], lbT[:], -1.0, 1.0, ALU.mult, ALU.add, [lbT], [omlT])
            self.ts(nomlT[:], omlT[:], -1.0, None, ALU.mult, None, [omlT], [nomlT])
            self.ts(omlrow[:], lbrow[:], -1.0, 1.0, ALU.mult, ALU.add, [lbrow], [omlrow])
            wig = P.sb(st, [128, KC, 1024], BF16, "hwig")
            self.loadw(wig, "w_in", l, OFF["hg_i"], 1024)
            qT = P.sb(st, [128, 4, TP], BF16, "hqT")
            with P.scope() as sp:
                wq = P.sb(sp, [128, KC, 512], BF16, "hwq")
                self.loadw(wq, "w_in", l, OFF["hg_q"], 512)
                for h in range(4):
                    for b in range(len(cfg.blocks)):
                        c0, n = cfg.blocks[b]
                        pp = self.proj_fm(wq, h * 128, b)
                        self.act(qT[:, h, c0:c0 + n], pp[:, 0:n], AF.Silu, [pp], [qT])
            wrow = self.rowbc(st, "hg_norm", l, 512)

            def temps(di):
                F = lambda nm, dt=F32: P.sb(st, [128, 512], dt, nm)
                T = dict(sig=F("hsig"), logf=F("hlogf"), ktm=F("hktm"), sigT=F("hsigT"), kTt=F("hkTt"), E1=F("hE1"), Ec=F("hEc"),
                         EL=P.sb(st, [128, 16], F32, "hEL"), qq=F("hqq", BF16), kk=F("hkk", BF16), Kd=F("hKd", BF16),
                         ST=F("hST", BF16), v=F("hv", BF16), QeZ=P.sb(st, [128, 4 * 512], BF16, "hQeZ"),
                         KdZ=P.sb(st, [128, 4, 512], BF16, "hKdZ"), Sst=P.sb(st, [128, 5, 512], BF16, "hSst"),
                         wf=P.sb(st, [128, KC, 512], BF16, "hwf"), og=F("hog"))
                self.ms(T["QeZ"][:], 0.0, [T["QeZ"]])
                self.loadw(T["wf"], "w_in", l, OFF["hg_ff"] if di == 0 else OFF["hg_fb"], 512)
                return T

            def dpass(di, T):
                dname, order = (("f", cfg.fwd), ("b", cfg.bwd))[di]
                sig, logf, ktm, sigT, kTt, E1, Ec, EL = T["sig"], T["logf"], T["ktm"], T["sigT"], T["kTt"], T["E1"], T["Ec"], T["EL"]
                qq, kk, Kd, ST, vt, QeZ, KdZ, Sst, wf, og = (T["qq"], T["kk"], T["Kd"], T["ST"], T["v"], T["QeZ"], T["KdZ"], T["Sst"],
                                                             T["wf"], T["og"])
                E2, Ed = sigT, sig
                self.ms(Sst[:, 0, :], 0.0, [Sst])
                corder = [0, 1, 2, 3] if di == 0 else [3, 2, 1, 0]
                for jt, i in enumerate(order):
                    sl = (lambda k: k) if jt % 2 == 0 else (lambda k: 4 - k)
                    tc = cfg.ts[i]
                    hb = self.hTb[cfg.tile_block(i)]
                    pf = self.proj_tm(wf, 0, 512, i)
                    self.act(sig[:], pf[:], AF.Sigmoid, [pf], [sig])
                    pv = self.proj_tm(wig, 0, 512, i)
                    self.act(vt[:], pv[:], AF.Identity, [pv], [vt])
                    yield
                    if l > 0:
                        self.tt(sig[:], sig[:], omlrow[:, di * 512:(di + 1) * 512], ALU.mult, [sig, omlrow], [sig])
                        self.tt(sig[:], sig[:], lbrow[:, di * 512:(di + 1) * 512], ALU.add, [sig, lbrow], [sig])
                    self.act(logf[:], sig[:], AF.Ln, [sig], [logf])
                    self.ts(ktm[:], sig[:], -1.0, 1.0, ALU.mult, ALU.add, [sig], [ktm], eng="pool")
                    yield
                    pff = P.ps(True)
                    for h in range(4):
                        self.tr(pff[:, h * 128:(h + 1) * 128], ktm[:, h * 128:(h + 1) * 128], self.ident[:], [ktm, self.ident], [pff])
                    yield
                    self.act(kTt[:], pff[:], AF.Identity, [pff], [kTt])
                    P.rel(pff)
                    pcr, pcc, pcl = P.ps(True), P.ps(True), P.ps(True)
                    for h in range(4):
                        lf = logf[:, h * 128:(h + 1) * 128]
                        self.mm(pcr[:, h * 128:(h + 1) * 128], lf, cc[dname][:, 0:128], True, True, [logf, cc[dname]], [pcr])
                        self.mm(pcc[:, h * 128:(h + 1) * 128], lf, cc[dname][:, 128:256], True, True, [logf, cc[dname]], [pcc])
                        self.mm(pcl[:, h * 4:(h + 1) * 4], lf, clast[:], True, True, [logf, clast], [pcl])
                    yield
                    self.act(E1[:], pcr[:], AF.Exp, [pcr], [E1])
                    self.act(E2[:], pcr[:], AF.Exp, [pcr], [E2], scale=-1.0)
                    self.act(Ec[:], pcc[:], AF.Exp, [pcc], [Ec])
                    self.act(EL[:], pcl[:, 0:16], AF.Exp, [pcl], [EL])
                    P.rel(pcr, pcc, pcl)
                    plc = P.ps(True)
                    self.mm(plc[:], cgt[dname][:], logf[:], True, True, [cgt[dname], logf], [plc])
                    yield
                    self.tt(qq[:].rearrange("p (h t) -> p h t", h=4), qT[:, :, tc:tc + 128], E1[:].rearrange("p (h t) -> p h t", h=4),
                            ALU.mult, [qT, E1], [qq])
                    self.tt(kk[:], kTt[:], E2[:], ALU.mult, [kTt, E2], [kk])
                    self.act(Ed[:], plc[:], AF.Exp, [plc], [Ed])
                    P.rel(plc)
                    yield
                    self.tt(mkap(QeZ[:], [[512, 4], [160, 4], [1, 32]]),
                            qT[:, :, tc:tc + 128].rearrange("p h (c j) -> p h c j", c=4),
                            Ec[:].rearrange("p (h c j) -> p h c j", h=4, c=4), ALU.mult, [qT, Ec], [QeZ])
                    self.tt(Kd[:], ktm[:], Ed[:], ALU.mult, [ktm, Ed], [Kd], eng="pool")
                    psc = P.ps(True)
                    for h in range(4):
                        self.mm(psc[:, h * 128:(h + 1) * 128], kk[:, h * 128:(h + 1) * 128], qq[:, h * 128:(h + 1) * 128],
                                True, True, [kk, qq], [psc])
                    yield
                    for c in range(4):
                        self.act(KdZ[:, c, :], Kd[:], AF.Copy, [Kd, clast], [KdZ], scale=clast[:, c:c + 1])
                    self.tt(ST[:].rearrange("p (h t) -> p h t", h=4), psc[:].rearrange("p (h t) -> p h t", h=4),
                            mkap(msk[dname][:], [[0, 4], [1, 128]]), ALU.mult, [psc, msk[dname]], [ST])
                    P.rel(psc)
                    yield
                    for kx, c in enumerate(corder):
                        pS = P.ps(True)
                        for h in range(4):
                            self.mm(pS[:, h * 128:(h + 1) * 128], KdZ[:, c, h * 128:(h + 1) * 128], vt[:, h * 128:(h + 1) * 128],
                                    True, True, [KdZ, vt], [pS])
                        yield
                        for h in range(4):
                            self.stt(Sst[:, sl(kx + 1), h * 128:(h + 1) * 128], Sst[:, sl(kx), h * 128:(h + 1) * 128],
                                     EL[:, h * 4 + c:h * 4 + c + 1], pS[:, h * 128:(h + 1) * 128], ALU.mult, ALU.add,
                                     [Sst, EL, pS], [Sst])
                        P.rel(pS)
                        yield
                    po = P.ps(True)
                    for h in range(4):
                        self.mm(po[:, h * 128:(h + 1) * 128], ST[:, h * 128:(h + 1) * 128], vt[:, h * 128:(h + 1) * 128],
                                True, False, [ST, vt], [po])
                        for kx, c in enumerate(corder):
                            self.mm(po[:, h * 128:(h + 1) * 128], QeZ[:, h * 512 + c * 128:h * 512 + (c + 1) * 128],
                                    Sst[:, sl(kx), h * 128:(h + 1) * 128], False, kx == 3, [QeZ, Sst], [po])
                    yield
                    self.put_opart(po, og, di, i)
                    P.rel(po)
                    yield

            TT_ = [temps(0), temps(1)]
            self.run_lockstep([dpass(0, TT_[0]), dpass(1, TT_[1])])
            self.readout_sweep(st, "rms", wig, 512, 4, wrow)

    def mixer_gdn(self, l, s):
        cfg, P, dr = self.cfg, self.P, self.dr
        NT, TP = cfg.NT, cfg.TP
        ident, ones = self.ident, self.CN["ones"]
        hv = lambda t_: t_[:].rearrange("p (h t) -> p h t", h=4)
        with P.scope() as st:
            cinc = {"f": self.const(st, "gd_cinc_f"), "b": self.const(st, "gd_cinc_b")}
            cgt = {"f": self.const(st, "gd_cgt_f"), "b": self.const(st, "gd_cgt_b")}
            mb = {"f": self.const(st, "gd_mb_f"), "b": self.const(st, "gd_mb_b")}
            stc = {"f": self.const(st, "gd_st_f"), "b": self.const(st, "gd_st_b")}
            mo = {"f": self.const(st, "gd_mo_f"), "b": self.const(st, "gd_mo_b")}
            qT = P.sb(st, [128, 2, TP], BF16, "gqT")
            kTz = P.sb(st, [128, 4, TP], BF16, "gkTz")
            self.ms(kTz[:].rearrange("p a b -> p (a b)"), 0.0, [kTz])
            k_tm = P.sb(st, [128, NT, 256], BF16, "gktm")
            v_tm = P.sb(st, [128, NT, 512], BF16, "gvtm")
            with P.scope() as sp:
                bo64 = self.const(sp, "bo64")
                cw = P.sb(sp, [128, 8, 4], F32, "gcw")
                P.dma("sp", cw[:], dr["gdn_convT"][l], (), [cw])
                wqkv = P.sb(sp, [128, KC, 1024], BF16, "gwqkv")
                self.loadw(wqkv, "w_in", l, OFF["gdn_qkv"], 1024)
                xs = P.sb(sp, [128, TP], F32, "gxs")
                y = P.sb(sp, [128, TP], F32, "gy")
                sq = P.sb(sp, [128, 512], F32, "gsq")
                rn = P.sb(sp, [128, 512], F32, "grn")
                self.ms(xs[:, TP - 2:TP], 0.0, [xs])
                self.ms(y[:, 0:2], 0.0, [y])
                self.ms(y[:, TP - 2:TP], 0.0, [y])
                n_ = TP - 4
                for c in range(8):
                    for b in range(len(cfg.blocks)):
                        c0, n = cfg.blocks[b]
                        pp = self.proj_fm(wqkv, c * 128, b)
                        self.cp(xs[:, c0:c0 + n], pp[:, 0:n], [pp], [xs])
                    self.ts(y[:, 2:2 + n_], xs[:, 0:n_], cw[:, c, 0:1], None, ALU.mult, None, [xs, cw], [y])
                    for j in (1, 2, 3):
                        self.stt(y[:, 2:2 + n_], xs[:, j:j + n_], cw[:, c, j:j + 1], y[:, 2:2 + n_], ALU.mult, ALU.add, [xs, cw, y], [y])
                    self.act(y[:, 2:2 + n_], y[:, 2:2 + n_], AF.Silu, [y], [y])
                    if c < 4:
                        for b in range(len(cfg.blocks)):
                            c0, n = cfg.blocks[b]
                            self.tt(sq[:, 0:n], y[:, c0:c0 + n], y[:, c0:c0 + n], ALU.mult, [y], [sq])
                            pss = P.ps()
                            self.mm(pss[:, 0:n], bo64[:], sq[:, 0:n], True, True, [bo64, sq], [pss])
                            self.act(rn[:, 0:n], pss[:, 0:n], AF.Sqrt, [pss, self.eps_t], [rn], bias=self.eps_t[:, 0:1])
                            self.V("dve", lambda e, n=n: e.reciprocal(out=rn[:, 0:n], in_=rn[:, 0:n]), [rn], [rn])
                            if c < 2:
                                self.stt(qT[:, c, c0:c0 + n], y[:, c0:c0 + n], 0.125, rn[:, 0:n], ALU.mult, ALU.mult, [y, rn], [qT])
                            else:
                                self.tt(y[:, c0:c0 + n], y[:, c0:c0 + n], rn[:, 0:n], ALU.mult, [y, rn], [y])
                                for hh in range(2):
                                    self.cp(kTz[hh * 64:(hh + 1) * 64, 2 * (c - 2) + hh, c0:c0 + n], y[hh * 64:(hh + 1) * 64, c0:c0 + n],
                                            [y], [kTz])
                    if c >= 2:
                        for i in range(NT):
                            pt = P.ps()
                            self.tr(pt[:, 0:128], y[:, cfg.ts[i]:cfg.ts[i] + 128], ident[:], [y, ident], [pt])
                            if c < 4:
                                self.cp(k_tm[:, i, (c - 2) * 128:(c - 1) * 128], pt[:, 0:128], [pt], [k_tm])
                            else:
                                self.cp(v_tm[:, i, (c - 4) * 128:(c - 3) * 128], pt[:, 0:128], [pt], [v_tm])
            alog = self.rowbc(st, "gdn_alog", l, 8)
            dtb = self.rowbc(st, "gdn_dtb", l, 8)
            negA = P.sb(st, [128, 8], F32, "gnegA")
            self.act(negA[:], alog[:], AF.Exp, [alog], [negA])
            self.ts(negA[:], negA[:], -1.0, None, ALU.mult, None, [negA], [negA])
            wab = P.sb(st, [128, KC, 16], BF16, "gwab")
            self.loadw(wab, "w_in", l, OFF["gdn_a"], 16)
            wg = P.sb(st, [128, KC, 512], BF16, "gwg")
            self.loadw(wg, "w_in", l, OFF["gdn_g"], 512)
            wrow = self.rowbc(st, "gdn_norm", l, 512)
            def temps():
                F = lambda nm, dt=F32: P.sb(st, [128, 512], dt, nm)
                T = dict(G4=F("gG4"), Cg4=F("gCg4"), Dinc=F("gDinc"), Xo=F("gXo", INV_DT), Bm=F("gBm", INV_DT), Cm=F("gCm", INV_DT),
                         ZTs=F("gZTs", INV_DT), Xb=F("gXb", INV_DT),
                         QKm=F("gQKm", BF16), Bmb=F("gBmb", BF16), U0=F("gU0", BF16), sm=P.sb(st, [128, 8, 4], F32, "gsm"),
                         Ke=P.sb(st, [128, 256], BF16, "gKe"), WZ=P.sb(st, [128, 4, 128], BF16, "gWZ"),
                         KdZ=P.sb(st, [128, 4, 128], BF16, "gKdZ"), QtZ=P.sb(st, [128, 4, 128], BF16, "gQtZ"),
                         MT=P.sb(st, [128, 2, 128], BF16, "gMT"), elp=P.sb(st, [128, 2], F32, "gelp"),
                         Gbp=P.sb(st, [128, 2, 128], F32, "gGbp"), QeT=P.sb(st, [128, 2, 128], F32, "gQeT"),
                         S=P.sb(st, [128, 2, 128], BF16, "gS"), og=F("gog"))
                for z in ("WZ", "KdZ", "QtZ"):
                    self.ms(T[z][:].rearrange("p a b -> p (a b)"), 0.0, [T[z]])
                return T

            def dpass(di, T):
                dname, order = (("f", cfg.fwd), ("b", cfg.bwd))[di]
                G4, Cg4, Dinc, Xo, Bm, Cm, ZTs = T["G4"], T["Cg4"], T["Dinc"], T["Xo"], T["Bm"], T["Cm"], T["ZTs"]
                Dst, X = G4, T["Xb"]
                QKm, Bmb, U0, sm, Ke, WZ, KdZ, QtZ = T["QKm"], T["Bmb"], T["U0"], T["sm"], T["Ke"], T["WZ"], T["KdZ"], T["QtZ"]
                MT, elp, Gbp, QeT, S, og = T["MT"], T["elp"], T["Gbp"], T["QeT"], T["S"], T["og"]
                g_, beta = sm[:, 3, :], sm[:, 4, :]
                self.ms(S[:].rearrange("p a b -> p (a b)"), 0.0, [S])
                ci, cg, mbd, std, mod = cinc[dname], cgt[dname], mb[dname], stc[dname], mo[dname]
                for i in order:
                    tc = cfg.ts[i]
                    pab = self.proj_tm(wab, 0, 16, i)
                    self.tt(sm[:, 0, :], pab[:, di * 4:di * 4 + 4], dtb[:, di * 4:di * 4 + 4], ALU.add, [pab, dtb], [sm])
                    self.act(beta, pab[:, 8 + di * 4:8 + di * 4 + 4], AF.Sigmoid, [pab], [sm])
                    self.act(sm[:, 1, :], sm[:, 0, :], AF.Exp, [sm], [sm])
                    self.act(sm[:, 2, :], sm[:, 1, :], AF.Ln, [sm], [sm], bias=1.0)
                    yield
                    self.tt(g_, sm[:, 2, :], negA[:, di * 4:di * 4 + 4], ALU.mult, [sm, negA], [sm])
                    pcm = P.ps(True)
                    self.mm(pcm[:, 0:4], ci[:], g_, True, True, [ci, sm], [pcm])
                    self.mm(pcm[:, 4:8], cg[:], g_, True, True, [cg, sm], [pcm])
                    self.mm(pcm[:, 8:12], ones[:], g_, True, True, [ones, sm], [pcm])
                    gb = mkap(g_, [[1, 4], [0, 128]])
                    self.tt(hv(G4), mkap(ones[:], [[0, 4], [1, 128]]), gb, ALU.mult, [ones, sm], [G4])
                    self.stt(hv(Cg4), mkap(ci[:], [[0, 4], [1, 128]]), -1.0, gb, ALU.mult, ALU.mult, [ci, sm], [Cg4])
                    yield
                    self.act(sm[:, 5:8, :].rearrange("p a b -> p (a b)"), pcm[:, 0:12], AF.Exp, [pcm], [sm])
                    P.rel(pcm)
                    self.cp(Gbp[:].rearrange("p r (a t) -> p r a t", a=2), hv(G4)[:, :, 0:64].rearrange("p (r a) t -> p r a t", a=2),
                            [G4], [Gbp])
                    prel = P.ps(True)
                    for h in range(4):
                        o_ = prel[:, h * 128:(h + 1) * 128]
                        self.mm(o_, G4[:, h * 128:(h + 1) * 128], ci[:], True, False, [G4, ci], [prel])
                        self.mm(o_, Cg4[:, h * 128:(h + 1) * 128], ones[:], False, False, [Cg4, ones], [prel])
                        self.mm(o_, ident[:], mbd[:], False, True, [ident, mbd], [prel])
                    pec = P.ps(True)
                    for pr_ in range(2):
                        self.mm(pec[:, pr_ * 128:(pr_ + 1) * 128], Gbp[:, pr_, :], ci[:], True, True, [Gbp, ci], [pec])
                    yield
                    self.act(Dinc[:], prel[:], AF.Exp, [prel], [Dinc])
                    self.act(QeT[:].rearrange("p a b -> p (a b)"), pec[:, 0:256], AF.Exp, [pec], [QeT])
                    P.rel(prel, pec)
                    for h in range(4):
                        a = h % 2
                        self.act(Ke[:, h * 64:(h + 1) * 64], k_tm[:, i, h * 64:(h + 1) * 64], AF.Copy, [k_tm, sm], [Ke], scale=sm[:, 5, h:h + 1])
                        self.act(KdZ[:, h, a * 64:(a + 1) * 64], k_tm[:, i, h * 64:(h + 1) * 64], AF.Copy, [k_tm, sm], [KdZ],
                                 scale=sm[:, 6, h:h + 1])
                    pkk, pqk = P.ps(True), P.ps(True)
                    for h in range(4):
                        self.mm(pkk[:, h * 128:(h + 1) * 128], kTz[:, h, tc:tc + 128], kTz[:, h, tc:tc + 128], True, True, [kTz], [pkk])
                        self.mm(pqk[:, h * 128:(h + 1) * 128], kTz[:, h, tc:tc + 128], qT[:, h // 2, tc:tc + 128], True, True, [kTz, qT], [pqk])
                    yield
                    self.tt(hv(Dst), hv(Dinc), mkap(std[:], [[0, 4], [1, 128]]), ALU.mult, [Dinc, std], [Dst])
                    self.tt(QKm[:], pqk[:], Dinc[:], ALU.mult, [pqk, Dinc], [QKm])
                    self.tt(QeT[:], QeT[:], qT[:, :, tc:tc + 128], ALU.mult, [QeT, qT], [QeT])
                    yield
                    for h in range(4):
                        hs_ = slice(h * 128, (h + 1) * 128)
                        self.stt(X[:, hs_], pkk[:, hs_], sm[:, 4, h:h + 1], Dst[:, hs_], ALU.mult, ALU.mult, [pkk, sm, Dst], [X])
                    P.rel(pkk, pqk)
                    idn = ident if INV_DT == F32 else self.identb
                    self.cp(hv(Bm), mkap(idn[:], [[0, 4], [1, 128]]), [idn], [Bm])
                    self.cp(hv(Cm), mkap(idn[:], [[0, 4], [1, 128]]), [idn], [Cm])
                    yield
                    for lv in range(7):
                        self.tt(hv(Xo), hv(X), mkap(mod[:, lv * 128:(lv + 1) * 128], [[0, 4], [1, 128]]), ALU.mult, [X, mod], [Xo])
                        pz = P.ps(True)
                        for h in range(4):
                            hs_ = slice(h * 128, (h + 1) * 128)
                            self.mm(pz[:, hs_], Xo[:, hs_], Cm[:, hs_], True, True, [Xo, Cm], [pz])
                        yield
                        self.act(ZTs[:], pz[:], AF.Identity, [pz], [ZTs])
                        P.rel(pz)
                        pb2, pc2 = P.ps(True), P.ps(True)
                        for h in range(4):
                            hs_ = slice(h * 128, (h + 1) * 128)
                            self.mm(pb2[:, hs_], ZTs[:, hs_], Bm[:, hs_], True, True, [ZTs, Bm], [pb2])
                            self.mm(pc2[:, hs_], Bm[:, hs_], ZTs[:, hs_], True, True, [Bm, ZTs], [pc2])
                        yield
                        self.tt(Bm[:], Bm[:], pb2[:], ALU.subtract, [Bm, pb2], [Bm])
                        self.tt(Cm[:], Cm[:], pc2[:], ALU.subtract, [Cm, pc2], [Cm])
                        P.rel(pb2, pc2)
                        yield
                    self.act(Bmb[:], Bm[:], AF.Identity, [Bm], [Bmb])
                    pu, pw = P.ps(True), P.ps(True)
                    for h in range(4):
                        hs_ = slice(h * 128, (h + 1) * 128)
                        self.mm(pu[:, hs_], Bmb[:, hs_], v_tm[:, i, hs_], True, True, [Bmb, v_tm], [pu])
                    for h in range(4):
                        self.mm(pw[:, h * 64:(h + 1) * 64], Bmb[:, h * 128:(h + 1) * 128], Ke[:, h * 64:(h + 1) * 64], True, True, [Bmb, Ke], [pw])
                    yield
                    for h in range(4):
                        a = h % 2
                        self.act(U0[:, h * 128:(h + 1) * 128], pu[:, h * 128:(h + 1) * 128], AF.Copy, [pu, sm], [U0], scale=sm[:, 4, h:h + 1])
                        self.ts(WZ[:, h, a * 64:(a + 1) * 64], pw[:, h * 64:(h + 1) * 64], sm[:, 4, h:h + 1], None, ALU.mult, None, [pw, sm], [WZ])
                    P.rel(pu, pw)
                    for pr_ in range(2):
                        for a in range(2):
                            self.cp(elp[a * 64:(a + 1) * 64, pr_:pr_ + 1], sm[a * 64:(a + 1) * 64, 7, 2 * pr_ + a:2 * pr_ + a + 1], [sm], [elp])
                    yield
                    pmt, pqt = P.ps(True), P.ps(True)
                    for pr_ in range(2):
                        ps_ = slice(pr_ * 128, (pr_ + 1) * 128)
                        for a in range(2):
                            h = 2 * pr_ + a
                            self.mm(pmt[:, ps_], WZ[:, h, :], KdZ[:, h, :], a == 0, a == 1, [WZ, KdZ], [pmt])
                        for a in range(2):
                            h = 2 * pr_ + a
                            self.mm(pqt[:, ps_], WZ[:, h, :], QKm[:, h * 128:(h + 1) * 128], a == 0, a == 1, [WZ, QKm], [pqt])
                    yield
                    for pr_ in range(2):
                        ps_ = slice(pr_ * 128, (pr_ + 1) * 128)
                        self.stt(MT[:, pr_, :], ident[:], elp[:, pr_:pr_ + 1], pmt[:, ps_], ALU.mult, ALU.subtract, [ident, elp, pmt], [MT])
                    for h in range(4):
                        pr_, a = h // 2, h % 2
                        self.tt(QtZ[a * 64:(a + 1) * 64, h, :], QeT[a * 64:(a + 1) * 64, pr_, :], pqt[a * 64:(a + 1) * 64, pr_ * 128:(pr_ + 1) * 128],
                                ALU.subtract, [QeT, pqt], [QtZ])
                    P.rel(pmt, pqt)
                    yield
                    po = P.ps(True)
                    for h in range(4):
                        hs_ = slice(h * 128, (h + 1) * 128)
                        self.mm(po[:, hs_], QKm[:, hs_], U0[:, hs_], True, False, [QKm, U0], [po])
                        self.mm(po[:, hs_], QtZ[:, h, :], S[:, h // 2, :], False, True, [QtZ, S], [po])
                    pS = P.ps(True)
                    for pr_ in range(2):
                        ps_ = slice(pr_ * 128, (pr_ + 1) * 128)
                        for a in range(2):
                            h = 2 * pr_ + a
                            self.mm(pS[:, ps_], KdZ[:, h, :], U0[:, h * 128:(h + 1) * 128], a == 0, False, [KdZ, U0], [pS])
                        self.mm(pS[:, ps_], MT[:, pr_, :], S[:, pr_, :], False, True, [MT, S], [pS])
                    yield
                    self.put_opart(po, og, di, i)
                    self.cp(S[:].rearrange("p a b -> p (a b)"), pS[:, 0:256], [pS], [S])
                    P.rel(po, pS)
                    yield

            TT_ = [temps(), temps()]
            self.run_lockstep([dpass(0, TT_[0]), dpass(1, TT_[1])])
            self.readout_sweep(st, "rms", wg, 0, 8, wrow)

    def ln_affine(self, xpre, xo, st3, grow_, brow_):
        s6, mv, rs = st3
        self.do_stats(xpre, s6, mv, rs)
        self.ts(xo[:], xpre[:], mv[:, 0:1], rs[:, 0:1], ALU.subtract, ALU.mult, [xpre, mv, rs], [xo])
        self.tt(xo[:], xo[:], grow_[:], ALU.mult, [xo, grow_], [xo])
        self.tt(xo[:], xo[:], brow_[:], ALU.add, [xo, brow_], [xo])

    def phase_c(self, l, s, last):
        cfg, P, dr = self.cfg, self.P, self.dr
        nseq, NT, TP, nct = cfg.nseq, cfg.NT, cfg.TP, cfg.nct
        with P.scope() as st:
            G1 = {s: self.grow(st, l, 2, s)}
            if not last:
                G1[nseq] = self.grow(st, l, 2, nseq)
            wbr = P.sb(st, [128, 16, D], BF16, "wbr")
            self._dst_tt = wbr
            self.load_cast(wbr[:], dr["w_branch"][l].rearrange("j (k p) n -> p (j k) n", p=128), 16, D)
            wout = P.sb(st, [128, KC, D], BF16, "wout")
            self.loadw(wout, "w_out", l, 0, D)
            lg = self.rowbc(st, "ln_mix_g", l)
            lb = self.rowbc(st, "ln_mix_b", l)
            brb = P.sb(st, [128, 16, 512], BF16, "brb")
            wmg = [P.sb(st, [128, KC, 512], BF16, "wmg") for _ in range(2)]
            gsb = [P.sb(st, [128, 512], F32, "gsb") for _ in range(2)]
            macc = P.sb(st, [128, 512], F32, "macc")
            mgT = P.sb(st, [128, KC, 512], BF16, "mgT")
            xts = [P.sb(st, [128, D], F32, "xc") for _ in range(2)]
            xps = [P.sb(st, [128, D], F32, "xp")] * 2
            xms = [P.sb(st, [128, D], F32, "xm") for _ in range(2)]
            xns = [P.sb(st, [128, D], F32, "xn")] * 2
            st3 = [self.st3(st) for _ in range(2)]
            st3b = [self.st3(st) for _ in range(2)]
            it = 0
            for b in range(len(cfg.blocks)):
                if last and b == 0:
                    continue
                c0, n = cfg.blocks[b]
                P.dma("sp", brb[:, :, 0:n], self.BRT[:, :, c0:c0 + n].rearrange("c p t -> p c t"), [self.bBRT], [brb])
                for nch in range(8):
                    wm = wmg[nch % 2]
                    for j in range(4):
                        self.loadw(wm, "w_in", l, OFF["mg"] + j * D + nch * 128, 128, d0=j * 128)
                    for j in range(4):
                        pp = self.proj_fm(wm, j * 128, b)
                        g = gsb[j % 2]
                        self.act(g[:, 0:n], pp[:, 0:n], AF.Sigmoid, [pp], [g])
                        pb = P.ps()
                        for kk in range(4):
                            self.mm(pb[:, 0:n], wbr[:, j * 4 + kk, nch * 128:(nch + 1) * 128], brb[:, j * 4 + kk, 0:n],
                                    kk == 0, kk == 3, [wbr, brb], [pb])
                        if j == 0:
                            self.tt(macc[:, 0:n], g[:, 0:n], pb[:, 0:n], ALU.mult, [g, pb], [macc])
                        else:
                            self.tt(g[:, 0:n], g[:, 0:n], pb[:, 0:n], ALU.mult, [g, pb], [g])
                            if j < 3:
                                self.tt(macc[:, 0:n], macc[:, 0:n], g[:, 0:n], ALU.add, [macc, g], [macc])
                            else:
                                self.tt(mgT[:, nch, 0:n], macc[:, 0:n], g[:, 0:n], ALU.add, [macc, g], [mgT])
                for i in range(NT):
                    if cfg.tile_block(i) != b:
                        continue
                    tc = cfg.ts[i] - c0
                    xt, xp, xm, xn = xts[it % 2], xps[it % 2], xms[it % 2], xns[it % 2]
                    slot = s if i >= nct else nseq
                    src, rb = self.x_src(l, s, i)
                    P.dma("sp", xt[:], src, rb, [xt])
                    for nb in range(2):
                        py = P.ps()
                        for k in range(KC):
                            self.mm(py[:], mgT[:, k, tc:tc + 128], wout[:, k, nb * 512:(nb + 1) * 512], k == 0, k == KC - 1,
                                    [mgT, wout], [py])
                        sl_ = slice(nb * 512, (nb + 1) * 512)
                        self.tt(xp[:, sl_], py[:], G1[slot][:, sl_], ALU.mult, [py, G1[slot]], [xp])
                        self.stt(xp[:, sl_], xt[:, sl_], ALPHA, xp[:, sl_], ALU.mult, ALU.add, [xt, xp], [xp])
                    self.ln_affine(xp, xm, st3[it % 2], lg, lb)
                    P.dma(STQ, self.XM[s, i * 128:(i + 1) * 128, :], xm[:], [xm], [self.bXM])
                    self.ln_mod_T(xm, xn, st3b[it % 2], i, s, 3, 4)
                    it += 1

    def phase_d(self, l, s, last, hook=None):
        cfg, P, dr = self.cfg, self.P, self.dr
        TP = cfg.TP
        with P.scope() as st:
            cw = P.sb(st, [128, 44, 3], F32, "fcw")
            P.dma("sp", cw[:], dr["ffn_convT"][l], (), [cw])
            cb = P.sb(st, [128, 44], F32, "fcb")
            P.dma("sp", cb[:], dr["ffn_convbT"][l], (), [cb])
            wts = [P.sb(st, [128, KC, 256], BF16, "wup") for _ in range(2)]
            us = [[P.sb(st, [128, TP], F32, "us") for _ in range(2)] for _ in range(2)]
            cv = [[P.sb(st, [128, TP], F32, "cv") for _ in range(2)] for _ in range(2)]
            ab = [P.sb(st, [128, TP], BF16, "ab") for _ in range(2)]
            for a in us:
                for u in a:
                    self.ms(u[:], 0.0, [u])
            for a in ab:
                self.ms(a[:], 0.0, [a])
            n_ = TP - 4
            for c in range(22):
                if c == 2 and hook is not None:
                    hook()
                wt = wts[c % 2]
                self.loadw(wt, "w_up", l, c * 128, 128, d0=0)
                self.loadw(wt, "w_up", l, DFF + c * 128, 128, d0=128)
                u2, c2, a_ = us[c % 2], cv[c % 2], ab[c % 2]
                for b in range(len(cfg.blocks)):
                    if last and b == 0:
                        continue
                    c0, n = cfg.blocks[b]
                    for vg in range(2):
                        pp = self.proj_fm(wt, vg * 128, b)
                        if vg == 0:
                            self.cp(u2[vg][:, c0:c0 + n], pp[:, 0:n], [pp], [u2[vg]])
                        else:
                            self.act(u2[vg][:, c0:c0 + n], pp[:, 0:n], AF.Identity, [pp], [u2[vg]])
                for vg in range(2):
                    ch = c + 22 * vg
                    self.ts(c2[vg][:, 2:2 + n_], u2[vg][:, 1:1 + n_], cw[:, ch, 0:1], cb[:, ch:ch + 1], ALU.mult, ALU.add,
                            [u2[vg], cw, cb], [c2[vg]])
                    for j in (1, 2):
                        self.stt(c2[vg][:, 2:2 + n_], u2[vg][:, 1 + j:1 + j + n_], cw[:, ch, j:j + 1], c2[vg][:, 2:2 + n_],
                                 ALU.mult, ALU.add, [u2[vg], cw, c2[vg]], [c2[vg]])
                self.act(c2[1][:, 2:2 + n_], c2[1][:, 2:2 + n_], AF.Silu, [c2[1]], [c2[1]])
                self.tt(a_[:, 2:2 + n_], c2[1][:, 2:2 + n_], c2[0][:, 2:2 + n_], ALU.mult, [c2[0], c2[1]], [a_])
                P.dma(STQ, self.ACTD[c], a_[:], [a_], [self.bACTD])

    def phase_de(self, l, s, last):
        cfg, P, dr = self.cfg, self.P, self.dr
        nseq = cfg.nseq
        with P.scope() as st:
            G2 = {s: self.grow(st, l, 5, s)}
            if not last:
                G2[nseq] = self.grow(st, l, 5, nseq)
            wd = P.sb(st, [128, 22, D], BF16, "wd")
            lg = self.rowbc(st, "ln_ffn_g", l)
            lb = self.rowbc(st, "ln_ffn_b", l)

            def load_wd():
                self._dst_tt = wd
                self.load_cast(wd[:], dr["w_down"][l].rearrange("(k p) n -> p k n", p=128), 22, D)

            self.phase_d(l, s, last, hook=load_wd)
            self.phase_e(l, s, last, st, G2, wd, lg, lb)

    def phase_e(self, l, s, last, st, G2, wd, lg, lb):
        cfg, P, dr = self.cfg, self.P, self.dr
        nseq, NT, TP, nct = cfg.nseq, cfg.NT, cfg.TP, cfg.nct
        if True:
            ats = [P.sb(st, [128, 22, 128], BF16, "at") for _ in range(2)]
            xts = [P.sb(st, [128, D], F32, "xe") for _ in range(2)]
            xps = [P.sb(st, [128, D], F32, "xpe") for _ in range(2)]
            xos = [P.sb(st, [128, D], F32, "xo") for _ in range(2)]
            st3 = [self.st3(st) for _ in range(2)]
            it = 0
            for i in range(NT):
                if last and i < nct:
                    continue
                at, xt, xp, xo = ats[it % 2], xts[it % 2], xps[it % 2], xos[it % 2]
                slot = s if i >= nct else nseq
                tcol = cfg.ts[i]
                P.dma("sp", at[:], self.ACTD[:, :, tcol:tcol + 128].rearrange("c p t -> p c t"), [self.bACTD], [at])
                P.dma("sp", xt[:], self.XM[s, i * 128:(i + 1) * 128, :], [self.bXM], [xt])
                for nb in range(2):
                    py = P.ps()
                    for k in range(22):
                        self.mm(py[:], at[:, k, :], wd[:, k, nb * 512:(nb + 1) * 512], k == 0, k == 21, [at, wd], [py])
                    sl_ = slice(nb * 512, (nb + 1) * 512)
                    self.tt(xp[:, sl_], py[:], G2[slot][:, sl_], ALU.mult, [py, G2[slot]], [xp])
                    self.stt(xp[:, sl_], xt[:, sl_], ALPHA, xp[:, sl_], ALU.mult, ALU.add, [xt, xp], [xp])
                self.ln_affine(xp, xo, st3[it % 2], lg, lb)
                if last:
                    d_ = P.dma(STQ, self.out[s, (i - nct) * 128:(i - nct + 1) * 128, :], xo[:], [xo], [])
                    self.outs.append(d_)
                else:
                    P.dma(STQ, self.XA[s, i * 128:(i + 1) * 128, :], xo[:], [xo], [self.bXA])
                it += 1


_CACHE = {}


def run_cfg(inputs, cfg, core_batches):
    consts = make_consts(cfg)
    maps = []
    for cb in core_batches:
        m = host_layout(inputs, cfg, cb)
        for k, v in consts.items():
            m["c_" + k] = np.ascontiguousarray(v, dtype=np.float32)
        maps.append(m)
    shapes = {k: v.shape for k, v in maps[0].items()}
    key = (cfg.nseq, cfg.Tc, cfg.Tl, tuple(cfg.mixers))
    if key not in _CACHE:
        _CACHE[key] = Builder(cfg, shapes).build()
    nc = _CACHE[key]
    res = run_bass_kernel_spmd(nc, maps, core_ids=list(range(len(maps))))
    if getattr(cfg, "debug", False):
        global DBG
        DBG = res.results
    return np.concatenate([np.asarray(r["out"]) for r in res.results], axis=0)


def kernel(**inputs):
    B = np.asarray(inputs["x"]).shape[0]
    ncores = 8
    per = B // ncores
    cfg = Cfg(per, np.asarray(inputs["ctx"]).shape[1], np.asarray(inputs["x"]).shape[1])
    cbs = [list(range(c * per, (c + 1) * per)) for c in range(ncores)]
    return run_cfg(inputs, cfg, cbs).astype(np.float32)
```
